# Optimizing a Trainium2 kernel written in Bass

```python
import math
import jax, jax.numpy as jnp
from jax import lax
import numpy as np

D_MODEL = 1024
BATCH = 2
SEQ = 8192
DEPTH = 1

MIX_WIDTH = D_MODEL
ATTN_WIDTH = MIX_WIDTH // 2
POOL_WIDTH = MIX_WIDTH - ATTN_WIDTH

HEAD_DIM = 64
N_HEADS = ATTN_WIDTH // HEAD_DIM
N_KV_HEADS = 2
GROUP = N_HEADS // N_KV_HEADS
KV_WIDTH = N_KV_HEADS * HEAD_DIM
WINDOW = 128
BLOCK = 128
ROPE_THETA = 10000.0

POOL_WINDOWS = (2, 4, 8, 16)
N_POOL_GROUPS = len(POOL_WINDOWS)
POOL_GROUP_WIDTH = POOL_WIDTH // N_POOL_GROUPS

IN_PROJ_WIDTH = ATTN_WIDTH + 2 * KV_WIDTH + POOL_WIDTH

FFN_MULT_OF = 256
D_FF = ((8 * D_MODEL // 3 + FFN_MULT_OF - 1) // FFN_MULT_OF) * FFN_MULT_OF

N_MOD = 6
RMS_EPS = 1e-6

kernel_name = "hymba_pool_swa_sink_adaln_block"


def rmsnorm(x, g):
    xf = x.astype(jnp.float32)
    y = xf * lax.rsqrt(jnp.mean(xf * xf, axis=-1, keepdims=True) + RMS_EPS)
    return (y * g.astype(jnp.float32)).astype(x.dtype)


def modulate(h, shift, scale):
    return h * (1.0 + scale[:, None, :]) + shift[:, None, :]


def apply_rope(t, positions):
    half = HEAD_DIM // 2
    inv_freq = ROPE_THETA ** (-jnp.arange(half, dtype=jnp.float32) * (2.0 / HEAD_DIM))
    ang = positions.astype(jnp.float32)[:, :, None] * inv_freq[None, None, :]
    cos = jnp.cos(ang)[:, :, None, :]
    sin = jnp.sin(ang)[:, :, None, :]
    tf = t.astype(jnp.float32)
    t1, t2 = tf[..., :half], tf[..., half:]
    out = jnp.concatenate([t1 * cos - t2 * sin, t2 * cos + t1 * sin], axis=-1)
    return out.astype(t.dtype)


def sliding_window_attention_with_sinks(q, k, v, sinks):
    b, s = q.shape[0], q.shape[1]
    nb = s // BLOCK
    qb = q.reshape(b, nb, BLOCK, N_KV_HEADS, GROUP, HEAD_DIM)

    def with_prev_block(t):
        tb = t.reshape(b, nb, BLOCK, N_KV_HEADS, HEAD_DIM)
        prev = jnp.pad(tb, ((0, 0), (1, 0), (0, 0), (0, 0), (0, 0)))[:, :nb]
        return jnp.concatenate([prev, tb], axis=2)

    kk = with_prev_block(k)
    vv = with_prev_block(v)

    scale = 1.0 / math.sqrt(HEAD_DIM)
    logits = jnp.einsum('bnqhgd,bnkhd->bnhgqk', qb.astype(jnp.float32), kk.astype(jnp.float32)) * scale

    qi = jnp.arange(BLOCK)[:, None]
    kj = jnp.arange(2 * BLOCK)[None, :]
    rel = kj - BLOCK - qi
    band = (rel <= 0) & (rel > -WINDOW)
    not_pad = (jnp.arange(nb)[:, None] > 0) | (jnp.arange(2 * BLOCK)[None, :] >= BLOCK)
    mask = band[None, :, :] & not_pad[:, None, :]
    logits = jnp.where(mask[None, :, None, None, :, :], logits, -jnp.inf)

    sink = sinks.astype(jnp.float32).reshape(N_KV_HEADS, GROUP)[None, None, :, :, None, None]
    m = jnp.maximum(jnp.max(logits, axis=-1, keepdims=True), sink)
    p = jnp.exp(logits - m)
    denom = jnp.sum(p, axis=-1, keepdims=True) + jnp.exp(sink - m)
    out = jnp.einsum('bnhgqk,bnkhd->bnqhgd', p / denom, vv.astype(jnp.float32))
    return out.reshape(b, s, N_HEADS * HEAD_DIM).astype(q.dtype)


def multiscale_pool_mixer(u, w_pool, pool_scale):
    s = u.shape[1]
    outs = []
    for gi, w in enumerate(POOL_WINDOWS):
        ug = u[..., gi * POOL_GROUP_WIDTH:(gi + 1) * POOL_GROUP_WIDTH]
        uf = ug.astype(jnp.float32)
        cs = jnp.cumsum(uf, axis=1)
        shifted = jnp.pad(cs, ((0, 0), (w, 0), (0, 0)))[:, :s]
        count = jnp.minimum(jnp.arange(s) + 1, w).astype(jnp.float32)
        mean = (cs - shifted) / count[None, :, None]
        pooled = (mean - uf).astype(u.dtype)
        outs.append(jnp.einsum('bsc,cd->bsd', pooled, w_pool[gi]))
    return jnp.concatenate(outs, axis=-1) * pool_scale


def setup_inputs(seed: int = 0) -> dict:
    key = jax.random.key(seed)
    ks = jax.random.split(key, 20)
    f32 = jnp.float32
    x = jax.random.normal(ks[0], (BATCH, SEQ, D_MODEL), f32)
    c = jax.random.normal(ks[1], (BATCH, D_MODEL), f32)
    offsets = jax.random.randint(ks[2], (BATCH, 1), 0, 1024, dtype=jnp.int32)
    positions = offsets + jnp.arange(SEQ, dtype=jnp.int32)[None, :]
    w_ada = jax.random.normal(ks[3], (D_MODEL, N_MOD * D_MODEL), f32) * (D_MODEL ** -0.5) * 0.5
    b_ada = jax.random.normal(ks[4], (N_MOD * D_MODEL,), f32) * 0.02
    norm1 = 1.0 + 0.05 * jax.random.normal(ks[5], (D_MODEL,), f32)
    w_in = jax.random.normal(ks[6], (D_MODEL, IN_PROJ_WIDTH), f32) * (D_MODEL ** -0.5)
    sinks = jax.random.normal(ks[7], (N_HEADS,), f32) * 0.5
    w_pool = jax.random.normal(ks[8], (N_POOL_GROUPS, POOL_GROUP_WIDTH, POOL_GROUP_WIDTH), f32) * (POOL_GROUP_WIDTH ** -0.5)
    pool_scale = 1.0 + 0.1 * jax.random.normal(ks[9], (POOL_WIDTH,), f32)
    w_out = jax.random.normal(ks[10], (MIX_WIDTH, D_MODEL), f32) * (MIX_WIDTH ** -0.5)
    norm2 = 1.0 + 0.05 * jax.random.normal(ks[11], (D_MODEL,), f32)
    w_gate = jax.random.normal(ks[12], (D_MODEL, D_FF), f32) * (D_MODEL ** -0.5)
    w_up = jax.random.normal(ks[13], (D_MODEL, D_FF), f32) * (D_MODEL ** -0.5)
    w_down = jax.random.normal(ks[14], (D_FF, D_MODEL), f32) * (D_FF ** -0.5)
    norm_f = 1.0 + 0.05 * jax.random.normal(ks[15], (D_MODEL,), f32)
    return {"x": x, "c": c, "positions": positions, "w_ada": w_ada, "b_ada": b_ada,
            "norm1": norm1, "w_in": w_in, "sinks": sinks, "w_pool": w_pool,
            "pool_scale": pool_scale, "w_out": w_out, "norm2": norm2,
            "w_gate": w_gate, "w_up": w_up, "w_down": w_down, "norm_f": norm_f}


def reference(x, c, positions, w_ada, b_ada, norm1, w_in, sinks, w_pool, pool_scale,
              w_out, norm2, w_gate, w_up, w_down, norm_f):
    b, s, _ = x.shape
    mod = jax.nn.silu(c) @ w_ada + b_ada
    shift1, scale1, gate1, shift2, scale2, gate2 = jnp.split(mod, N_MOD, axis=-1)

    for _ in range(DEPTH):
        h = modulate(rmsnorm(x, norm1), shift1, scale1)
        u = h @ w_in
        q = u[..., :ATTN_WIDTH].reshape(b, s, N_HEADS, HEAD_DIM)
        k = u[..., ATTN_WIDTH:ATTN_WIDTH + KV_WIDTH].reshape(b, s, N_KV_HEADS, HEAD_DIM)
        v = u[..., ATTN_WIDTH + KV_WIDTH:ATTN_WIDTH + 2 * KV_WIDTH].reshape(b, s, N_KV_HEADS, HEAD_DIM)
        u_pool = u[..., ATTN_WIDTH + 2 * KV_WIDTH:]

        q = apply_rope(q, positions)
        k = apply_rope(k, positions)
        attn_out = sliding_window_attention_with_sinks(q, k, v, sinks)
        pool_out = multiscale_pool_mixer(u_pool, w_pool, pool_scale)

        mixed = jnp.concatenate([attn_out, pool_out], axis=-1) @ w_out
        x = x + gate1[:, None, :] * mixed

        h2 = modulate(rmsnorm(x, norm2), shift2, scale2)
        ff = (jax.nn.silu(h2 @ w_gate) * (h2 @ w_up)) @ w_down
        x = x + gate2[:, None, :] * ff

    return rmsnorm(x, norm_f)
```

```python
import math
from contextlib import ExitStack

import numpy as np
import concourse.bass as bass
import concourse.mybir as mybir
from concourse.bass_utils import run_bass_kernel_spmd

F32 = mybir.dt.float32
F32R = mybir.dt.float32r
BF16 = mybir.dt.bfloat16
I32 = mybir.dt.int32
ALU = mybir.AluOpType
AF = mybir.ActivationFunctionType

D = 1024
SEQ = 8192
NCORES = 8
TOK = 2048
NT = 17
FF = 2816
NFC = FF // 128
GROUPS = [2, 4, 4, 4, 4, 4]
EPS = 1e-6
NSEM_DMA = {"sp": 46, "pool": 50}
DBG_SKIP = set()


class Ev:
    __slots__ = ("sem", "val", "eng", "dma")

    def __init__(self, sem, val, eng, dma):
        self.sem, self.val, self.eng, self.dma = sem, val, eng, dma


class Sched:
    def __init__(self, nc, es):
        self.nc = nc
        self.engs = {"pe": nc.tensor, "act": nc.scalar, "dve": nc.vector, "pool": nc.gpsimd, "sp": nc.sync}
        self.sem = {e: es.enter_context(nc.semaphore("c_" + e)) for e in ("pe", "act", "dve", "pool")}
        self.cnt = {e: 0 for e in self.sem}
        self.dsem = {q: [es.enter_context(nc.semaphore("d_%s%d" % (q, i))) for i in range(NSEM_DMA[q])]
                     for q in ("sp", "pool")}
        self.dcnt = {q: 0 for q in self.dsem}
        self.dval = {}
        self.waited = {}
        self.lastw = {}
        self.readers = {}

    def _need(self, e, sem, val):
        k = (e, id(sem))
        if self.waited.get(k, 0) < val:
            self.engs[e].wait_ge(sem, val)
            self.waited[k] = val

    def _deps(self, e, reads, writes, true_psw=()):
        need = {}

        def add(ev, raw, psum=False):
            if ev is None:
                return
            if (not ev.dma) and ev.eng == e and not raw and not (psum and e != "pe"):
                return
            k = id(ev.sem)
            if k not in need or need[k][1] < ev.val:
                need[k] = (ev.sem, ev.val)

        for k in reads:
            add(self.lastw.get(k), True)
        for k in writes:
            add(self.lastw.get(k), False, k in true_psw)
            for ev in self.readers.get(k, ()):
                add(ev, False)
        for sem, val in need.values():
            self._need(e, sem, val)

    def _record(self, ev, reads, writes):
        for k in writes:
            self.lastw[k] = ev
            self.readers[k] = []
        for k in reads:
            if k not in writes:
                self.readers.setdefault(k, []).append(ev)

    def op(self, e, fn, reads=(), writes=(), sync_waw=()):
        psr = [k for k in reads if isinstance(k, tuple) and k[0] == "ps"]
        true_psw = [k for k in writes if isinstance(k, tuple) and k[0] == "ps"] + list(sync_waw)
        if psr:
            reads = [k for k in reads if k not in psr]
            writes = list(writes) + [k for k in psr if k not in writes]
        self._deps(e, reads, writes, true_psw)
        inst = fn()
        self.cnt[e] += 1
        inst.then_inc(self.sem[e], 1)
        self._record(Ev(self.sem[e], self.cnt[e], e, False), reads, writes)

    def dma(self, q, fn, reads=(), writes=()):
        self._deps(q, reads, writes)
        j = self.dcnt[q]
        self.dcnt[q] += 1
        slot = j % NSEM_DMA[q]
        sem = self.dsem[q][slot]
        prev = self.dval.get((q, slot), 0)
        if prev:
            self._need(q, sem, prev)
        inst = fn()
        inst.then_inc(sem, 16)
        self.dval[(q, slot)] = prev + 16
        self._record(Ev(sem, prev + 16, q, True), reads, writes)

    def barrier(self, engines=("pe", "act", "dve", "pool", "sp")):
        for e in engines:
            for e2 in self.sem:
                if e2 != e and self.cnt[e2]:
                    self._need(e, self.sem[e2], self.cnt[e2])
            for (q, slot), v in self.dval.items():
                self._need(e, self.dsem[q][slot], v)
        self.lastw.clear()
        self.readers.clear()


def build(dbg=(), stop=None, nsteps=None):
    nc = bass.Bass("TRN2", target_bir_lowering=False)
    dt = nc.dram_tensor
    x_d = dt("x", [NT * 128, D], F32, kind="ExternalInput").ap()
    pos_d = dt("pos", [128, NT], I32, kind="ExternalInput").ap()
    c_d = dt("c", [128, 8], F32, kind="ExternalInput").ap()
    flag_d = dt("flag", [128, 1], F32, kind="ExternalInput").ap()
    wada_d = dt("w_ada", [D, 6 * D], F32, kind="ExternalInput").ap()
    bfm_d = dt("b_fm", [128, 48], F32, kind="ExternalInput").ap()
    bada_d = dt("b_ada", [1, 6 * D], F32, kind="ExternalInput").ap()
    n1_d = dt("norm1", [128, 8], F32, kind="ExternalInput").ap()
    n2_d = dt("norm2", [128, 8], F32, kind="ExternalInput").ap()
    nf_d = dt("norm_f", [1, D], F32, kind="ExternalInput").ap()
    win_d = dt("w_in", [D, 1280], F32, kind="ExternalInput").ap()
    sinks_d = dt("sinks", [1, 8], F32, kind="ExternalInput").ap()
    wpool_d = dt("w_pool", [4, 128, 128], F32, kind="ExternalInput").ap()
    psc_d = dt("pool_scale", [128, 4], F32, kind="ExternalInput").ap()
    wout_d = dt("w_out", [D, D], F32, kind="ExternalInput").ap()
    wg_d = dt("w_gate", [D, FF], F32, kind="ExternalInput").ap()
    wu_d = dt("w_up", [D, FF], F32, kind="ExternalInput").ap()
    wd_d = dt("w_down", [FF, D], F32, kind="ExternalInput").ap()
    y_d = dt("y", [TOK, D], F32, kind="ExternalOutput").ap()
    dbg_d = {}
    for name, shape in dbg:
        dbg_d[name] = dt("dbg_" + name, list(shape), F32, kind="ExternalOutput").ap()

    es = ExitStack()
    with es:
        S = Sched(nc, es)
        sb = lambda name, shape, dtype: es.enter_context(nc.sbuf_tensor("s_" + name, shape, dtype))
        resid = sb("resid", [128, 16, D], F32)
        h2T = sb("h2T", [128, 8, TOK], BF16)
        ident = sb("ident", [128, 128], BF16)
        bands = sb("bands", [128, 16, 128], BF16)
        cst = sb("cst", [128, 3, NT, 32], F32)
        modT = sb("modT", [128, 48], F32)
        g1T = sb("g1T", [128, 8], F32)
        g2T = sb("g2T", [128, 8], F32)
        n1T = sb("n1T", [128, 8], F32)
        n2T = sb("n2T", [128, 8], F32)
        bfm = sb("bfm", [128, 48], F32)
        gate = sb("gate", [128, D], F32)
        esink = sb("esink", [128, 8], F32)
        flag = sb("flag", [128, 1], F32)
        psc = sb("psc", [128, 4], F32)
        ss = sb("ss", [128, 64], F32)
        ms = sb("ms", [128, 64], F32)
        rstd = sb("rstd", [128, 64], F32)
        neghalf = sb("neghalf", [128, 1], F32)
        sc2 = sb("sc2", [128, 8, 2], BF16)
        wst = sb("wst", [128, 2, D], F32)
        ring = sb("ring", [128, 3, 8, 256], BF16)
        ps = [es.enter_context(nc.psum_tensor("ps%d" % i, [128, 512], F32)) for i in range(8)]

        wg_v = wg_d.rearrange("(k p) n -> p k n", p=128)
        wu_v = wu_d.rearrange("(k p) n -> p k n", p=128)
        G0 = GROUPS[0] * 128

        def bfv(bank):
            return ps[bank][:, :].bitcast(BF16).rearrange("p (k t) -> p k t", t=128)

        scr_i = [0]

        def scratch():
            scr_i[0] ^= 1
            return 6 + scr_i[0]

        def dump(name, src_ap, key, rows=None):
            if name in dbg_d:
                S.dma("sp", lambda: nc.sync.dma_start(out=dbg_d[name], in_=src_ap), reads=[key])

        es1 = ExitStack()
        with es1:
            sb1 = lambda name, shape, dtype: es1.enter_context(nc.sbuf_tensor("s_" + name, shape, dtype))
            cT = sb1("cT", [128, 8], F32)
            scb = sb1("scb", [128, 8, 128], BF16)
            posi = sb1("posi", [128, NT], I32)
            S.dma("sp", lambda: nc.sync.dma_start(out=cT[:], in_=c_d[:, :]), writes=["cT"])
            S.dma("sp", lambda: nc.sync.dma_start(out=bfm[:], in_=bfm_d[:, :]), writes=["bfm"])
            S.dma("sp", lambda: nc.sync.dma_start(out=n1T[:], in_=n1_d[:, :]), writes=["n1T"])
            S.dma("sp", lambda: nc.sync.dma_start(out=n2T[:], in_=n2_d[:, :]), writes=["n2T"])
            S.dma("sp", lambda: nc.sync.dma_start(out=flag[:], in_=flag_d[:, :]), writes=["flag"])
            S.dma("sp", lambda: nc.sync.dma_start(out=psc[:], in_=psc_d[:, :]), writes=["psc"])
            S.dma("sp", lambda: nc.sync.dma_start(out=posi[:], in_=pos_d[:, :]), writes=["posi"])
            S.dma("sp", lambda: nc.sync.dma_start(out=esink[:], in_=sinks_d[0:1, :].to_broadcast([128, 8])),
                  writes=["esink"])

            S.op("act", lambda: nc.scalar.activation(out=cT[:], in_=cT[:], func=AF.Silu), reads=["cT"], writes=["cT"])
            S.op("dve", lambda: nc.vector.tensor_copy(out=sc2[:], in_=cT[:].unsqueeze(2).to_broadcast([128, 8, 2])),
                 reads=["cT"], writes=["sc2"])
            S.op("dve", lambda: nc.vector.tensor_copy(out=scb[:], in_=cT[:].unsqueeze(2).to_broadcast([128, 8, 128])),
                 reads=["cT"], writes=["scb"])
            S.op("act", lambda: nc.scalar.activation(out=esink[:], in_=esink[:], func=AF.Exp),
                 reads=["esink"], writes=["esink"])
            S.op("pool", lambda: nc.gpsimd.memset(neghalf[:], -0.5), writes=["neghalf"])

            wada_v = wada_d.rearrange("(k p) n -> p k n", p=128)
            mod_state = {"next_dma": 0, "next_pe": 0}
            order = list(range(24))

            def mod_dma():
                b = mod_state["next_dma"]
                if b >= 24:
                    return
                mod_state["next_dma"] += 1
                slot = b % 3
                S.dma("pool", lambda: nc.gpsimd.dma_start(out=ring[:, slot, :, :],
                                                          in_=wada_v[:, :, b * 256:(b + 1) * 256]),
                      writes=[("ring", slot)])

            def mod_pe():
                b = mod_state["next_pe"]
                if b >= 24:
                    return
                mod_state["next_pe"] += 1
                slot = b % 3
                v = b // 4
                off = (b % 4) * 256
                bank = b if b < 8 else scratch()
                if v in (2, 5):
                    gt = gate
                    gk = "gate"

                    def f():
                        for k in range(8):
                            i_ = nc.tensor.matmul(ps[bank][:, 0:256], lhsT=scb[:, k, :],
                                                  rhs=ring[:, slot, k, :],
                                                  start=(k == 0), stop=(k == 7))
                        return i_
                    S.op("pe", f, reads=["scb", ("ring", slot)], writes=[("ps", bank)])
                    S.op("dve", lambda: nc.vector.tensor_tensor(out=gt[:, off:off + 256], in0=ps[bank][:, 0:256],
                                                                in1=gt[:, off:off + 256], op=ALU.add),
                         reads=[("ps", bank)], writes=[gk])
                else:
                    def f():
                        for cc in range(2):
                            for k in range(8):
                                i_ = nc.tensor.matmul(ps[bank][:, 2 * cc:2 * cc + 2],
                                                      lhsT=ring[:, slot, k, cc * 128:(cc + 1) * 128],
                                                      rhs=sc2[:, k, 0:2],
                                                      start=(k == 0), stop=(k == 7))
                        return i_
                    S.op("pe", f, reads=["sc2", ("ring", slot)], writes=[("ps", bank)])
                    col = v * 8 + (b % 4) * 2
                    S.op("dve", lambda: nc.vector.tensor_tensor(
                        out=modT[:, col:col + 2], in0=ps[bank][:, 0:4].rearrange("p (c t) -> p c t", t=2)[:, :, 0],
                        in1=bfm[:, col:col + 2], op=ALU.add),
                        reads=[("ps", bank), "bfm"], writes=["modT"])
                if mod_state["next_dma"] < 8 or mod_state["next_pe"] > 8:
                    mod_dma()

            S.dma("sp", lambda: nc.sync.dma_start(out=gate[:], in_=bada_d[0:1, 2048:3072].to_broadcast([128, D])),
                  writes=["gate"])
            mod_dma()
            mod_dma()
            mod_dma()

            es2 = ExitStack()
            with es2:
                sb2 = lambda name, shape, dtype: es2.enter_context(nc.sbuf_tensor("s_" + name, shape, dtype))
                win = sb2("win", [128, 8, 1280], BF16)
                wout = sb2("wout", [128, 8, D], BF16)
                wpool = sb2("wpool", [128, 4, 128], BF16)
                xhat = sb2("xhat", [128, 1, D], BF16)
                xhat2 = sb2("xhat2", [128, 1, D], BF16)
                h1T = sb2("h1T", [128, 2, 8, 128], BF16)
                tmpA = sb2("tmpA", [128, 640], F32)
                tmpB = sb2("tmpB", [128, 640], F32)
                qkr = sb2("qkr", [128, 2, 640], BF16)
                qT = sb2("qT", [128, 2, 4, 128], BF16)
                kT = sb2("kT", [128, 3, 128], BF16)
                NV = 6
                vaug = sb2("vaug", [128, NV, 2, 65], BF16)
                NU = 4
                upl = sb2("upl", [128, NU, 512], BF16)
                PT = sb2("PT", [128, 1, 2, 2, 512], BF16)
                attn = sb2("attn", [128, 2, 512], BF16)
                dn = sb2("dn", [128, 2, 4], F32)
                rc = sb2("rc", [128, 2, 4], F32)
                pooledT = sb2("pooledT", [128, 2, 4, 128], BF16)
                AT = sb2("AT", [128, 2, 8, 128], BF16)

                win_v = win_d.rearrange("(k p) n -> p k n", p=128)
                def load_win():
                    S.dma("pool", lambda: nc.gpsimd.dma_start(out=win[:, :, 0:640], in_=win_v[:, :, 0:640]),
                          writes=["win"])
                    S.dma("pool", lambda: nc.gpsimd.dma_start(out=win[:, :, 640:1280], in_=win_v[:, :, 640:1280]),
                          writes=["win2"])
                    S.dma("pool", lambda: nc.gpsimd.dma_start(out=wpool[:], in_=wpool_d.rearrange("g c d -> c g d")),
                          writes=["wpool"])

                scr = h2T[:, :, :].rearrange("p k t -> p (k t)").bitcast(F32)
                scri = h2T[:, :, :].rearrange("p k t -> p (k t)").bitcast(I32)
                idf = scr[:, 0:128]
                bm = scr[:, 128:256]
                bt = scr[:, 256:384]
                bt2 = scr[:, 384:512]
                colsc = scr[:, 512:640]
                posf = scr[:, 640:640 + NT]
                invf = scr[:, 672:704]
                thb = scr[:, 704:736]
                A4 = 2 * NT * 32
                v4 = lambda a: a.rearrange("p (s t d) -> p s t d", s=2, d=32)
                ang = v4(scr[:, 1024:1024 + A4])
                kf = v4(scr[:, 2112:2112 + A4])
                fx = v4(scr[:, 3200:3200 + A4])
                ki = v4(scri[:, 4288:4288 + A4])

                def pl(fn, r=(), w=()):
                    S.op("pool", fn, reads=r, writes=w)

                def dv(fn, r=(), w=()):
                    S.op("dve", fn, reads=r, writes=w)

                dmat = scr[:, 5504:5632]
                tpl = scr[:, 5632:5760]
                pl(lambda: nc.gpsimd.iota(dmat, pattern=[[1, 128]], base=0, channel_multiplier=-1,
                                          allow_small_or_imprecise_dtypes=True), w=["dmat"])
                pl(lambda: nc.gpsimd.iota(tpl, pattern=[[1, 128]], base=1, channel_multiplier=0,
                                          allow_small_or_imprecise_dtypes=True), w=["tpl"])
                dv(lambda: nc.vector.tensor_scalar(out=idf, in0=dmat, scalar1=0.0, scalar2=None, op0=ALU.is_equal),
                   r=["dmat"], w=["idf"])
                dv(lambda: nc.vector.tensor_copy(out=ident[:], in_=idf), r=["idf"], w=["ident"])
                def consts_bands(gsel):
                    for g, w in [(g_, (2, 4, 8, 16)[g_]) for g_ in gsel]:
                        hw = (w - 1) / 2.0
                        dv(lambda: nc.vector.tensor_scalar(out=bt, in0=dmat, scalar1=0.0, scalar2=None,
                                                           op0=ALU.is_ge), r=["dmat"], w=["bt"])
                        dv(lambda: nc.vector.scalar_tensor_tensor(out=bm, in0=dmat, scalar=float(w - 1) + 0.25, in1=bt,
                                                                  op0=ALU.is_le, op1=ALU.mult),
                           r=["dmat", "bt"], w=["bm"])
                        dv(lambda: nc.vector.scalar_tensor_tensor(out=bt, in0=bm, scalar=1.0 / w, in1=idf,
                                                                  op0=ALU.mult, op1=ALU.subtract),
                           r=["bm", "idf"], w=["bt"])
                        dv(lambda: nc.vector.tensor_copy(out=bands[:, 0 * 4 + g, :], in_=bt), r=["bt"], w=["bands"])
                        dv(lambda: nc.vector.tensor_scalar(out=colsc, in0=tpl, scalar1=float(w), scalar2=None,
                                                           op0=ALU.min), r=["tpl"], w=["colsc"])
                        dv(lambda: nc.vector.reciprocal(out=colsc, in_=colsc), r=["colsc"], w=["colsc"])
                        dv(lambda: nc.vector.tensor_tensor(out=bt2, in0=bm, in1=colsc, op=ALU.mult),
                           r=["bm", "colsc"], w=["bt2"])
                        dv(lambda: nc.vector.tensor_tensor(out=bt2, in0=bt2, in1=idf, op=ALU.subtract),
                           r=["bt2", "idf"], w=["bt2"])
                        dv(lambda: nc.vector.tensor_tensor(out=bt, in0=bt, in1=bt2, op=ALU.subtract),
                           r=["bt", "bt2"], w=["bt"])
                        dv(lambda: nc.vector.scalar_tensor_tensor(out=bt, in0=bt, scalar=flag[:, 0:1], in1=bt2,
                                                                  op0=ALU.mult, op1=ALU.add),
                           r=["bt", "bt2", "flag"], w=["bt"])
                        dv(lambda: nc.vector.tensor_copy(out=bands[:, 2 * 4 + g, :], in_=bt), r=["bt"], w=["bands"])
                        dv(lambda: nc.vector.tensor_scalar(out=bm, in0=dmat, scalar1=float(w - 129) + 0.25,
                                                           scalar2=1.0 / w, op0=ALU.is_le, op1=ALU.mult),
                           r=["dmat"], w=["bm"])
                        dv(lambda: nc.vector.tensor_copy(out=bands[:, 1 * 4 + g, :], in_=bm), r=["bm"], w=["bands"])
                        dv(lambda: nc.vector.tensor_scalar(out=bands[:, 3 * 4 + g, :], in0=bm, scalar1=flag[:, 0:1],
                                                           scalar2=None, op0=ALU.mult), r=["bm", "flag"], w=["bands"])


                def consts_rope():
                    pl(lambda: nc.gpsimd.iota(invf[:], pattern=[[1, 32]], base=0, channel_multiplier=0,
                                              allow_small_or_imprecise_dtypes=True), w=["invf"])
                    pl(lambda: nc.gpsimd.tensor_scalar(out=invf[:], in0=invf[:], scalar1=-1.0 / 32.0, scalar2=0.0,
                                                       op0=ALU.mult, op1=ALU.add), r=["invf"], w=["invf"])
                    pl(lambda: nc.gpsimd.memset(thb[:], 10000.0), w=["thb"])
                    pl(lambda: nc.gpsimd.tensor_tensor(out=invf[:], in0=thb[:], in1=invf[:], op=ALU.pow),
                       r=["invf", "thb"], w=["invf"])

                def consts_rope_dve():
                    S.op("dve", lambda: nc.vector.tensor_copy(out=posf[:], in_=posi[:]), reads=["posi"], writes=["posf"])
                    S.op("dve", lambda: nc.vector.tensor_tensor(
                        out=ang[:, 0, :, :], in0=posf[:].unsqueeze(2).to_broadcast([128, NT, 32]),
                        in1=invf[:].unsqueeze(1).to_broadcast([128, NT, 32]), op=ALU.mult),
                        reads=["posf", "invf"], writes=["ang"])
                    S.op("dve", lambda: nc.vector.tensor_scalar(out=ang[:, 1, :, :], in0=ang[:, 0, :, :],
                                                                scalar1=math.pi / 2, scalar2=None, op0=ALU.add),
                         reads=["ang"], writes=["ang"])
                    TWO_PI_HI = 6.28125
                    TWO_PI_LO = 2.0 * math.pi - 6.28125
                    S.op("dve", lambda: nc.vector.tensor_scalar(out=kf[:], in0=ang[:], scalar1=1.0 / (2 * math.pi),
                                                                scalar2=None, op0=ALU.mult), reads=["ang"], writes=["kf"])
                    S.op("dve", lambda: nc.vector.tensor_copy(out=ki[:], in_=kf[:]), reads=["kf"], writes=["ki"])
                    S.op("dve", lambda: nc.vector.tensor_copy(out=kf[:], in_=ki[:]), reads=["ki"], writes=["kf"])
                    S.op("dve", lambda: nc.vector.scalar_tensor_tensor(out=ang[:], in0=kf[:], scalar=-TWO_PI_HI,
                                                                       in1=ang[:], op0=ALU.mult, op1=ALU.add),
                         reads=["kf", "ang"], writes=["ang"])
                    S.op("dve", lambda: nc.vector.scalar_tensor_tensor(out=ang[:], in0=kf[:], scalar=-TWO_PI_LO,
                                                                       in1=ang[:], op0=ALU.mult, op1=ALU.add),
                         reads=["kf", "ang"], writes=["ang"])
                    S.op("dve", lambda: nc.vector.tensor_scalar(out=fx[:], in0=ang[:], scalar1=math.pi,
                                                                scalar2=-2 * math.pi, op0=ALU.is_gt, op1=ALU.mult),
                         reads=["ang"], writes=["fx"])
                    S.op("dve", lambda: nc.vector.tensor_tensor(out=ang[:], in0=ang[:], in1=fx[:], op=ALU.add),
                         reads=["ang", "fx"], writes=["ang"])
                    S.op("dve", lambda: nc.vector.tensor_scalar(out=fx[:], in0=ang[:], scalar1=-math.pi,
                                                                scalar2=2 * math.pi, op0=ALU.is_lt, op1=ALU.mult),
                         reads=["ang"], writes=["fx"])
                    S.op("dve", lambda: nc.vector.tensor_tensor(out=ang[:], in0=ang[:], in1=fx[:], op=ALU.add),
                         reads=["ang", "fx"], writes=["ang"])
                    S.op("dve", lambda: nc.vector.tensor_scalar(out=ang[:], in0=ang[:], scalar1=math.pi,
                                                                scalar2=-math.pi, op0=ALU.min, op1=ALU.max),
                         reads=["ang"], writes=["ang"])
                    S.op("act", lambda: nc.scalar.activation(out=cst[:, 1, :, :], in_=ang[:, 0, :, :], func=(AF.Copy if "nosin" in DBG_SKIP else AF.Sin)),
                         reads=["ang"], writes=["cst"])
                    S.op("act", lambda: nc.scalar.activation(out=cst[:, 0, :, :], in_=ang[:, 1, :, :], func=(AF.Copy if "nosin" in DBG_SKIP else AF.Sin)),
                         reads=["ang"], writes=["cst"])
                    S.op("dve", lambda: nc.vector.tensor_scalar(out=cst[:, 2, :, :], in0=cst[:, 1, :, :], scalar1=-1.0,
                                                                scalar2=None, op0=ALU.mult),
                         reads=["cst"], writes=["cst"])

                def consts_done():
                    S.op("dve", lambda: nc.vector.memset(ss[:, 63:64], 0.0),
                         writes=["idf", "bm", "bt", "bt2", "colsc", "posf", "invf", "thb", "ang", "kf", "ki", "fx",
                                 "dmat", "tpl", "h2Tscr_done"])

                S.op("pool", lambda: nc.gpsimd.memset(vaug[:], 1.0), writes=[("vaug", s) for s in range(NV)])

                consts_bands((0,))
                for b_ in range(8):
                    mod_pe()
                    if b_ == 1:
                        consts_rope()
                        consts_bands((1,))
                        consts_rope_dve()
                    if b_ == 4:
                        load_win()
                consts_bands((2,))
                S.op("dve", lambda: nc.vector.scalar_tensor_tensor(out=g1T[:], in0=modT[:, 8:16], scalar=1.0,
                                                                   in1=n1T[:], op0=ALU.add, op1=ALU.mult),
                     reads=["modT", "n1T"], writes=["g1T"])
                mod_ready = {"g2": False}

                def xt_ap(i):
                    return wst[:, 1, :] if i == 0 else resid[:, i - 1, :]

                def xkey(i):
                    return ("wst", 1) if i == 0 else ("resid", i - 1)

                def st_load(i):
                    if i == 0:
                        S.dma("sp", lambda: nc.sync.dma_start(out=xt_ap(0), in_=x_d[0:128, :]), writes=[xkey(0)])
                    elif i % 2 == 1:
                        thr = [("h1T", (i - 3) % 2, 7)] if i >= 3 else []
                        S.dma("sp", lambda: nc.sync.dma_start(
                            out=resid[:, i - 1:i + 1, :],
                            in_=x_d[i * 128:(i + 2) * 128, :].rearrange("(c p) d -> p c d", p=128)),
                            reads=thr, writes=[("resid", i - 1), ("resid", i)])

                def rstd_ops(col, src_ap, src_key, junk_ap, junk_key):
                    S.op("act", lambda: nc.scalar.activation(out=junk_ap, in_=src_ap, func=AF.Square,
                                                             **({} if "noacc" in DBG_SKIP else {"accum_out": ss[:, col:col + 1]})),
                         reads=[src_key], writes=[junk_key, ("ss", col)], sync_waw=[junk_key])
                    S.op("pool", lambda: nc.gpsimd.tensor_scalar(out=ms[:, col:col + 1], in0=ss[:, col:col + 1],
                                                                 scalar1=1.0 / D, scalar2=EPS, op0=ALU.mult, op1=ALU.add),
                         reads=[("ss", col)], writes=[("ms", col)])
                    S.op("pool", lambda: nc.gpsimd.tensor_tensor(out=rstd[:, col:col + 1], in0=ms[:, col:col + 1],
                                                                 in1=neghalf[:, 0:1], op=ALU.pow),
                         reads=[("ms", col), "neghalf"], writes=[("rstd", col)])

                def st_norm(i):
                    s = i % 2
                    if "norm" in DBG_SKIP and i >= 1:
                        return
                    rstd_ops(i, xt_ap(i), xkey(i), xhat[:, 0, :], ("xhat", 0))
                    S.op("pool", lambda: nc.gpsimd.tensor_scalar(out=xhat[:, 0, :], in0=xt_ap(i), scalar1=rstd[:, i:i + 1],
                                                                 scalar2=0.0, op0=ALU.mult, op1=ALU.add),
                         reads=[xkey(i), ("rstd", i)], writes=[("xhat", 0)])

                def transp_mod(src_ap_fn, src_key, gT, shT_ap, gkeys, dst_fn, dst_key, ev_eng="dve"):
                    def f():
                        for k in range(8):
                            i_ = nc.tensor.transpose(out=bfv(0)[:, k, :], in_=src_ap_fn(k), identity=ident[:])
                        return i_
                    if "trxpe" not in DBG_SKIP:
                        S.op("pe", f, reads=[src_key, "ident"], writes=[("ps", 0)])
                    for k in range(8):
                        if "trxev" in DBG_SKIP:
                            break
                        if ev_eng == "act":
                            S.op("act", lambda: nc.scalar.activation(out=dst_fn(k), in_=bfv(0)[:, k, :], func=AF.Identity,
                                                                     scale=gT[:, k:k + 1], bias=shT_ap[:, k:k + 1]),
                                 reads=[("ps", 0)] + gkeys, writes=[dst_key + (k,)])
                        else:
                            S.op("dve", lambda: nc.vector.tensor_scalar(out=dst_fn(k), in0=bfv(0)[:, k, :],
                                                                        scalar1=gT[:, k:k + 1], scalar2=shT_ap[:, k:k + 1],
                                                                        op0=ALU.mult, op1=ALU.add),
                                 reads=[("ps", 0)] + gkeys, writes=[dst_key + (k,)])

                def st_trx(i):
                    s = i % 2
                    if "trx" in DBG_SKIP:
                        return
                    transp_mod(lambda k: xhat[:, 0, k * 128:(k + 1) * 128], ("xhat", 0), g1T, modT[:, 0:8],
                               ["g1T", "modT"], lambda k: h1T[:, s, k, :], ("h1T", s),
                               ev_eng=("act" if i <= 3 else "dve"))

                inproj_bank = [1]

                def st_inproj(i, bsel=(0, 1, 2)):
                    s = i % 2
                    if "noinproj" in DBG_SKIP:
                        return
                    su = i % NU
                    sv = i % NV
                    hkeys = [("h1T", s, k) for k in range(8)]
                    banks = []
                    for (c0, w, wk) in [((0, 512, "win"), (512, 512, "win2"), (1024, 256, "win2"))[b_] for b_ in bsel]:
                        bank = inproj_bank[0]
                        inproj_bank[0] = 3 - bank
                        banks.append(bank)

                        def f():
                            for k in range(8):
                                i_ = nc.tensor.matmul(ps[bank][:, 0:w], lhsT=h1T[:, s, k, :], rhs=win[:, k, c0:c0 + w],
                                                      start=(k == 0), stop=(k == 7))
                            return i_
                        S.op("pe", f, reads=hkeys + (["win"] if c0 == 0 else ["win", "win2"] if c0 == 512 else ["win2"]),
                             writes=[("ps", bank)])
                        if c0 == 0:
                            rope(i, bank, 0, 8)
                        elif c0 == 512:
                            rope(i, bank, 512, 2)
                            if "novu" in DBG_SKIP:
                                continue
                            S.op("act", lambda: nc.scalar.activation(
                                out=vaug[:, sv, :, 0:64], in_=ps[bank][:, 128:256].rearrange("p (g d) -> p g d", d=64),
                                func=AF.Copy), reads=[("ps", bank)], writes=[("vaug", sv)])
                            S.op("act", lambda: nc.scalar.activation(out=upl[:, su, 0:256], in_=ps[bank][:, 256:512],
                                                                     func=AF.Copy),
                                 reads=[("ps", bank)], writes=[("upl", su, 0)])
                        elif "novu" not in DBG_SKIP:
                            S.op("act", lambda: nc.scalar.activation(out=upl[:, su, 256:512], in_=ps[bank][:, 0:256],
                                                                     func=AF.Copy),
                                 reads=[("ps", bank)], writes=[("upl", su, 1)])
                    s2 = i % 2
                    for g in (range(2) if 0 in bsel else ()):
                        S.op("pool", lambda: nc.gpsimd.tensor_tensor(
                            out=qkr[:, s2, 0:512].rearrange("p (c g d) -> p g c d", g=2, d=64)[:, g, :, :],
                            in0=tmpA[:, g * 256:(g + 1) * 256].rearrange("p (c d) -> p c d", d=64),
                            in1=tmpB[:, g * 256:(g + 1) * 256].rearrange("p (c d) -> p c d", d=64), op=ALU.add),
                            reads=["tmpA0", "tmpB0", "tmpB0a"], writes=[("qkr", s2, g)])
                    if 1 in bsel:
                        S.op("pool", lambda: nc.gpsimd.tensor_tensor(out=qkr[:, s2, 512:640], in0=tmpA[:, 512:640],
                                                                     in1=tmpB[:, 512:640], op=ALU.add),
                             reads=["tmpA512", "tmpB512", "tmpB512a"], writes=[("qkr", s2, 2)])

                def rope(i, bank, off, nh):
                    if "norope" in DBG_SKIP:
                        return
                    wdt = nh * 64
                    src = ps[bank][:, 0:wdt].rearrange("p (h two d) -> p h two d", two=2, d=32)
                    dA = tmpA[:, off:off + wdt].rearrange("p (h two d) -> p h two d", two=2, d=32)
                    dB = tmpB[:, off:off + wdt].rearrange("p (h two d) -> p h two d", two=2, d=32)
                    cos_b = cst[:, 0, i, :].unsqueeze(1).unsqueeze(1).to_broadcast([128, nh, 2, 32])
                    sin_b = cst[:, 1, i, :].unsqueeze(1).to_broadcast([128, nh, 32])
                    nsin_b = cst[:, 2, i, :].unsqueeze(1).to_broadcast([128, nh, 32])
                    ka, kb = "tmpA%d" % off, "tmpB%d" % off
                    S.op("dve", lambda: nc.vector.tensor_tensor(out=dA, in0=src, in1=cos_b, op=ALU.mult),
                         reads=[("ps", bank), "cst"], writes=[ka])
                    S.op("dve", lambda: nc.vector.tensor_tensor(out=dB[:, :, 0, :], in0=src[:, :, 1, :], in1=nsin_b,
                                                                op=ALU.mult),
                         reads=[("ps", bank), "cst"], writes=[kb + "a"])
                    S.op("dve", lambda: nc.vector.tensor_tensor(out=dB[:, :, 1, :], in0=src[:, :, 0, :], in1=sin_b,
                                                                op=ALU.mult),
                         reads=[("ps", bank), "cst"], writes=[kb])

                def st_qkT(i):
                    s = i % 2
                    sk = i % 3
                    bank = scratch()
                    qv = qkr[:, s, :].rearrange("p (h d) -> p h d", d=64)

                    def f():
                        for c in range(4):
                            nc.tensor.transpose(out=bfv(bank)[:, c, :], in_=qkr[:, s, c * 128:(c + 1) * 128],
                                                identity=ident[:])
                        return nc.tensor.transpose(out=bfv(bank)[:, 4, :], in_=qkr[:, s, 512:640], identity=ident[:])
                    S.op("pe", f, reads=[("qkr", s, 0), ("qkr", s, 1), ("qkr", s, 2), "ident"], writes=[("ps", bank)])
                    S.op("dve", lambda: nc.vector.tensor_copy(out=qT[:, s, :, :], in_=bfv(bank)[:, 0:4, :]),
                         reads=[("ps", bank)], writes=[("qT", s)])
                    S.op("act", lambda: nc.scalar.activation(out=kT[:, sk, :], in_=bfv(bank)[:, 4, :], func=AF.Copy),
                         reads=[("ps", bank)], writes=[("kT", sk)])

                def st_S(i, gsel=(0, 1)):
                    s = i % 2
                    sp0 = 0
                    skc, skp = i % 3, (i - 1) % 3
                    for g in gsel:
                        pr = slice(64 * g, 64 * g + 64)
                        for jj, sk in ((0, skp), (1, skc)):
                            bank = 3 + jj
                            S.op("pe", lambda: nc.tensor.matmul(ps[bank][:, :], lhsT=kT[pr, sk, :],
                                                                rhs=qT[pr, s, :, :], start=True, stop=True),
                                 reads=[("kT", sk), ("qT", s)], writes=[("ps", bank)])
                            S.op("act", lambda: nc.scalar.activation(out=PT[:, 0, g, jj, :], in_=ps[bank][:, :],
                                                                     func=AF.Exp, scale=0.125),
                                 reads=[("ps", bank)], writes=[("PT", 0, g, jj)])
                    if 1 not in gsel:
                        return
                    S.op("pool", lambda: nc.gpsimd.affine_select(
                        out=PT[:, 0, :, 0, :], in_=PT[:, 0, :, 0, :], compare_op=ALU.is_gt, fill=0.0, base=0,
                        pattern=[[0, 2], [0, 4], [-1, 128]], channel_multiplier=1),
                        reads=[("PT", 0, 0, 0), ("PT", 0, 1, 0)], writes=[("PT", 0, 0, 0), ("PT", 0, 1, 0)])
                    S.op("pool", lambda: nc.gpsimd.affine_select(
                        out=PT[:, 0, :, 1, :], in_=PT[:, 0, :, 1, :], compare_op=ALU.is_ge, fill=0.0, base=0,
                        pattern=[[0, 2], [0, 4], [1, 128]], channel_multiplier=-1),
                        reads=[("PT", 0, 0, 1), ("PT", 0, 1, 1)], writes=[("PT", 0, 0, 1), ("PT", 0, 1, 1)])
                    if i == 1:
                        S.op("pool", lambda: nc.gpsimd.tensor_scalar(
                            out=PT[:, 0, :, 0, :], in0=PT[:, 0, :, 0, :], scalar1=flag[:, 0:1], scalar2=0.0,
                            op0=ALU.mult, op1=ALU.add),
                            reads=[("PT", 0, 0, 0), ("PT", 0, 1, 0), "flag"], writes=[("PT", 0, 0, 0), ("PT", 0, 1, 0)])

                def st_PV(i, gsel=(0, 1)):
                    s = i % 2
                    svc, svp = i % NV, (i - 1) % NV
                    for g in gsel:
                        O = ps[5][:, 0:260].rearrange("p (c e) -> p c e", e=65)

                        def f():
                            for c in range(4):
                                nc.tensor.matmul(O[:, c, :], lhsT=PT[:, 0, g, 0, c * 128:(c + 1) * 128],
                                                 rhs=vaug[:, svp, g, :], start=True, stop=False)
                                i_ = nc.tensor.matmul(O[:, c, :], lhsT=PT[:, 0, g, 1, c * 128:(c + 1) * 128],
                                                      rhs=vaug[:, svc, g, :], start=False, stop=True)
                            return i_
                        S.op("pe", f, reads=[("PT", 0, g, 0), ("PT", 0, g, 1), ("vaug", svp), ("vaug", svc)],
                             writes=[("ps", 5)])
                        S.op("dve", lambda: nc.vector.tensor_tensor(out=dn[:, g, :], in0=O[:, :, 64],
                                                                    in1=esink[:, 4 * g:4 * g + 4], op=ALU.add),
                             reads=[("ps", 5), "esink"], writes=[("dn", g)])
                        S.op("dve", lambda: nc.vector.reciprocal(out=rc[:, g, :], in_=dn[:, g, :]),
                             reads=[("dn", g)], writes=[("rc", g)])
                        av = attn[:, s, :].rearrange("p (h d) -> p h d", d=64)
                        S.op("dve", lambda: nc.vector.tensor_tensor(
                            out=av[:, 4 * g:4 * g + 4, :], in0=O[:, :, 0:64],
                            in1=rc[:, g, :].unsqueeze(2).to_broadcast([128, 4, 64]), op=ALU.mult),
                            reads=[("ps", 5), ("rc", g)], writes=[("attn", s, g)])

                def st_attnT(i):
                    s = i % 2
                    sa = i % 2
                    bank = scratch()

                    def f():
                        for cc in range(4):
                            i_ = nc.tensor.transpose(out=bfv(bank)[:, cc, :], in_=attn[:, s, cc * 128:(cc + 1) * 128],
                                                     identity=ident[:])
                        return i_
                    S.op("pe", f, reads=[("attn", s, 0), ("attn", s, 1), "ident"], writes=[("ps", bank)])
                    S.op("act", lambda: nc.scalar.activation(out=AT[:, sa, 0:4, :], in_=bfv(bank)[:, 0:4, :], func=AF.Copy),
                         reads=[("ps", bank)], writes=[("AT", sa, 0)])

                def st_pool1(i):
                    suc, sup = i % NU, (i - 1) % NU
                    sp_ = i % 2
                    bank = scratch()
                    kp, kc = (3, 2) if i == 1 else (1, 0)
                    Y = ps[bank][:, :].rearrange("p (g t) -> p g t", t=128)

                    def f():
                        for g in range(4):
                            nc.tensor.matmul(Y[:, g, :], lhsT=upl[:, sup, g * 128:(g + 1) * 128],
                                             rhs=bands[:, kp * 4 + g, :], start=True, stop=False)
                            i_ = nc.tensor.matmul(Y[:, g, :], lhsT=upl[:, suc, g * 128:(g + 1) * 128],
                                                  rhs=bands[:, kc * 4 + g, :], start=False, stop=True)
                        return i_
                    S.op("pe", f, reads=[("upl", sup, 0), ("upl", sup, 1), ("upl", suc, 0), ("upl", suc, 1), "bands"],
                         writes=[("ps", bank)])
                    S.op("act", lambda: nc.scalar.activation(out=pooledT[:, sp_, :, :], in_=Y, func=AF.Copy),
                         reads=[("ps", bank)], writes=[("pooledT", sp_)])

                def st_pool2(i):
                    sp_ = i % 2
                    sa = i % 2
                    bank = scratch()
                    Z = ps[bank][:, :].rearrange("p (g t) -> p g t", t=128)

                    def f():
                        for g in range(4):
                            i_ = nc.tensor.matmul(Z[:, g, :], lhsT=wpool[:, g, :], rhs=pooledT[:, sp_, g, :],
                                                  start=True, stop=True)
                        return i_
                    S.op("pe", f, reads=[("pooledT", sp_), "wpool"], writes=[("ps", bank)])
                    S.op("dve", lambda: nc.vector.tensor_tensor(out=AT[:, sa, 4:8, :], in0=Z,
                                                                in1=psc[:].unsqueeze(2).to_broadcast([128, 4, 128]),
                                                                op=ALU.mult),
                         reads=[("ps", bank), "psc"], writes=[("AT", sa, 1)])

                def st_outproj(i, hsel=(0, 1)):
                    sa = i % 2
                    for hc in hsel:
                        bank = scratch()

                        def f():
                            for k in range(8):
                                i_ = nc.tensor.matmul(ps[bank][:, :], lhsT=AT[:, sa, k, :],
                                                      rhs=wout[:, k, hc * 512:(hc + 1) * 512],
                                                      start=(k == 0), stop=(k == 7))
                            return i_
                        S.op("pe", f, reads=[("AT", sa, 0), ("AT", sa, 1)] + [("wout", k) for k in range(8)],
                             writes=[("ps", bank)])
                        S.op("dve", lambda: nc.vector.tensor_tensor(
                            out=resid[:, i - 1, hc * 512:(hc + 1) * 512], in0=ps[bank][:, :],
                            in1=resid[:, i - 1, hc * 512:(hc + 1) * 512], op=ALU.add),
                            reads=[("ps", bank), ("resid", i - 1)], writes=[("resid", i - 1)])

                def st_norm2(i):
                    col = 20 + i
                    s = i % 2
                    rstd_ops(col, resid[:, i - 1, :], ("resid", i - 1), xhat2[:, 0, :], ("xhat2", 0))
                    if i >= 12:
                        S.op("dve", lambda: nc.vector.tensor_scalar(out=xhat2[:, 0, :], in0=resid[:, i - 1, :],
                                                                    scalar1=rstd[:, col:col + 1], scalar2=None,
                                                                    op0=ALU.mult),
                             reads=[("resid", i - 1), ("rstd", col)], writes=[("xhat2", 0)])
                    else:
                        S.op("pool", lambda: nc.gpsimd.tensor_scalar(out=xhat2[:, 0, :], in0=resid[:, i - 1, :],
                                                                     scalar1=rstd[:, col:col + 1], scalar2=0.0,
                                                                     op0=ALU.mult, op1=ALU.add),
                             reads=[("resid", i - 1), ("rstd", col)], writes=[("xhat2", 0)])

                def st_trx2(i):
                    s = i % 2
                    if not mod_ready["g2"]:
                        S.op("dve", lambda: nc.vector.scalar_tensor_tensor(out=g2T[:], in0=modT[:, 32:40], scalar=1.0,
                                                                           in1=n2T[:], op0=ALU.add, op1=ALU.mult),
                             reads=["modT", "n2T"], writes=["g2T"])
                        mod_ready["g2"] = True
                    transp_mod(lambda k: xhat2[:, 0, k * 128:(k + 1) * 128], ("xhat2", 0), g2T, modT[:, 24:32],
                               ["g2T", "modT"] + (["h2Tscr_done"] if i == 1 else []),
                               lambda k: h2T[:, k, (i - 1) * 128:i * 128], ("h2T", i - 1), ev_eng="act")

                wout_state = {"k": 0}

                def wout_fold():
                    k = wout_state["k"]
                    if k >= 8 or "wout" in DBG_SKIP:
                        return
                    wout_state["k"] += 2
                    S.dma("sp", lambda: nc.sync.dma_start(
                        out=wst[:, 0:2, :], in_=wout_d[k * 128:(k + 2) * 128, :].rearrange("(c p) d -> p c d", p=128)),
                        writes=[("wst", 0), ("wst", 1)])
                    for slot in range(2):
                        S.op("pool", lambda: nc.gpsimd.tensor_tensor(out=wout[:, k + slot, :], in0=wst[:, slot, :],
                                                                     in1=gate[:, :], op=ALU.mult),
                             reads=[("wst", slot), "gate"], writes=[("wout", k + slot)])
                    k += 1
                    if k == 7:
                        S.dma("sp", lambda: nc.sync.dma_start(out=gate[:],
                                                              in_=bada_d[0:1, 5120:6144].to_broadcast([128, D])),
                              reads=[], writes=["gate"])

                stages = [
                    (6, 1, st_PV, ((0,),)),
                    (5, 1, st_S, ((0,),)),
                    (8, 1, st_outproj, ((0,),)),
                    (10, 1, st_trx2, ()),
                    (9, 1, st_norm2, ()),
                    (6, 1, st_PV, ((1,),)),
                    (5, 1, st_S, ((1,),)),
                    (8, 1, st_outproj, ((1,),)),
                    (4, 0, st_qkT, ()),
                    (3, 0, st_inproj, ((0,),)),
                    (7, 1, st_attnT, ()),
                    (3, 0, st_inproj, ((1,),)),
                    (6, 1, st_pool2, ()),
                    (3, 0, st_inproj, ((2,),)),
                    (5, 1, st_pool1, ()),
                    (2, 0, st_trx, ()),
                    (1, 0, st_norm, ()),
                    (0, 0, st_load, ()),
                ]
                depth = max(o for o, _, _, _ in stages)
                for step in range((NT + depth) if nsteps is None else nsteps):
                    if stop == "setup":
                        break
                    order = stages
                    if step >= NT:
                        order = [st_ for st_ in stages if st_[2] is not st_norm2]
                        k_ = [j for j, st_ in enumerate(order) if st_[2] is st_trx][0]
                        order.insert(k_, (9, 1, st_norm2, ()))
                    for off, first, fn, extra in order:
                        i = step - off
                        if first <= i < NT:
                            fn(i, *extra)
                    if step == 1:
                        for _ in range(3):
                            mod_dma()
                    if step >= 3 and not ("modp1" in DBG_SKIP and step >= 2):
                        for _ in range(2):
                            if mod_state["next_pe"] < 20 or wout_state["k"] == 8:
                                mod_pe()
                    if mod_state["next_pe"] >= 12:
                        wout_fold()
                    if step == 0:
                        consts_bands((3,))
                        consts_done()
                    if mod_state["next_pe"] == 24 and not mod_state.get("g0"):
                        mod_state["g0"] = True
                        S.dma("pool", lambda: nc.gpsimd.dma_start(out=ring[:, 0, :, :], in_=wg_v[:, :, 0:G0]),
                              writes=[("ring", 0)])
                        S.dma("pool", lambda: nc.gpsimd.dma_start(out=ring[:, 1, :, :], in_=wu_v[:, :, 0:G0]),
                              writes=[("ring", 1)])
                while mod_state["next_pe"] < 24 and "modp1" not in DBG_SKIP:
                    mod_pe()
                if "x1" in dbg_d:
                    S.dma("sp", lambda: nc.sync.dma_start(out=dbg_d["x1"].rearrange("(t p) d -> p t d", p=128),
                                                          in_=resid[:, :, :]), reads=[("resid", t) for t in range(16)])
                S.barrier()
        es4 = ExitStack()
        with es4:
            sb4 = lambda name, shape, dtype: es4.enter_context(nc.sbuf_tensor("s_" + name, shape, dtype))
            wg = sb4("wg", [128, 2, 8, 512], BF16)
            wu = sb4("wu", [128, 2, 8, 512], BF16)
            wd = sb4("wd", [128, 2, 4, D], BF16)
            sg = sb4("sg", [128, 2, 512], F32)
            aT = sb4("aT", [128, 2, 4, 512], BF16)
            ybuf = sb4("ybuf", [128, 2, D], F32)
            yjunk = sb4("yjunk", [128, D], BF16)
            normf = sb4("normf", [128, D], F32)
            S.dma("sp", lambda: nc.sync.dma_start(out=normf[:], in_=nf_d[0:1, :].to_broadcast([128, D])),
                  writes=["normf"])
            gstart = [sum(GROUPS[:i]) for i in range(len(GROUPS))]
            wst_i = [0]

            def load_group(gi):
                b = gi % 2
                ncg = GROUPS[gi]
                f0 = gstart[gi]
                if gi > 0 or not mod_state.get("g0"):
                    S.dma("pool", lambda: nc.gpsimd.dma_start(out=wg[:, b, :, 0:ncg * 128],
                                                              in_=wg_v[:, :, f0 * 128:(f0 + ncg) * 128]),
                          writes=[("wg", b)])
                    S.dma("pool", lambda: nc.gpsimd.dma_start(out=wu[:, b, :, 0:ncg * 128],
                                                              in_=wu_v[:, :, f0 * 128:(f0 + ncg) * 128]),
                          writes=[("wu", b)])
                for j in range(0, ncg, 2):
                    f = f0 + j
                    S.dma("sp", lambda: nc.sync.dma_start(
                        out=wst[:, 0:2, :], in_=wd_d[f * 128:(f + 2) * 128, :].rearrange("(c p) d -> p c d", p=128)),
                        writes=[("wst", 0), ("wst", 1)])
                    for slot in range(2):
                        S.op("pool", lambda: nc.gpsimd.tensor_tensor(out=wd[:, b, j + slot, :], in0=wst[:, slot, :],
                                                                     in1=gate[:, :], op=ALU.mult),
                             reads=[("wst", slot), "gate"], writes=[("wd", b, j + slot)])

            par = {"gu": 0, "dn": 0, "a": 0, "y": 0}

            def gu_unit(gi, tb):
                b = gi % 2
                ncg = GROUPS[gi]
                if gi == 0 and mod_state.get("g0"):
                    wgs, wus, wgk, wuk = ring[:, 0, :, :], ring[:, 1, :, :], ("ring", 0), ("ring", 1)
                else:
                    wgs, wus, wgk, wuk = wg[:, b, :, :], wu[:, b, :, :], ("wg", b), ("wu", b)
                sa = par["a"]
                par["a"] ^= 1
                for j in range(ncg):
                    p = par["gu"]
                    par["gu"] ^= 1
                    gb, ub = p, 2 + p

                    def fg():
                        for k in range(8):
                            i_ = nc.tensor.matmul(ps[gb][:, :], lhsT=wgs[:, k, j * 128:(j + 1) * 128],
                                                  rhs=h2T[:, k, tb * 512:(tb + 1) * 512], start=(k == 0), stop=(k == 7))
                        return i_

                    def fu():
                        for k in range(8):
                            i_ = nc.tensor.matmul(ps[ub][:, :], lhsT=wus[:, k, j * 128:(j + 1) * 128],
                                                  rhs=h2T[:, k, tb * 512:(tb + 1) * 512], start=(k == 0), stop=(k == 7))
                        return i_
                    hk = [("h2T", t, k) for t in range(tb * 4, tb * 4 + 4) for k in range(8)]
                    S.op("pe", fg, reads=[wgk] + hk, writes=[("ps", gb)])
                    S.op("pe", fu, reads=[wuk] + hk, writes=[("ps", ub)])
                    S.op("act", lambda: nc.scalar.activation(out=sg[:, p, :], in_=ps[gb][:, :], func=AF.Silu),
                         reads=[("ps", gb)], writes=[("sg", p)])
                    S.op("dve", lambda: nc.vector.tensor_tensor(out=aT[:, sa, j, :], in0=ps[ub][:, :], in1=sg[:, p, :],
                                                                op=ALU.mult),
                         reads=[("ps", ub), ("sg", p)], writes=[("aT", sa, j)])
                return sa

            def dn_unit(gi, tb, sa):
                b = gi % 2
                ncg = GROUPS[gi]
                last = gi == len(GROUPS) - 1
                for tt in range(4):
                    tile = tb * 4 + tt
                    for hc in range(2):
                        p = par["dn"]
                        par["dn"] ^= 1
                        bank = 4 + p

                        def f():
                            for j in range(ncg):
                                i_ = nc.tensor.matmul(ps[bank][:, :], lhsT=aT[:, sa, j, tt * 128:(tt + 1) * 128],
                                                      rhs=wd[:, b, j, hc * 512:(hc + 1) * 512],
                                                      start=(j == 0), stop=(j == ncg - 1))
                            return i_
                        S.op("pe", f, reads=[("aT", sa, j) for j in range(ncg)] + [("wd", b, j) for j in range(ncg)],
                             writes=[("ps", bank)])
                        S.op("dve", lambda: nc.vector.tensor_tensor(
                            out=resid[:, tile, hc * 512:(hc + 1) * 512], in0=ps[bank][:, :],
                            in1=resid[:, tile, hc * 512:(hc + 1) * 512], op=ALU.add),
                            reads=[("ps", bank), ("resid", tile)], writes=[("resid", tile)])
                    if last:
                        flush_y()
                        col = 40 + tile
                        sy = par["y"]
                        par["y"] = (sy + 1) % 2
                        rstd_ops(col, resid[:, tile, :], ("resid", tile), yjunk[:, :], "yjunk")
                        deferred_y.append((tile, col, sy))

            deferred_y = []

            def flush_y():
                while deferred_y:
                    tile, col, sy = deferred_y.pop(0)
                    S.op("dve", lambda: nc.vector.scalar_tensor_tensor(
                        out=ybuf[:, sy, :], in0=resid[:, tile, :], scalar=rstd[:, col:col + 1], in1=normf[:, :],
                        op0=ALU.mult, op1=ALU.mult),
                        reads=[("resid", tile), ("rstd", col), "normf"], writes=[("ybuf", sy)])
                    if sy % 2 == 1:
                        S.dma("sp", lambda: nc.sync.dma_start(
                            out=y_d[(tile - 1) * 128:(tile + 1) * 128, :].rearrange("(c p) d -> p c d", p=128),
                            in_=ybuf[:, sy - 1:sy + 1, :]), reads=[("ybuf", sy - 1), ("ybuf", sy)])

            units = [(gi, tb) for gi in range(len(GROUPS)) for tb in range(4)]
            if stop is not None:
                units = []
            else:
                load_group(0)
                load_group(1)
            pending = None
            for n, (gi, tb) in enumerate(units):
                if tb == 3 and gi + 2 < len(GROUPS):
                    pass
                sa = gu_unit(gi, tb)
                if pending is not None:
                    dn_unit(*pending)
                    pg = pending[0]
                    if pending[1] == 3 and pg + 2 < len(GROUPS):
                        load_group(pg + 2)
                pending = (gi, tb, sa)
            if pending is not None:
                dn_unit(*pending)
            flush_y()
            S.barrier()
    return nc


_CACHE = {}


def _prep_inputs(x, c, positions, w_ada, b_ada, norm1, w_in, sinks, w_pool, pool_scale, w_out, norm2,
                 w_gate, w_up, w_down, norm_f):
    f = lambda a: np.ascontiguousarray(np.asarray(a), dtype=np.float32)
    x = f(x)
    positions = np.asarray(positions).astype(np.int32)
    fm = lambda v: np.ascontiguousarray(f(v).reshape(-1, 128).T)
    shared = {
        "w_ada": f(w_ada), "b_fm": fm(b_ada), "b_ada": f(b_ada).reshape(1, -1),
        "norm1": fm(norm1), "norm2": fm(norm2), "norm_f": f(norm_f).reshape(1, -1),
        "w_in": f(w_in), "sinks": f(sinks).reshape(1, 8), "w_pool": f(w_pool),
        "pool_scale": fm(pool_scale), "w_out": f(w_out), "w_gate": f(w_gate), "w_up": f(w_up), "w_down": f(w_down),
    }
    in_maps = []
    per_b = SEQ // TOK
    for core in range(NCORES):
        b, q = divmod(core, per_b)
        s0 = q * TOK
        xs = np.zeros((NT * 128, D), np.float32)
        ps_ = np.zeros((NT * 128,), np.int32)
        if q == 0:
            xs[128:] = x[b, 0:TOK]
            ps_[128:] = positions[b, 0:TOK]
        else:
            xs[:] = x[b, s0 - 128:s0 + TOK]
            ps_[:] = positions[b, s0 - 128:s0 + TOK]
        m = dict(shared)
        m["x"] = xs
        m["pos"] = np.ascontiguousarray(ps_.reshape(NT, 128).T)
        m["c"] = fm(f(c)[b])
        m["flag"] = np.full((128, 1), 0.0 if q == 0 else 1.0, np.float32)
        in_maps.append(m)
    return in_maps


def kernel(**inputs):
    in_maps = _prep_inputs(**inputs)
    if "nc" not in _CACHE:
        _CACHE["nc"] = build()
    res = run_bass_kernel_spmd(_CACHE["nc"], in_maps, core_ids=list(range(NCORES)))
    out = np.empty((2, SEQ, D), np.float32)
    per_b = SEQ // TOK
    for core in range(NCORES):
        b, q = divmod(core, per_b)
        out[b, q * TOK:(q + 1) * TOK] = res.results[core]["y"]
    return out
```

```python
import math
from contextlib import ExitStack

import numpy as np
import concourse.bass as bass
import concourse.mybir as mybir
from concourse.bass_utils import run_bass_kernel_spmd

F32 = mybir.dt.float32
F32R = mybir.dt.float32r
BF16 = mybir.dt.bfloat16
I32 = mybir.dt.int32
ALU = mybir.AluOpType
AF = mybir.ActivationFunctionType

D = 1024
SEQ = 8192
NCORES = 8
TOK = 2048
NT = 17
FF = 2816
NFC = FF // 128
GROUPS = [2, 4, 4, 4, 4, 4]
EPS = 1e-6
NSEM_DMA = {"sp": 46, "pool": 50}
DBG_SKIP = set()


class Ev:
    __slots__ = ("sem", "val", "eng", "dma")

    def __init__(self, sem, val, eng, dma):
        self.sem, self.val, self.eng, self.dma = sem, val, eng, dma


class Sched:
    def __init__(self, nc, es):
        self.nc = nc
        self.engs = {"pe": nc.tensor, "act": nc.scalar, "dve": nc.vector, "pool": nc.gpsimd, "sp": nc.sync}
        self.sem = {e: es.enter_context(nc.semaphore("c_" + e)) for e in ("pe", "act", "dve", "pool")}
        self.cnt = {e: 0 for e in self.sem}
        self.dsem = {q: [es.enter_context(nc.semaphore("d_%s%d" % (q, i))) for i in range(NSEM_DMA[q])]
                     for q in ("sp", "pool")}
        self.dcnt = {q: 0 for q in self.dsem}
        self.dval = {}
        self.waited = {}
        self.lastw = {}
        self.readers = {}

    def _need(self, e, sem, val):
        k = (e, id(sem))
        if self.waited.get(k, 0) < val:
            self.engs[e].wait_ge(sem, val)
            self.waited[k] = val

    def _deps(self, e, reads, writes, true_psw=()):
        need = {}

        def add(ev, raw, psum=False):
            if ev is None:
                return
            if (not ev.dma) and ev.eng == e and not raw and not (psum and e != "pe"):
                return
            k = id(ev.sem)
            if k not in need or need[k][1] < ev.val:
                need[k] = (ev.sem, ev.val)

        for k in reads:
            add(self.lastw.get(k), True)
        for k in writes:
            add(self.lastw.get(k), False, k in true_psw)
            for ev in self.readers.get(k, ()):
                add(ev, False)
        for sem, val in need.values():
            self._need(e, sem, val)

    def _record(self, ev, reads, writes):
        for k in writes:
            self.lastw[k] = ev
            self.readers[k] = []
        for k in reads:
            if k not in writes:
                self.readers.setdefault(k, []).append(ev)

    def op(self, e, fn, reads=(), writes=(), sync_waw=()):
        psr = [k for k in reads if isinstance(k, tuple) and k[0] == "ps"]
        true_psw = [k for k in writes if isinstance(k, tuple) and k[0] == "ps"] + list(sync_waw)
        if psr:
            reads = [k for k in reads if k not in psr]
            writes = list(writes) + [k for k in psr if k not in writes]
        self._deps(e, reads, writes, true_psw)
        inst = fn()
        self.cnt[e] += 1
        inst.then_inc(self.sem[e], 1)
        self._record(Ev(self.sem[e], self.cnt[e], e, False), reads, writes)

    def dma(self, q, fn, reads=(), writes=()):
        self._deps(q, reads, writes)
        j = self.dcnt[q]
        self.dcnt[q] += 1
        slot = j % NSEM_DMA[q]
        sem = self.dsem[q][slot]
        prev = self.dval.get((q, slot), 0)
        if prev:
            self._need(q, sem, prev)
        inst = fn()
        inst.then_inc(sem, 16)
        self.dval[(q, slot)] = prev + 16
        self._record(Ev(sem, prev + 16, q, True), reads, writes)

    def barrier(self, engines=("pe", "act", "dve", "pool", "sp")):
        for e in engines:
            for e2 in self.sem:
                if e2 != e and self.cnt[e2]:
                    self._need(e, self.sem[e2], self.cnt[e2])
            for (q, slot), v in self.dval.items():
                self._need(e, self.dsem[q][slot], v)
        self.lastw.clear()
        self.readers.clear()


def build(dbg=(), stop=None, nsteps=None):
    nc = bass.Bass("TRN2", target_bir_lowering=False)
    dt = nc.dram_tensor
    x_d = dt("x", [NT * 128, D], F32, kind="ExternalInput").ap()
    pos_d = dt("pos", [128, NT], I32, kind="ExternalInput").ap()
    c_d = dt("c", [128, 8], F32, kind="ExternalInput").ap()
    flag_d = dt("flag", [128, 1], F32, kind="ExternalInput").ap()
    wada_d = dt("w_ada", [D, 6 * D], F32, kind="ExternalInput").ap()
    bfm_d = dt("b_fm", [128, 48], F32, kind="ExternalInput").ap()
    bada_d = dt("b_ada", [1, 6 * D], F32, kind="ExternalInput").ap()
    n1_d = dt("norm1", [128, 8], F32, kind="ExternalInput").ap()
    n2_d = dt("norm2", [128, 8], F32, kind="ExternalInput").ap()
    nf_d = dt("norm_f", [1, D], F32, kind="ExternalInput").ap()
    win_d = dt("w_in", [D, 1280], F32, kind="ExternalInput").ap()
    sinks_d = dt("sinks", [1, 8], F32, kind="ExternalInput").ap()
    wpool_d = dt("w_pool", [4, 128, 128], F32, kind="ExternalInput").ap()
    psc_d = dt("pool_scale", [128, 4], F32, kind="ExternalInput").ap()
    wout_d = dt("w_out", [D, D], F32, kind="ExternalInput").ap()
    wg_d = dt("w_gate", [D, FF], F32, kind="ExternalInput").ap()
    wu_d = dt("w_up", [D, FF], F32, kind="ExternalInput").ap()
    wd_d = dt("w_down", [FF, D], F32, kind="ExternalInput").ap()
    y_d = dt("y", [TOK, D], F32, kind="ExternalOutput").ap()
    dbg_d = {}
    for name, shape in dbg:
        dbg_d[name] = dt("dbg_" + name, list(shape), F32, kind="ExternalOutput").ap()

    es = ExitStack()
    with es:
        S = Sched(nc, es)
        sb = lambda name, shape, dtype: es.enter_context(nc.sbuf_tensor("s_" + name, shape, dtype))
        resid = sb("resid", [128, 16, D], F32)
        h2T = sb("h2T", [128, 8, TOK], BF16)
        ident = sb("ident", [128, 128], BF16)
        bands = sb("bands", [128, 16, 128], BF16)
        cst = sb("cst", [128, 3, NT, 32], F32)
        modT = sb("modT", [128, 48], F32)
        g1T = sb("g1T", [128, 8], F32)
        g2T = sb("g2T", [128, 8], F32)
        n1T = sb("n1T", [128, 8], F32)
        n2T = sb("n2T", [128, 8], F32)
        bfm = sb("bfm", [128, 48], F32)
        gate = sb("gate", [128, D], F32)
        esink = sb("esink", [128, 8], F32)
        flag = sb("flag", [128, 1], F32)
        psc = sb("psc", [128, 4], F32)
        ss = sb("ss", [128, 64], F32)
        ms = sb("ms", [128, 64], F32)
        rstd = sb("rstd", [128, 64], F32)
        neghalf = sb("neghalf", [128, 1], F32)
        sc2 = sb("sc2", [128, 8, 2], BF16)
        wst = sb("wst", [128, 2, D], F32)
        ring = sb("ring", [128, 3, 8, 256], BF16)
        ps = [es.enter_context(nc.psum_tensor("ps%d" % i, [128, 512], F32)) for i in range(8)]

        wg_v = wg_d.rearrange("(k p) n -> p k n", p=128)
        wu_v = wu_d.rearrange("(k p) n -> p k n", p=128)
        G0 = GROUPS[0] * 128

        def bfv(bank):
            return ps[bank][:, :].bitcast(BF16).rearrange("p (k t) -> p k t", t=128)

        scr_i = [0]

        def scratch():
            scr_i[0] ^= 1
            return 6 + scr_i[0]

        def dump(name, src_ap, key, rows=None):
            if name in dbg_d:
                S.dma("sp", lambda: nc.sync.dma_start(out=dbg_d[name], in_=src_ap), reads=[key])

        es1 = ExitStack()
        with es1:
            sb1 = lambda name, shape, dtype: es1.enter_context(nc.sbuf_tensor("s_" + name, shape, dtype))
            cT = sb1("cT", [128, 8], F32)
            scb = sb1("scb", [128, 8, 128], BF16)
            posi = sb1("posi", [128, NT], I32)
            S.dma("sp", lambda: nc.sync.dma_start(out=cT[:], in_=c_d[:, :]), writes=["cT"])
            S.dma("sp", lambda: nc.sync.dma_start(out=bfm[:], in_=bfm_d[:, :]), writes=["bfm"])
            S.dma("sp", lambda: nc.sync.dma_start(out=n1T[:], in_=n1_d[:, :]), writes=["n1T"])
            S.dma("sp", lambda: nc.sync.dma_start(out=n2T[:], in_=n2_d[:, :]), writes=["n2T"])
            S.dma("sp", lambda: nc.sync.dma_start(out=flag[:], in_=flag_d[:, :]), writes=["flag"])
            S.dma("sp", lambda: nc.sync.dma_start(out=psc[:], in_=psc_d[:, :]), writes=["psc"])
            S.dma("sp", lambda: nc.sync.dma_start(out=posi[:], in_=pos_d[:, :]), writes=["posi"])
            S.dma("sp", lambda: nc.sync.dma_start(out=esink[:], in_=sinks_d[0:1, :].to_broadcast([128, 8])),
                  writes=["esink"])

            S.op("act", lambda: nc.scalar.activation(out=cT[:], in_=cT[:], func=AF.Silu), reads=["cT"], writes=["cT"])
            S.op("dve", lambda: nc.vector.tensor_copy(out=sc2[:], in_=cT[:].unsqueeze(2).to_broadcast([128, 8, 2])),
                 reads=["cT"], writes=["sc2"])
            S.op("dve", lambda: nc.vector.tensor_copy(out=scb[:], in_=cT[:].unsqueeze(2).to_broadcast([128, 8, 128])),
                 reads=["cT"], writes=["scb"])
            S.op("act", lambda: nc.scalar.activation(out=esink[:], in_=esink[:], func=AF.Exp),
                 reads=["esink"], writes=["esink"])
            S.op("pool", lambda: nc.gpsimd.memset(neghalf[:], -0.5), writes=["neghalf"])

            wada_v = wada_d.rearrange("(k p) n -> p k n", p=128)
            mod_state = {"next_dma": 0, "next_pe": 0}
            order = list(range(24))

            def mod_dma():
                b = mod_state["next_dma"]
                if b >= 24:
                    return
                mod_state["next_dma"] += 1
                slot = b % 3
                S.dma("pool", lambda: nc.gpsimd.dma_start(out=ring[:, slot, :, :],
                                                          in_=wada_v[:, :, b * 256:(b + 1) * 256]),
                      writes=[("ring", slot)])

            def mod_pe():
                b = mod_state["next_pe"]
                if b >= 24:
                    return
                mod_state["next_pe"] += 1
                slot = b % 3
                v = b // 4
                off = (b % 4) * 256
                bank = b if b < 8 else scratch()
                if v in (2, 5):
                    gt = gate
                    gk = "gate"

                    def f():
                        for k in range(8):
                            i_ = nc.tensor.matmul(ps[bank][:, 0:256], lhsT=scb[:, k, :],
                                                  rhs=ring[:, slot, k, :],
                                                  start=(k == 0), stop=(k == 7))
                        return i_
                    S.op("pe", f, reads=["scb", ("ring", slot)], writes=[("ps", bank)])
                    S.op("dve", lambda: nc.vector.tensor_tensor(out=gt[:, off:off + 256], in0=ps[bank][:, 0:256],
                                                                in1=gt[:, off:off + 256], op=ALU.add),
                         reads=[("ps", bank)], writes=[gk])
                else:
                    def f():
                        for cc in range(2):
                            for k in range(8):
                                i_ = nc.tensor.matmul(ps[bank][:, 2 * cc:2 * cc + 2],
                                                      lhsT=ring[:, slot, k, cc * 128:(cc + 1) * 128],
                                                      rhs=sc2[:, k, 0:2],
                                                      start=(k == 0), stop=(k == 7))
                        return i_
                    S.op("pe", f, reads=["sc2", ("ring", slot)], writes=[("ps", bank)])
                    col = v * 8 + (b % 4) * 2
                    S.op("dve", lambda: nc.vector.tensor_tensor(
                        out=modT[:, col:col + 2], in0=ps[bank][:, 0:4].rearrange("p (c t) -> p c t", t=2)[:, :, 0],
                        in1=bfm[:, col:col + 2], op=ALU.add),
                        reads=[("ps", bank), "bfm"], writes=["modT"])
                if mod_state["next_dma"] < 8 or mod_state["next_pe"] > 8:
                    mod_dma()

            S.dma("sp", lambda: nc.sync.dma_start(out=gate[:], in_=bada_d[0:1, 2048:3072].to_broadcast([128, D])),
                  writes=["gate"])
            mod_dma()
            mod_dma()
            mod_dma()

            es2 = ExitStack()
            with es2:
                sb2 = lambda name, shape, dtype: es2.enter_context(nc.sbuf_tensor("s_" + name, shape, dtype))
                win = sb2("win", [128, 8, 1280], BF16)
                wout = sb2("wout", [128, 8, D], BF16)
                wpool = sb2("wpool", [128, 4, 128], BF16)
                xhat = sb2("xhat", [128, 1, D], BF16)
                xhat2 = sb2("xhat2", [128, 1, D], BF16)
                h1T = sb2("h1T", [128, 2, 8, 128], BF16)
                tmpA = sb2("tmpA", [128, 640], F32)
                tmpB = sb2("tmpB", [128, 640], F32)
                qkr = sb2("qkr", [128, 2, 640], BF16)
                qT = sb2("qT", [128, 2, 4, 128], BF16)
                kT = sb2("kT", [128, 3, 128], BF16)
                NV = 6
                vaug = sb2("vaug", [128, NV, 2, 65], BF16)
                NU = 4
                upl = sb2("upl", [128, NU, 512], BF16)
                PT = sb2("PT", [128, 1, 2, 2, 512], BF16)
                attn = sb2("attn", [128, 2, 512], BF16)
                dn = sb2("dn", [128, 2, 4], F32)
                rc = sb2("rc", [128, 2, 4], F32)
                pooledT = sb2("pooledT", [128, 2, 4, 128], BF16)
                AT = sb2("AT", [128, 2, 8, 128], BF16)

                win_v = win_d.rearrange("(k p) n -> p k n", p=128)
                def load_win():
                    S.dma("pool", lambda: nc.gpsimd.dma_start(out=win[:, :, 512:1280], in_=win_v[:, :, 512:1280]),
                          writes=["win2"])
                    S.dma("pool", lambda: nc.gpsimd.dma_start(out=win[:, :, 0:512], in_=win_v[:, :, 0:512]),
                          writes=["win"])
                    S.dma("pool", lambda: nc.gpsimd.dma_start(out=wpool[:], in_=wpool_d.rearrange("g c d -> c g d")),
                          writes=["wpool"])

                scr = h2T[:, :, :].rearrange("p k t -> p (k t)").bitcast(F32)
                scri = h2T[:, :, :].rearrange("p k t -> p (k t)").bitcast(I32)
                idf = scr[:, 0:128]
                bm = scr[:, 128:256]
                bt = scr[:, 256:384]
                bt2 = scr[:, 384:512]
                colsc = scr[:, 512:640]
                posf = scr[:, 640:640 + NT]
                invf = scr[:, 672:704]
                thb = scr[:, 704:736]
                A4 = 2 * NT * 32
                v4 = lambda a: a.rearrange("p (s t d) -> p s t d", s=2, d=32)
                ang = v4(scr[:, 1024:1024 + A4])
                kf = v4(scr[:, 2112:2112 + A4])
                fx = v4(scr[:, 3200:3200 + A4])
                ki = v4(scri[:, 4288:4288 + A4])

                def pl(fn, r=(), w=()):
                    S.op("pool", fn, reads=r, writes=w)

                def dv(fn, r=(), w=()):
                    S.op("dve", fn, reads=r, writes=w)

                dmat = scr[:, 5504:5632]
                tpl = scr[:, 5632:5760]
                pl(lambda: nc.gpsimd.iota(dmat, pattern=[[1, 128]], base=0, channel_multiplier=-1,
                                          allow_small_or_imprecise_dtypes=True), w=["dmat"])
                pl(lambda: nc.gpsimd.iota(tpl, pattern=[[1, 128]], base=1, channel_multiplier=0,
                                          allow_small_or_imprecise_dtypes=True), w=["tpl"])
                dv(lambda: nc.vector.tensor_scalar(out=idf, in0=dmat, scalar1=0.0, scalar2=None, op0=ALU.is_equal),
                   r=["dmat"], w=["idf"])
                dv(lambda: nc.vector.tensor_copy(out=ident[:], in_=idf), r=["idf"], w=["ident"])
                def consts_bands(gsel):
                    for g, w in [(g_, (2, 4, 8, 16)[g_]) for g_ in gsel]:
                        hw = (w - 1) / 2.0
                        dv(lambda: nc.vector.tensor_scalar(out=bt, in0=dmat, scalar1=0.0, scalar2=None,
                                                           op0=ALU.is_ge), r=["dmat"], w=["bt"])
                        dv(lambda: nc.vector.scalar_tensor_tensor(out=bm, in0=dmat, scalar=float(w - 1) + 0.25, in1=bt,
                                                                  op0=ALU.is_le, op1=ALU.mult),
                           r=["dmat", "bt"], w=["bm"])
                        dv(lambda: nc.vector.scalar_tensor_tensor(out=bt, in0=bm, scalar=1.0 / w, in1=idf,
                                                                  op0=ALU.mult, op1=ALU.subtract),
                           r=["bm", "idf"], w=["bt"])
                        dv(lambda: nc.vector.tensor_copy(out=bands[:, 0 * 4 + g, :], in_=bt), r=["bt"], w=["bands"])
                        dv(lambda: nc.vector.tensor_scalar(out=colsc, in0=tpl, scalar1=float(w), scalar2=None,
                                                           op0=ALU.min), r=["tpl"], w=["colsc"])
                        dv(lambda: nc.vector.reciprocal(out=colsc, in_=colsc), r=["colsc"], w=["colsc"])
                        dv(lambda: nc.vector.tensor_tensor(out=bt2, in0=bm, in1=colsc, op=ALU.mult),
                           r=["bm", "colsc"], w=["bt2"])
                        dv(lambda: nc.vector.tensor_tensor(out=bt2, in0=bt2, in1=idf, op=ALU.subtract),
                           r=["bt2", "idf"], w=["bt2"])
                        dv(lambda: nc.vector.tensor_tensor(out=bt, in0=bt, in1=bt2, op=ALU.subtract),
                           r=["bt", "bt2"], w=["bt"])
                        dv(lambda: nc.vector.scalar_tensor_tensor(out=bt, in0=bt, scalar=flag[:, 0:1], in1=bt2,
                                                                  op0=ALU.mult, op1=ALU.add),
                           r=["bt", "bt2", "flag"], w=["bt"])
                        dv(lambda: nc.vector.tensor_copy(out=bands[:, 2 * 4 + g, :], in_=bt), r=["bt"], w=["bands"])
                        dv(lambda: nc.vector.tensor_scalar(out=bm, in0=dmat, scalar1=float(w - 129) + 0.25,
                                                           scalar2=1.0 / w, op0=ALU.is_le, op1=ALU.mult),
                           r=["dmat"], w=["bm"])
                        dv(lambda: nc.vector.tensor_copy(out=bands[:, 1 * 4 + g, :], in_=bm), r=["bm"], w=["bands"])
                        dv(lambda: nc.vector.tensor_scalar(out=bands[:, 3 * 4 + g, :], in0=bm, scalar1=flag[:, 0:1],
                                                           scalar2=None, op0=ALU.mult), r=["bm", "flag"], w=["bands"])


                def consts_rope():
                    pl(lambda: nc.gpsimd.iota(invf[:], pattern=[[1, 32]], base=0, channel_multiplier=0,
                                              allow_small_or_imprecise_dtypes=True), w=["invf"])
                    pl(lambda: nc.gpsimd.tensor_scalar(out=invf[:], in0=invf[:], scalar1=-1.0 / 32.0, scalar2=0.0,
                                                       op0=ALU.mult, op1=ALU.add), r=["invf"], w=["invf"])
                    pl(lambda: nc.gpsimd.memset(thb[:], 10000.0), w=["thb"])
                    pl(lambda: nc.gpsimd.tensor_tensor(out=invf[:], in0=thb[:], in1=invf[:], op=ALU.pow),
                       r=["invf", "thb"], w=["invf"])

                def consts_rope_dve():
                    S.op("dve", lambda: nc.vector.tensor_copy(out=posf[:], in_=posi[:]), reads=["posi"], writes=["posf"])
                    S.op("dve", lambda: nc.vector.tensor_tensor(
                        out=ang[:, 0, :, :], in0=posf[:].unsqueeze(2).to_broadcast([128, NT, 32]),
                        in1=invf[:].unsqueeze(1).to_broadcast([128, NT, 32]), op=ALU.mult),
                        reads=["posf", "invf"], writes=["ang"])
                    S.op("dve", lambda: nc.vector.tensor_scalar(out=ang[:, 1, :, :], in0=ang[:, 0, :, :],
                                                                scalar1=math.pi / 2, scalar2=None, op0=ALU.add),
                         reads=["ang"], writes=["ang"])
                    TWO_PI_HI = 6.28125
                    TWO_PI_LO = 2.0 * math.pi - 6.28125
                    S.op("dve", lambda: nc.vector.tensor_scalar(out=kf[:], in0=ang[:], scalar1=1.0 / (2 * math.pi),
                                                                scalar2=None, op0=ALU.mult), reads=["ang"], writes=["kf"])
                    S.op("dve", lambda: nc.vector.tensor_copy(out=ki[:], in_=kf[:]), reads=["kf"], writes=["ki"])
                    S.op("dve", lambda: nc.vector.tensor_copy(out=kf[:], in_=ki[:]), reads=["ki"], writes=["kf"])
                    S.op("dve", lambda: nc.vector.scalar_tensor_tensor(out=ang[:], in0=kf[:], scalar=-TWO_PI_HI,
                                                                       in1=ang[:], op0=ALU.mult, op1=ALU.add),
                         reads=["kf", "ang"], writes=["ang"])
                    S.op("dve", lambda: nc.vector.scalar_tensor_tensor(out=ang[:], in0=kf[:], scalar=-TWO_PI_LO,
                                                                       in1=ang[:], op0=ALU.mult, op1=ALU.add),
                         reads=["kf", "ang"], writes=["ang"])
                    S.op("dve", lambda: nc.vector.tensor_scalar(out=fx[:], in0=ang[:], scalar1=math.pi,
                                                                scalar2=-2 * math.pi, op0=ALU.is_gt, op1=ALU.mult),
                         reads=["ang"], writes=["fx"])
                    S.op("dve", lambda: nc.vector.tensor_tensor(out=ang[:], in0=ang[:], in1=fx[:], op=ALU.add),
                         reads=["ang", "fx"], writes=["ang"])
                    S.op("dve", lambda: nc.vector.tensor_scalar(out=fx[:], in0=ang[:], scalar1=-math.pi,
                                                                scalar2=2 * math.pi, op0=ALU.is_lt, op1=ALU.mult),
                         reads=["ang"], writes=["fx"])
                    S.op("dve", lambda: nc.vector.tensor_tensor(out=ang[:], in0=ang[:], in1=fx[:], op=ALU.add),
                         reads=["ang", "fx"], writes=["ang"])
                    S.op("dve", lambda: nc.vector.tensor_scalar(out=ang[:], in0=ang[:], scalar1=math.pi,
                                                                scalar2=-math.pi, op0=ALU.min, op1=ALU.max),
                         reads=["ang"], writes=["ang"])
                    S.op("act", lambda: nc.scalar.activation(out=cst[:, 1, :, :], in_=ang[:, 0, :, :], func=(AF.Copy if "nosin" in DBG_SKIP else AF.Sin)),
                         reads=["ang"], writes=["cst"])
                    S.op("act", lambda: nc.scalar.activation(out=cst[:, 0, :, :], in_=ang[:, 1, :, :], func=(AF.Copy if "nosin" in DBG_SKIP else AF.Sin)),
                         reads=["ang"], writes=["cst"])
                    S.op("dve", lambda: nc.vector.tensor_scalar(out=cst[:, 2, :, :], in0=cst[:, 1, :, :], scalar1=-1.0,
                                                                scalar2=None, op0=ALU.mult),
                         reads=["cst"], writes=["cst"])

                def consts_done():
                    S.op("dve", lambda: nc.vector.memset(ss[:, 63:64], 0.0),
                         writes=["idf", "bm", "bt", "bt2", "colsc", "posf", "invf", "thb", "ang", "kf", "ki", "fx",
                                 "dmat", "tpl", "h2Tscr_done"])

                S.op("pool", lambda: nc.gpsimd.memset(vaug[:], 1.0), writes=[("vaug", s) for s in range(NV)])

                consts_bands((0,))
                for b_ in range(8):
                    mod_pe()
                    if b_ == 1:
                        consts_rope()
                        consts_bands((1,))
                        consts_rope_dve()
                    if b_ == 4:
                        load_win()
                consts_bands((2,))
                S.op("dve", lambda: nc.vector.scalar_tensor_tensor(out=g1T[:], in0=modT[:, 8:16], scalar=1.0,
                                                                   in1=n1T[:], op0=ALU.add, op1=ALU.mult),
                     reads=["modT", "n1T"], writes=["g1T"])
                mod_ready = {"g2": False}

                def xt_ap(i):
                    return wst[:, 1, :] if i == 0 else resid[:, i - 1, :]

                def xkey(i):
                    return ("wst", 1) if i == 0 else ("resid", i - 1)

                def st_load(i):
                    if i == 0:
                        S.dma("sp", lambda: nc.sync.dma_start(out=xt_ap(0), in_=x_d[0:128, :]), writes=[xkey(0)])
                    elif i % 2 == 1:
                        thr = [("h1T", (i - 3) % 2, 7)] if i >= 3 else []
                        S.dma("sp", lambda: nc.sync.dma_start(
                            out=resid[:, i - 1:i + 1, :],
                            in_=x_d[i * 128:(i + 2) * 128, :].rearrange("(c p) d -> p c d", p=128)),
                            reads=thr, writes=[("resid", i - 1), ("resid", i)])

                def rstd_ops(col, src_ap, src_key, junk_ap, junk_key):
                    S.op("act", lambda: nc.scalar.activation(out=junk_ap, in_=src_ap, func=AF.Square,
                                                             **({} if "noacc" in DBG_SKIP else {"accum_out": ss[:, col:col + 1]})),
                         reads=[src_key], writes=[junk_key, ("ss", col)], sync_waw=[junk_key])
                    S.op("pool", lambda: nc.gpsimd.tensor_scalar(out=ms[:, col:col + 1], in0=ss[:, col:col + 1],
                                                                 scalar1=1.0 / D, scalar2=EPS, op0=ALU.mult, op1=ALU.add),
                         reads=[("ss", col)], writes=[("ms", col)])
                    S.op("pool", lambda: nc.gpsimd.tensor_tensor(out=rstd[:, col:col + 1], in0=ms[:, col:col + 1],
                                                                 in1=neghalf[:, 0:1], op=ALU.pow),
                         reads=[("ms", col), "neghalf"], writes=[("rstd", col)])

                def st_norm(i):
                    s = i % 2
                    if "norm" in DBG_SKIP and i >= 1:
                        return
                    rstd_ops(i, xt_ap(i), xkey(i), xhat[:, 0, :], ("xhat", 0))
                    S.op("pool", lambda: nc.gpsimd.tensor_scalar(out=xhat[:, 0, :], in0=xt_ap(i), scalar1=rstd[:, i:i + 1],
                                                                 scalar2=0.0, op0=ALU.mult, op1=ALU.add),
                         reads=[xkey(i), ("rstd", i)], writes=[("xhat", 0)])

                def transp_mod(src_ap_fn, src_key, gT, shT_ap, gkeys, dst_fn, dst_key, ev_eng="dve"):
                    def f():
                        for k in range(8):
                            i_ = nc.tensor.transpose(out=bfv(0)[:, k, :], in_=src_ap_fn(k), identity=ident[:])
                        return i_
                    if "trxpe" not in DBG_SKIP:
                        S.op("pe", f, reads=[src_key, "ident"], writes=[("ps", 0)])
                    for k in range(8):
                        if "trxev" in DBG_SKIP:
                            break
                        if ev_eng == "act":
                            S.op("act", lambda: nc.scalar.activation(out=dst_fn(k), in_=bfv(0)[:, k, :], func=AF.Identity,
                                                                     scale=gT[:, k:k + 1], bias=shT_ap[:, k:k + 1]),
                                 reads=[("ps", 0)] + gkeys, writes=[dst_key + (k,)])
                        else:
                            S.op("dve", lambda: nc.vector.tensor_scalar(out=dst_fn(k), in0=bfv(0)[:, k, :],
                                                                        scalar1=gT[:, k:k + 1], scalar2=shT_ap[:, k:k + 1],
                                                                        op0=ALU.mult, op1=ALU.add),
                                 reads=[("ps", 0)] + gkeys, writes=[dst_key + (k,)])

                def st_trx(i):
                    s = i % 2
                    if "trx" in DBG_SKIP:
                        return
                    transp_mod(lambda k: xhat[:, 0, k * 128:(k + 1) * 128], ("xhat", 0), g1T, modT[:, 0:8],
                               ["g1T", "modT"], lambda k: h1T[:, s, k, :], ("h1T", s),
                               ev_eng=("act" if i <= 3 else "dve"))

                inproj_bank = [1]

                def st_inproj(i, bsel=(0, 1, 2)):
                    s = i % 2
                    if i == 0:
                        bsel = tuple(b_ for b_ in bsel if b_ != 0)
                        if not bsel:
                            return
                    if "noinproj" in DBG_SKIP:
                        return
                    su = i % NU
                    sv = i % NV
                    hkeys = [("h1T", s, k) for k in range(8)]
                    banks = []
                    for (c0, w, wk) in [((0, 512, "win"), (512, 512, "win2"), (1024, 256, "win2"))[b_] for b_ in bsel]:
                        bank = inproj_bank[0]
                        inproj_bank[0] = 3 - bank
                        banks.append(bank)

                        def f():
                            for k in range(8):
                                i_ = nc.tensor.matmul(ps[bank][:, 0:w], lhsT=h1T[:, s, k, :], rhs=win[:, k, c0:c0 + w],
                                                      start=(k == 0), stop=(k == 7))
                            return i_
                        S.op("pe", f, reads=hkeys + (["win"] if c0 == 0 else ["win2"]),
                             writes=[("ps", bank)])
                        if c0 == 0:
                            rope(i, bank, 0, 8)
                        elif c0 == 512:
                            rope(i, bank, 512, 2)
                            if "novu" in DBG_SKIP:
                                continue
                            S.op("act", lambda: nc.scalar.activation(
                                out=vaug[:, sv, :, 0:64], in_=ps[bank][:, 128:256].rearrange("p (g d) -> p g d", d=64),
                                func=AF.Copy), reads=[("ps", bank)], writes=[("vaug", sv)])
                            S.op("act", lambda: nc.scalar.activation(out=upl[:, su, 0:256], in_=ps[bank][:, 256:512],
                                                                     func=AF.Copy),
                                 reads=[("ps", bank)], writes=[("upl", su, 0)])
                        elif "novu" not in DBG_SKIP:
                            S.op("act", lambda: nc.scalar.activation(out=upl[:, su, 256:512], in_=ps[bank][:, 0:256],
                                                                     func=AF.Copy),
                                 reads=[("ps", bank)], writes=[("upl", su, 1)])
                    s2 = i % 2
                    for g in (range(2) if 0 in bsel else ()):
                        S.op("pool", lambda: nc.gpsimd.tensor_tensor(
                            out=qkr[:, s2, 0:512].rearrange("p (c g d) -> p g c d", g=2, d=64)[:, g, :, :],
                            in0=tmpA[:, g * 256:(g + 1) * 256].rearrange("p (c d) -> p c d", d=64),
                            in1=tmpB[:, g * 256:(g + 1) * 256].rearrange("p (c d) -> p c d", d=64), op=ALU.add),
                            reads=["tmpA0", "tmpB0", "tmpB0a"], writes=[("qkr", s2, g)])
                    if 1 in bsel:
                        S.op("pool", lambda: nc.gpsimd.tensor_tensor(out=qkr[:, s2, 512:640], in0=tmpA[:, 512:640],
                                                                     in1=tmpB[:, 512:640], op=ALU.add),
                             reads=["tmpA512", "tmpB512", "tmpB512a"], writes=[("qkr", s2, 2)])

                def rope(i, bank, off, nh):
                    if "norope" in DBG_SKIP:
                        return
                    wdt = nh * 64
                    src = ps[bank][:, 0:wdt].rearrange("p (h two d) -> p h two d", two=2, d=32)
                    dA = tmpA[:, off:off + wdt].rearrange("p (h two d) -> p h two d", two=2, d=32)
                    dB = tmpB[:, off:off + wdt].rearrange("p (h two d) -> p h two d", two=2, d=32)
                    cos_b = cst[:, 0, i, :].unsqueeze(1).unsqueeze(1).to_broadcast([128, nh, 2, 32])
                    sin_b = cst[:, 1, i, :].unsqueeze(1).to_broadcast([128, nh, 32])
                    nsin_b = cst[:, 2, i, :].unsqueeze(1).to_broadcast([128, nh, 32])
                    ka, kb = "tmpA%d" % off, "tmpB%d" % off
                    S.op("dve", lambda: nc.vector.tensor_tensor(out=dA, in0=src, in1=cos_b, op=ALU.mult),
                         reads=[("ps", bank), "cst"], writes=[ka])
                    S.op("dve", lambda: nc.vector.tensor_tensor(out=dB[:, :, 0, :], in0=src[:, :, 1, :], in1=nsin_b,
                                                                op=ALU.mult),
                         reads=[("ps", bank), "cst"], writes=[kb + "a"])
                    S.op("dve", lambda: nc.vector.tensor_tensor(out=dB[:, :, 1, :], in0=src[:, :, 0, :], in1=sin_b,
                                                                op=ALU.mult),
                         reads=[("ps", bank), "cst"], writes=[kb])

                def st_qkT(i):
                    s = i % 2
                    sk = i % 3
                    bank = scratch()
                    qv = qkr[:, s, :].rearrange("p (h d) -> p h d", d=64)

                    def f():
                        for c in (range(4) if i > 0 else ()):
                            nc.tensor.transpose(out=bfv(bank)[:, c, :], in_=qkr[:, s, c * 128:(c + 1) * 128],
                                                identity=ident[:])
                        return nc.tensor.transpose(out=bfv(bank)[:, 4, :], in_=qkr[:, s, 512:640], identity=ident[:])
                    S.op("pe", f, reads=([("qkr", s, 0), ("qkr", s, 1)] if i > 0 else []) + [("qkr", s, 2), "ident"],
                         writes=[("ps", bank)])
                    if i > 0:
                        S.op("dve", lambda: nc.vector.tensor_copy(out=qT[:, s, :, :], in_=bfv(bank)[:, 0:4, :]),
                             reads=[("ps", bank)], writes=[("qT", s)])
                    S.op("act", lambda: nc.scalar.activation(out=kT[:, sk, :], in_=bfv(bank)[:, 4, :], func=AF.Copy),
                         reads=[("ps", bank)], writes=[("kT", sk)])

                def st_S(i, gsel=(0, 1)):
                    s = i % 2
                    sp0 = 0
                    skc, skp = i % 3, (i - 1) % 3
                    for g in gsel:
                        pr = slice(64 * g, 64 * g + 64)
                        for jj, sk in ((0, skp), (1, skc)):
                            bank = 3 + jj
                            S.op("pe", lambda: nc.tensor.matmul(ps[bank][:, :], lhsT=kT[pr, sk, :],
                                                                rhs=qT[pr, s, :, :], start=True, stop=True),
                                 reads=[("kT", sk), ("qT", s)], writes=[("ps", bank)])
                            S.op("act", lambda: nc.scalar.activation(out=PT[:, 0, g, jj, :], in_=ps[bank][:, :],
                                                                     func=AF.Exp, scale=0.125),
                                 reads=[("ps", bank)], writes=[("PT", 0, g, jj)])
                    if 1 not in gsel:
                        return
                    S.op("pool", lambda: nc.gpsimd.affine_select(
                        out=PT[:, 0, :, 0, :], in_=PT[:, 0, :, 0, :], compare_op=ALU.is_gt, fill=0.0, base=0,
                        pattern=[[0, 2], [0, 4], [-1, 128]], channel_multiplier=1),
                        reads=[("PT", 0, 0, 0), ("PT", 0, 1, 0)], writes=[("PT", 0, 0, 0), ("PT", 0, 1, 0)])
                    S.op("pool", lambda: nc.gpsimd.affine_select(
                        out=PT[:, 0, :, 1, :], in_=PT[:, 0, :, 1, :], compare_op=ALU.is_ge, fill=0.0, base=0,
                        pattern=[[0, 2], [0, 4], [1, 128]], channel_multiplier=-1),
                        reads=[("PT", 0, 0, 1), ("PT", 0, 1, 1)], writes=[("PT", 0, 0, 1), ("PT", 0, 1, 1)])
                    if i == 1:
                        S.op("pool", lambda: nc.gpsimd.tensor_scalar(
                            out=PT[:, 0, :, 0, :], in0=PT[:, 0, :, 0, :], scalar1=flag[:, 0:1], scalar2=0.0,
                            op0=ALU.mult, op1=ALU.add),
                            reads=[("PT", 0, 0, 0), ("PT", 0, 1, 0), "flag"], writes=[("PT", 0, 0, 0), ("PT", 0, 1, 0)])

                def st_PV(i, gsel=(0, 1)):
                    s = i % 2
                    svc, svp = i % NV, (i - 1) % NV
                    for g in gsel:
                        O = ps[5][:, 0:260].rearrange("p (c e) -> p c e", e=65)

                        def f():
                            for c in range(4):
                                nc.tensor.matmul(O[:, c, :], lhsT=PT[:, 0, g, 0, c * 128:(c + 1) * 128],
                                                 rhs=vaug[:, svp, g, :], start=True, stop=False)
                                i_ = nc.tensor.matmul(O[:, c, :], lhsT=PT[:, 0, g, 1, c * 128:(c + 1) * 128],
                                                      rhs=vaug[:, svc, g, :], start=False, stop=True)
                            return i_
                        S.op("pe", f, reads=[("PT", 0, g, 0), ("PT", 0, g, 1), ("vaug", svp), ("vaug", svc)],
                             writes=[("ps", 5)])
                        S.op("dve", lambda: nc.vector.tensor_tensor(out=dn[:, g, :], in0=O[:, :, 64],
                                                                    in1=esink[:, 4 * g:4 * g + 4], op=ALU.add),
                             reads=[("ps", 5), "esink"], writes=[("dn", g)])
                        S.op("dve", lambda: nc.vector.reciprocal(out=rc[:, g, :], in_=dn[:, g, :]),
                             reads=[("dn", g)], writes=[("rc", g)])
                        av = attn[:, s, :].rearrange("p (h d) -> p h d", d=64)
                        S.op("dve", lambda: nc.vector.tensor_tensor(
                            out=av[:, 4 * g:4 * g + 4, :], in0=O[:, :, 0:64],
                            in1=rc[:, g, :].unsqueeze(2).to_broadcast([128, 4, 64]), op=ALU.mult),
                            reads=[("ps", 5), ("rc", g)], writes=[("attn", s, g)])

                def st_attnT(i):
                    s = i % 2
                    sa = i % 2
                    bank = scratch()

                    def f():
                        for cc in range(4):
                            i_ = nc.tensor.transpose(out=bfv(bank)[:, cc, :], in_=attn[:, s, cc * 128:(cc + 1) * 128],
                                                     identity=ident[:])
                        return i_
                    S.op("pe", f, reads=[("attn", s, 0), ("attn", s, 1), "ident"], writes=[("ps", bank)])
                    S.op("act", lambda: nc.scalar.activation(out=AT[:, sa, 0:4, :], in_=bfv(bank)[:, 0:4, :], func=AF.Copy),
                         reads=[("ps", bank)], writes=[("AT", sa, 0)])

                def st_pool1(i):
                    suc, sup = i % NU, (i - 1) % NU
                    sp_ = i % 2
                    bank = scratch()
                    kp, kc = (3, 2) if i == 1 else (1, 0)
                    Y = ps[bank][:, :].rearrange("p (g t) -> p g t", t=128)

                    def f():
                        for g in range(4):
                            nc.tensor.matmul(Y[:, g, :], lhsT=upl[:, sup, g * 128:(g + 1) * 128],
                                             rhs=bands[:, kp * 4 + g, :], start=True, stop=False)
                            i_ = nc.tensor.matmul(Y[:, g, :], lhsT=upl[:, suc, g * 128:(g + 1) * 128],
                                                  rhs=bands[:, kc * 4 + g, :], start=False, stop=True)
                        return i_
                    S.op("pe", f, reads=[("upl", sup, 0), ("upl", sup, 1), ("upl", suc, 0), ("upl", suc, 1), "bands"],
                         writes=[("ps", bank)])
                    S.op("act", lambda: nc.scalar.activation(out=pooledT[:, sp_, :, :], in_=Y, func=AF.Copy),
                         reads=[("ps", bank)], writes=[("pooledT", sp_)])

                def st_pool2(i):
                    sp_ = i % 2
                    sa = i % 2
                    bank = scratch()
                    Z = ps[bank][:, :].rearrange("p (g t) -> p g t", t=128)

                    def f():
                        for g in range(4):
                            i_ = nc.tensor.matmul(Z[:, g, :], lhsT=wpool[:, g, :], rhs=pooledT[:, sp_, g, :],
                                                  start=True, stop=True)
                        return i_
                    S.op("pe", f, reads=[("pooledT", sp_), "wpool"], writes=[("ps", bank)])
                    S.op("dve", lambda: nc.vector.tensor_tensor(out=AT[:, sa, 4:8, :], in0=Z,
                                                                in1=psc[:].unsqueeze(2).to_broadcast([128, 4, 128]),
                                                                op=ALU.mult),
                         reads=[("ps", bank), "psc"], writes=[("AT", sa, 1)])

                def st_outproj(i, hsel=(0, 1)):
                    sa = i % 2
                    for hc in hsel:
                        bank = scratch()

                        def f():
                            for k in range(8):
                                i_ = nc.tensor.matmul(ps[bank][:, :], lhsT=AT[:, sa, k, :],
                                                      rhs=wout[:, k, hc * 512:(hc + 1) * 512],
                                                      start=(k == 0), stop=(k == 7))
                            return i_
                        S.op("pe", f, reads=[("AT", sa, 0), ("AT", sa, 1)] + [("wout", k) for k in range(8)],
                             writes=[("ps", bank)])
                        S.op("dve", lambda: nc.vector.tensor_tensor(
                            out=resid[:, i - 1, hc * 512:(hc + 1) * 512], in0=ps[bank][:, :],
                            in1=resid[:, i - 1, hc * 512:(hc + 1) * 512], op=ALU.add),
                            reads=[("ps", bank), ("resid", i - 1)], writes=[("resid", i - 1)])

                def st_norm2(i):
                    col = 20 + i
                    s = i % 2
                    rstd_ops(col, resid[:, i - 1, :], ("resid", i - 1), xhat2[:, 0, :], ("xhat2", 0))
                    if i >= 12:
                        S.op("dve", lambda: nc.vector.tensor_scalar(out=xhat2[:, 0, :], in0=resid[:, i - 1, :],
                                                                    scalar1=rstd[:, col:col + 1], scalar2=None,
                                                                    op0=ALU.mult),
                             reads=[("resid", i - 1), ("rstd", col)], writes=[("xhat2", 0)])
                    else:
                        S.op("pool", lambda: nc.gpsimd.tensor_scalar(out=xhat2[:, 0, :], in0=resid[:, i - 1, :],
                                                                     scalar1=rstd[:, col:col + 1], scalar2=0.0,
                                                                     op0=ALU.mult, op1=ALU.add),
                             reads=[("resid", i - 1), ("rstd", col)], writes=[("xhat2", 0)])

                def st_trx2(i):
                    s = i % 2
                    if not mod_ready["g2"]:
                        S.op("dve", lambda: nc.vector.scalar_tensor_tensor(out=g2T[:], in0=modT[:, 32:40], scalar=1.0,
                                                                           in1=n2T[:], op0=ALU.add, op1=ALU.mult),
                             reads=["modT", "n2T"], writes=["g2T"])
                        mod_ready["g2"] = True
                    transp_mod(lambda k: xhat2[:, 0, k * 128:(k + 1) * 128], ("xhat2", 0), g2T, modT[:, 24:32],
                               ["g2T", "modT"] + (["h2Tscr_done"] if i == 1 else []),
                               lambda k: h2T[:, k, (i - 1) * 128:i * 128], ("h2T", i - 1), ev_eng="act")

                wout_state = {"k": 0}

                def wout_fold():
                    k = wout_state["k"]
                    if k >= 8 or "wout" in DBG_SKIP:
                        return
                    wout_state["k"] += 2
                    S.dma("sp", lambda: nc.sync.dma_start(
                        out=wst[:, 0:2, :], in_=wout_d[k * 128:(k + 2) * 128, :].rearrange("(c p) d -> p c d", p=128)),
                        writes=[("wst", 0), ("wst", 1)])
                    for slot in range(2):
                        S.op("pool", lambda: nc.gpsimd.tensor_tensor(out=wout[:, k + slot, :], in0=wst[:, slot, :],
                                                                     in1=gate[:, :], op=ALU.mult),
                             reads=[("wst", slot), "gate"], writes=[("wout", k + slot)])
                    k += 1
                    if k == 7:
                        S.dma("sp", lambda: nc.sync.dma_start(out=gate[:],
                                                              in_=bada_d[0:1, 5120:6144].to_broadcast([128, D])),
                              reads=[], writes=["gate"])

                stages = [
                    (6, 1, st_PV, ((0,),)),
                    (5, 1, st_S, ((0,),)),
                    (8, 1, st_outproj, ((0,),)),
                    (10, 1, st_trx2, ()),
                    (9, 1, st_norm2, ()),
                    (6, 1, st_PV, ((1,),)),
                    (5, 1, st_S, ((1,),)),
                    (8, 1, st_outproj, ((1,),)),
                    (4, 0, st_qkT, ()),
                    (3, 0, st_inproj, ((0,),)),
                    (7, 1, st_attnT, ()),
                    (3, 0, st_inproj, ((1,),)),
                    (6, 1, st_pool2, ()),
                    (3, 0, st_inproj, ((2,),)),
                    (5, 1, st_pool1, ()),
                    (2, 0, st_trx, ()),
                    (1, 0, st_norm, ()),
                    (0, 0, st_load, ()),
                ]
                depth = max(o for o, _, _, _ in stages)
                for step in range((NT + depth) if nsteps is None else nsteps):
                    if stop == "setup":
                        break
                    order = stages
                    if step >= NT:
                        order = [st_ for st_ in stages if st_[2] is not st_norm2]
                        k_ = [j for j, st_ in enumerate(order) if st_[2] is st_trx][0]
                        order.insert(k_, (9, 1, st_norm2, ()))
                    for off, first, fn, extra in order:
                        i = step - off
                        if first <= i < NT:
                            fn(i, *extra)
                    if step == 1:
                        for _ in range(3):
                            mod_dma()
                    if step >= 3 and not ("modp1" in DBG_SKIP and step >= 2):
                        for _ in range(2):
                            if mod_state["next_pe"] < 20 or wout_state["k"] == 8:
                                mod_pe()
                    if mod_state["next_pe"] >= 12:
                        wout_fold()
                    if step == 0:
                        consts_bands((3,))
                        consts_done()
                    if mod_state["next_pe"] == 24 and not mod_state.get("g0"):
                        mod_state["g0"] = True
                        S.dma("pool", lambda: nc.gpsimd.dma_start(out=ring[:, 0, :, :], in_=wg_v[:, :, 0:G0]),
                              writes=[("ring", 0)])
                        S.dma("pool", lambda: nc.gpsimd.dma_start(out=ring[:, 1, :, :], in_=wu_v[:, :, 0:G0]),
                              writes=[("ring", 1)])
                while mod_state["next_pe"] < 24 and "modp1" not in DBG_SKIP:
                    mod_pe()
                if "x1" in dbg_d:
                    S.dma("sp", lambda: nc.sync.dma_start(out=dbg_d["x1"].rearrange("(t p) d -> p t d", p=128),
                                                          in_=resid[:, :, :]), reads=[("resid", t) for t in range(16)])
                S.barrier()
        es4 = ExitStack()
        with es4:
            sb4 = lambda name, shape, dtype: es4.enter_context(nc.sbuf_tensor("s_" + name, shape, dtype))
            wg = sb4("wg", [128, 2, 8, 512], BF16)
            wu = sb4("wu", [128, 2, 8, 512], BF16)
            wd = sb4("wd", [128, 2, 4, D], BF16)
            sg = sb4("sg", [128, 2, 512], F32)
            aT = sb4("aT", [128, 2, 4, 512], BF16)
            ybuf = sb4("ybuf", [128, 2, D], F32)
            yjunk = sb4("yjunk", [128, D], BF16)
            normf = sb4("normf", [128, D], F32)
            S.dma("sp", lambda: nc.sync.dma_start(out=normf[:], in_=nf_d[0:1, :].to_broadcast([128, D])),
                  writes=["normf"])
            gstart = [sum(GROUPS[:i]) for i in range(len(GROUPS))]
            wst_i = [0]

            def load_group(gi):
                b = gi % 2
                ncg = GROUPS[gi]
                f0 = gstart[gi]
                if gi > 0 or not mod_state.get("g0"):
                    S.dma("pool", lambda: nc.gpsimd.dma_start(out=wg[:, b, :, 0:ncg * 128],
                                                              in_=wg_v[:, :, f0 * 128:(f0 + ncg) * 128]),
                          writes=[("wg", b)])
                    S.dma("pool", lambda: nc.gpsimd.dma_start(out=wu[:, b, :, 0:ncg * 128],
                                                              in_=wu_v[:, :, f0 * 128:(f0 + ncg) * 128]),
                          writes=[("wu", b)])
                for j in range(0, ncg, 2):
                    f = f0 + j
                    S.dma("sp", lambda: nc.sync.dma_start(
                        out=wst[:, 0:2, :], in_=wd_d[f * 128:(f + 2) * 128, :].rearrange("(c p) d -> p c d", p=128)),
                        writes=[("wst", 0), ("wst", 1)])
                    for slot in range(2):
                        S.op("pool", lambda: nc.gpsimd.tensor_tensor(out=wd[:, b, j + slot, :], in0=wst[:, slot, :],
                                                                     in1=gate[:, :], op=ALU.mult),
                             reads=[("wst", slot), "gate"], writes=[("wd", b, j + slot)])

            par = {"gu": 0, "dn": 0, "a": 0, "y": 0}

            def gu_unit(gi, tb):
                b = gi % 2
                ncg = GROUPS[gi]
                if gi == 0 and mod_state.get("g0"):
                    wgs, wus, wgk, wuk = ring[:, 0, :, :], ring[:, 1, :, :], ("ring", 0), ("ring", 1)
                else:
                    wgs, wus, wgk, wuk = wg[:, b, :, :], wu[:, b, :, :], ("wg", b), ("wu", b)
                sa = par["a"]
                par["a"] ^= 1
                for j in range(ncg):
                    p = par["gu"]
                    par["gu"] ^= 1
                    gb, ub = p, 2 + p

                    def fg():
                        for k in range(8):
                            i_ = nc.tensor.matmul(ps[gb][:, :], lhsT=wgs[:, k, j * 128:(j + 1) * 128],
                                                  rhs=h2T[:, k, tb * 512:(tb + 1) * 512], start=(k == 0), stop=(k == 7))
                        return i_

                    def fu():
                        for k in range(8):
                            i_ = nc.tensor.matmul(ps[ub][:, :], lhsT=wus[:, k, j * 128:(j + 1) * 128],
                                                  rhs=h2T[:, k, tb * 512:(tb + 1) * 512], start=(k == 0), stop=(k == 7))
                        return i_
                    hk = [("h2T", t, k) for t in range(tb * 4, tb * 4 + 4) for k in range(8)]
                    S.op("pe", fg, reads=[wgk] + hk, writes=[("ps", gb)])
                    S.op("pe", fu, reads=[wuk] + hk, writes=[("ps", ub)])
                    S.op("act", lambda: nc.scalar.activation(out=sg[:, p, :], in_=ps[gb][:, :], func=AF.Silu),
                         reads=[("ps", gb)], writes=[("sg", p)])
                    S.op("dve", lambda: nc.vector.tensor_tensor(out=aT[:, sa, j, :], in0=ps[ub][:, :], in1=sg[:, p, :],
                                                                op=ALU.mult),
                         reads=[("ps", ub), ("sg", p)], writes=[("aT", sa, j)])
                return sa

            def dn_unit(gi, tb, sa):
                b = gi % 2
                ncg = GROUPS[gi]
                last = gi == len(GROUPS) - 1
                for tt in range(4):
                    tile = tb * 4 + tt
                    for hc in range(2):
                        p = par["dn"]
                        par["dn"] ^= 1
                        bank = 4 + p

                        def f():
                            for j in range(ncg):
                                i_ = nc.tensor.matmul(ps[bank][:, :], lhsT=aT[:, sa, j, tt * 128:(tt + 1) * 128],
                                                      rhs=wd[:, b, j, hc * 512:(hc + 1) * 512],
                                                      start=(j == 0), stop=(j == ncg - 1))
                            return i_
                        S.op("pe", f, reads=[("aT", sa, j) for j in range(ncg)] + [("wd", b, j) for j in range(ncg)],
                             writes=[("ps", bank)])
                        S.op("dve", lambda: nc.vector.tensor_tensor(
                            out=resid[:, tile, hc * 512:(hc + 1) * 512], in0=ps[bank][:, :],
                            in1=resid[:, tile, hc * 512:(hc + 1) * 512], op=ALU.add),
                            reads=[("ps", bank), ("resid", tile)], writes=[("resid", tile)])
                    if last:
                        flush_y()
                        col = 40 + tile
                        sy = par["y"]
                        par["y"] = (sy + 1) % 2
                        rstd_ops(col, resid[:, tile, :], ("resid", tile), yjunk[:, :], "yjunk")
                        deferred_y.append((tile, col, sy))

            deferred_y = []

            def flush_y():
                while deferred_y:
                    tile, col, sy = deferred_y.pop(0)
                    S.op("dve", lambda: nc.vector.scalar_tensor_tensor(
                        out=ybuf[:, sy, :], in0=resid[:, tile, :], scalar=rstd[:, col:col + 1], in1=normf[:, :],
                        op0=ALU.mult, op1=ALU.mult),
                        reads=[("resid", tile), ("rstd", col), "normf"], writes=[("ybuf", sy)])
                    if sy % 2 == 1:
                        S.dma("sp", lambda: nc.sync.dma_start(
                            out=y_d[(tile - 1) * 128:(tile + 1) * 128, :].rearrange("(c p) d -> p c d", p=128),
                            in_=ybuf[:, sy - 1:sy + 1, :]), reads=[("ybuf", sy - 1), ("ybuf", sy)])

            units = [(gi, tb) for gi in range(len(GROUPS)) for tb in range(4)]
            if stop is not None:
                units = []
            else:
                load_group(0)
                load_group(1)
            pending = None
            for n, (gi, tb) in enumerate(units):
                if tb == 3 and gi + 2 < len(GROUPS):
                    pass
                sa = gu_unit(gi, tb)
                if pending is not None:
                    dn_unit(*pending)
                    pg = pending[0]
                    if pending[1] == 3 and pg + 2 < len(GROUPS):
                        load_group(pg + 2)
                pending = (gi, tb, sa)
            if pending is not None:
                dn_unit(*pending)
            flush_y()
            S.barrier()
    return nc


_CACHE = {}


def _prep_inputs(x, c, positions, w_ada, b_ada, norm1, w_in, sinks, w_pool, pool_scale, w_out, norm2,
                 w_gate, w_up, w_down, norm_f):
    f = lambda a: np.ascontiguousarray(np.asarray(a), dtype=np.float32)
    x = f(x)
    positions = np.asarray(positions).astype(np.int32)
    fm = lambda v: np.ascontiguousarray(f(v).reshape(-1, 128).T)
    shared = {
        "w_ada": f(w_ada), "b_fm": fm(b_ada), "b_ada": f(b_ada).reshape(1, -1),
        "norm1": fm(norm1), "norm2": fm(norm2), "norm_f": f(norm_f).reshape(1, -1),
        "w_in": f(w_in), "sinks": f(sinks).reshape(1, 8), "w_pool": f(w_pool),
        "pool_scale": fm(pool_scale), "w_out": f(w_out), "w_gate": f(w_gate), "w_up": f(w_up), "w_down": f(w_down),
    }
    in_maps = []
    per_b = SEQ // TOK
    for core in range(NCORES):
        b, q = divmod(core, per_b)
        s0 = q * TOK
        xs = np.zeros((NT * 128, D), np.float32)
        ps_ = np.zeros((NT * 128,), np.int32)
        if q == 0:
            xs[128:] = x[b, 0:TOK]
            ps_[128:] = positions[b, 0:TOK]
        else:
            xs[:] = x[b, s0 - 128:s0 + TOK]
            ps_[:] = positions[b, s0 - 128:s0 + TOK]
        m = dict(shared)
        m["x"] = xs
        m["pos"] = np.ascontiguousarray(ps_.reshape(NT, 128).T)
        m["c"] = fm(f(c)[b])
        m["flag"] = np.full((128, 1), 0.0 if q == 0 else 1.0, np.float32)
        in_maps.append(m)
    return in_maps


def kernel(**inputs):
    in_maps = _prep_inputs(**inputs)
    if "nc" not in _CACHE:
        _CACHE["nc"] = build()
    res = run_bass_kernel_spmd(_CACHE["nc"], in_maps, core_ids=list(range(NCORES)))
    out = np.empty((2, SEQ, D), np.float32)
    per_b = SEQ // TOK
    for core in range(NCORES):
        b, q = divmod(core, per_b)
        out[b, q * TOK:(q + 1) * TOK] = res.results[core]["y"]
    return out
```

```python
import math
from contextlib import ExitStack

import numpy as np
import concourse.bass as bass
import concourse.mybir as mybir
from concourse.bass_utils import run_bass_kernel_spmd

F32 = mybir.dt.float32
F32R = mybir.dt.float32r
BF16 = mybir.dt.bfloat16
I32 = mybir.dt.int32
ALU = mybir.AluOpType
AF = mybir.ActivationFunctionType

D = 1024
SEQ = 8192
NCORES = 8
TOK = 2048
NT = 17
FF = 2816
NFC = FF // 128
GROUPS = [2, 4, 4, 4, 4, 4]
EPS = 1e-6
NSEM_DMA = {"sp": 46, "pool": 50}
DBG_SKIP = set()


class Ev:
    __slots__ = ("sem", "val", "eng", "dma")

    def __init__(self, sem, val, eng, dma):
        self.sem, self.val, self.eng, self.dma = sem, val, eng, dma


class Sched:
    def __init__(self, nc, es):
        self.nc = nc
        self.engs = {"pe": nc.tensor, "act": nc.scalar, "dve": nc.vector, "pool": nc.gpsimd, "sp": nc.sync}
        self.sem = {e: es.enter_context(nc.semaphore("c_" + e)) for e in ("pe", "act", "dve", "pool")}
        self.cnt = {e: 0 for e in self.sem}
        self.dsem = {q: [es.enter_context(nc.semaphore("d_%s%d" % (q, i))) for i in range(NSEM_DMA[q])]
                     for q in ("sp", "pool")}
        self.dcnt = {q: 0 for q in self.dsem}
        self.dval = {}
        self.waited = {}
        self.lastw = {}
        self.readers = {}

    def _need(self, e, sem, val):
        k = (e, id(sem))
        if self.waited.get(k, 0) < val:
            self.engs[e].wait_ge(sem, val)
            self.waited[k] = val

    def _deps(self, e, reads, writes, true_psw=()):
        need = {}

        def add(ev, raw, psum=False):
            if ev is None:
                return
            if (not ev.dma) and ev.eng == e and not raw and not (psum and e != "pe"):
                return
            k = id(ev.sem)
            if k not in need or need[k][1] < ev.val:
                need[k] = (ev.sem, ev.val)

        for k in reads:
            add(self.lastw.get(k), True)
        for k in writes:
            add(self.lastw.get(k), False, k in true_psw)
            for ev in self.readers.get(k, ()):
                add(ev, False)
        for sem, val in need.values():
            self._need(e, sem, val)

    def _record(self, ev, reads, writes):
        for k in writes:
            self.lastw[k] = ev
            self.readers[k] = []
        for k in reads:
            if k not in writes:
                self.readers.setdefault(k, []).append(ev)

    def op(self, e, fn, reads=(), writes=(), sync_waw=()):
        psr = [k for k in reads if isinstance(k, tuple) and k[0] == "ps"]
        true_psw = [k for k in writes if isinstance(k, tuple) and k[0] == "ps"] + list(sync_waw)
        if psr:
            reads = [k for k in reads if k not in psr]
            writes = list(writes) + [k for k in psr if k not in writes]
        self._deps(e, reads, writes, true_psw)
        inst = fn()
        self.cnt[e] += 1
        inst.then_inc(self.sem[e], 1)
        self._record(Ev(self.sem[e], self.cnt[e], e, False), reads, writes)

    def dma(self, q, fn, reads=(), writes=()):
        self._deps(q, reads, writes)
        j = self.dcnt[q]
        self.dcnt[q] += 1
        slot = j % NSEM_DMA[q]
        sem = self.dsem[q][slot]
        prev = self.dval.get((q, slot), 0)
        if prev:
            self._need(q, sem, prev)
        inst = fn()
        inst.then_inc(sem, 16)
        self.dval[(q, slot)] = prev + 16
        self._record(Ev(sem, prev + 16, q, True), reads, writes)

    def barrier(self, engines=("pe", "act", "dve", "pool", "sp")):
        for e in engines:
            for e2 in self.sem:
                if e2 != e and self.cnt[e2]:
                    self._need(e, self.sem[e2], self.cnt[e2])
            for (q, slot), v in self.dval.items():
                self._need(e, self.dsem[q][slot], v)
        self.lastw.clear()
        self.readers.clear()


def build(dbg=(), stop=None, nsteps=None):
    nc = bass.Bass("TRN2", target_bir_lowering=False)
    dt = nc.dram_tensor
    x_d = dt("x", [NT * 128, D], F32, kind="ExternalInput").ap()
    pos_d = dt("pos", [128, NT], I32, kind="ExternalInput").ap()
    c_d = dt("c", [128, 8], F32, kind="ExternalInput").ap()
    flag_d = dt("flag", [128, 1], F32, kind="ExternalInput").ap()
    wada_d = dt("w_ada", [D, 6 * D], F32, kind="ExternalInput").ap()
    bfm_d = dt("b_fm", [128, 48], F32, kind="ExternalInput").ap()
    bada_d = dt("b_ada", [1, 6 * D], F32, kind="ExternalInput").ap()
    n1_d = dt("norm1", [128, 8], F32, kind="ExternalInput").ap()
    n2_d = dt("norm2", [128, 8], F32, kind="ExternalInput").ap()
    nf_d = dt("norm_f", [1, D], F32, kind="ExternalInput").ap()
    win_d = dt("w_in", [D, 1280], F32, kind="ExternalInput").ap()
    sinks_d = dt("sinks", [1, 8], F32, kind="ExternalInput").ap()
    wpool_d = dt("w_pool", [4, 128, 128], F32, kind="ExternalInput").ap()
    psc_d = dt("pool_scale", [128, 4], F32, kind="ExternalInput").ap()
    wout_d = dt("w_out", [D, D], F32, kind="ExternalInput").ap()
    wg_d = dt("w_gate", [D, FF], F32, kind="ExternalInput").ap()
    wu_d = dt("w_up", [D, FF], F32, kind="ExternalInput").ap()
    wd_d = dt("w_down", [FF, D], F32, kind="ExternalInput").ap()
    y_d = dt("y", [TOK, D], F32, kind="ExternalOutput").ap()
    dbg_d = {}
    for name, shape in dbg:
        dbg_d[name] = dt("dbg_" + name, list(shape), F32, kind="ExternalOutput").ap()

    es = ExitStack()
    with es:
        S = Sched(nc, es)
        sb = lambda name, shape, dtype: es.enter_context(nc.sbuf_tensor("s_" + name, shape, dtype))
        resid = sb("resid", [128, 16, D], F32)
        h2T = sb("h2T", [128, 8, TOK], BF16)
        ident = sb("ident", [128, 128], BF16)
        bands = sb("bands", [128, 16, 128], BF16)
        cst = sb("cst", [128, 3, NT, 32], F32)
        modT = sb("modT", [128, 48], F32)
        g1T = sb("g1T", [128, 8], F32)
        g2T = sb("g2T", [128, 8], F32)
        n1T = sb("n1T", [128, 8], F32)
        n2T = sb("n2T", [128, 8], F32)
        bfm = sb("bfm", [128, 48], F32)
        gate = sb("gate", [128, D], F32)
        esink = sb("esink", [128, 8], F32)
        flag = sb("flag", [128, 1], F32)
        psc = sb("psc", [128, 4], F32)
        ss = sb("ss", [128, 64], F32)
        ms = sb("ms", [128, 64], F32)
        rstd = sb("rstd", [128, 64], F32)
        neghalf = sb("neghalf", [128, 1], F32)
        sc2 = sb("sc2", [128, 8, 2], BF16)
        wst = sb("wst", [128, 2, D], F32)
        ring = sb("ring", [128, 3, 8, 256], BF16)
        ps = [es.enter_context(nc.psum_tensor("ps%d" % i, [128, 512], F32)) for i in range(8)]

        wg_v = wg_d.rearrange("(k p) n -> p k n", p=128)
        wu_v = wu_d.rearrange("(k p) n -> p k n", p=128)
        G0 = GROUPS[0] * 128

        def bfv(bank):
            return ps[bank][:, :].bitcast(BF16).rearrange("p (k t) -> p k t", t=128)

        scr_i = [0]

        def scratch():
            scr_i[0] ^= 1
            return 6 + scr_i[0]

        def dump(name, src_ap, key, rows=None):
            if name in dbg_d:
                S.dma("sp", lambda: nc.sync.dma_start(out=dbg_d[name], in_=src_ap), reads=[key])

        es1 = ExitStack()
        with es1:
            sb1 = lambda name, shape, dtype: es1.enter_context(nc.sbuf_tensor("s_" + name, shape, dtype))
            cT = sb1("cT", [128, 8], F32)
            scb = sb1("scb", [128, 8, 128], BF16)
            posi = sb1("posi", [128, NT], I32)
            S.dma("sp", lambda: nc.sync.dma_start(out=cT[:], in_=c_d[:, :]), writes=["cT"])
            S.dma("sp", lambda: nc.sync.dma_start(out=bfm[:], in_=bfm_d[:, :]), writes=["bfm"])
            S.dma("sp", lambda: nc.sync.dma_start(out=n1T[:], in_=n1_d[:, :]), writes=["n1T"])
            S.dma("sp", lambda: nc.sync.dma_start(out=n2T[:], in_=n2_d[:, :]), writes=["n2T"])
            S.dma("sp", lambda: nc.sync.dma_start(out=flag[:], in_=flag_d[:, :]), writes=["flag"])
            S.dma("sp", lambda: nc.sync.dma_start(out=psc[:], in_=psc_d[:, :]), writes=["psc"])
            S.dma("sp", lambda: nc.sync.dma_start(out=posi[:], in_=pos_d[:, :]), writes=["posi"])
            S.dma("sp", lambda: nc.sync.dma_start(out=esink[:], in_=sinks_d[0:1, :].to_broadcast([128, 8])),
                  writes=["esink"])

            S.op("act", lambda: nc.scalar.activation(out=cT[:], in_=cT[:], func=AF.Silu), reads=["cT"], writes=["cT"])
            S.op("dve", lambda: nc.vector.tensor_copy(out=sc2[:], in_=cT[:].unsqueeze(2).to_broadcast([128, 8, 2])),
                 reads=["cT"], writes=["sc2"])
            S.op("dve", lambda: nc.vector.tensor_copy(out=scb[:], in_=cT[:].unsqueeze(2).to_broadcast([128, 8, 128])),
                 reads=["cT"], writes=["scb"])
            S.op("act", lambda: nc.scalar.activation(out=esink[:], in_=esink[:], func=AF.Exp),
                 reads=["esink"], writes=["esink"])
            S.op("pool", lambda: nc.gpsimd.memset(neghalf[:], -0.5), writes=["neghalf"])

            wada_v = wada_d.rearrange("(k p) n -> p k n", p=128)
            mod_state = {"next_dma": 0, "next_pe": 0}
            order = list(range(24))

            def mod_dma():
                b = mod_state["next_dma"]
                if b >= 24:
                    return
                mod_state["next_dma"] += 1
                slot = b % 3
                S.dma("pool", lambda: nc.gpsimd.dma_start(out=ring[:, slot, :, :],
                                                          in_=wada_v[:, :, b * 256:(b + 1) * 256]),
                      writes=[("ring", slot)])

            def mod_pe():
                b = mod_state["next_pe"]
                if b >= 24:
                    return
                mod_state["next_pe"] += 1
                slot = b % 3
                v = b // 4
                off = (b % 4) * 256
                bank = b if b < 8 else scratch()
                if v in (2, 5):
                    gt = gate
                    gk = "gate"

                    def f():
                        for k in range(8):
                            i_ = nc.tensor.matmul(ps[bank][:, 0:256], lhsT=scb[:, k, :],
                                                  rhs=ring[:, slot, k, :],
                                                  start=(k == 0), stop=(k == 7))
                        return i_
                    S.op("pe", f, reads=["scb", ("ring", slot)], writes=[("ps", bank)])
                    S.op("dve", lambda: nc.vector.tensor_tensor(out=gt[:, off:off + 256], in0=ps[bank][:, 0:256],
                                                                in1=gt[:, off:off + 256], op=ALU.add),
                         reads=[("ps", bank)], writes=[gk])
                else:
                    def f():
                        for cc in range(2):
                            for k in range(8):
                                i_ = nc.tensor.matmul(ps[bank][:, 2 * cc:2 * cc + 2],
                                                      lhsT=ring[:, slot, k, cc * 128:(cc + 1) * 128],
                                                      rhs=sc2[:, k, 0:2],
                                                      start=(k == 0), stop=(k == 7))
                        return i_
                    S.op("pe", f, reads=["sc2", ("ring", slot)], writes=[("ps", bank)])
                    col = v * 8 + (b % 4) * 2
                    S.op("dve", lambda: nc.vector.tensor_tensor(
                        out=modT[:, col:col + 2], in0=ps[bank][:, 0:4].rearrange("p (c t) -> p c t", t=2)[:, :, 0],
                        in1=bfm[:, col:col + 2], op=ALU.add),
                        reads=[("ps", bank), "bfm"], writes=["modT"])
                if mod_state["next_dma"] < 8 or mod_state["next_pe"] > 8:
                    mod_dma()

            S.dma("sp", lambda: nc.sync.dma_start(out=gate[:], in_=bada_d[0:1, 2048:3072].to_broadcast([128, D])),
                  writes=["gate"])
            mod_dma()
            mod_dma()
            mod_dma()

            es2 = ExitStack()
            with es2:
                sb2 = lambda name, shape, dtype: es2.enter_context(nc.sbuf_tensor("s_" + name, shape, dtype))
                win = sb2("win", [128, 8, 1280], BF16)
                wout = sb2("wout", [128, 8, D], BF16)
                wpool = sb2("wpool", [128, 4, 128], BF16)
                xhat = sb2("xhat", [128, 1, D], BF16)
                xhat2 = sb2("xhat2", [128, 1, D], BF16)
                h1T = sb2("h1T", [128, 2, 8, 128], BF16)
                tmpA = sb2("tmpA", [128, 640], F32)
                tmpB = sb2("tmpB", [128, 640], F32)
                qkr = sb2("qkr", [128, 2, 640], BF16)
                qT = sb2("qT", [128, 2, 4, 128], BF16)
                kT = sb2("kT", [128, 3, 128], BF16)
                NV = 6
                vaug = sb2("vaug", [128, NV, 2, 65], BF16)
                NU = 4
                upl = sb2("upl", [128, NU, 512], BF16)
                PT = sb2("PT", [128, 1, 2, 2, 512], BF16)
                attn = sb2("attn", [128, 2, 512], BF16)
                dn = sb2("dn", [128, 2, 4], F32)
                rc = sb2("rc", [128, 2, 4], F32)
                pooledT = sb2("pooledT", [128, 2, 4, 128], BF16)
                AT = sb2("AT", [128, 2, 8, 128], BF16)

                win_v = win_d.rearrange("(k p) n -> p k n", p=128)
                def load_win():
                    S.dma("pool", lambda: nc.gpsimd.dma_start(out=win[:, :, 512:1280], in_=win_v[:, :, 512:1280]),
                          writes=["win2"])
                    S.dma("pool", lambda: nc.gpsimd.dma_start(out=win[:, :, 0:512], in_=win_v[:, :, 0:512]),
                          writes=["win"])
                    S.dma("pool", lambda: nc.gpsimd.dma_start(out=wpool[:], in_=wpool_d.rearrange("g c d -> c g d")),
                          writes=["wpool"])

                scr = h2T[:, :, :].rearrange("p k t -> p (k t)").bitcast(F32)
                scri = h2T[:, :, :].rearrange("p k t -> p (k t)").bitcast(I32)
                idf = scr[:, 0:128]
                bm = scr[:, 128:256]
                bt = scr[:, 256:384]
                bt2 = scr[:, 384:512]
                colsc = scr[:, 512:640]
                posf = scr[:, 640:640 + NT]
                invf = scr[:, 672:704]
                thb = scr[:, 704:736]
                A4 = 2 * NT * 32
                v4 = lambda a: a.rearrange("p (s t d) -> p s t d", s=2, d=32)
                ang = v4(scr[:, 1024:1024 + A4])
                kf = v4(scr[:, 2112:2112 + A4])
                fx = v4(scr[:, 3200:3200 + A4])
                ki = v4(scri[:, 4288:4288 + A4])

                def pl(fn, r=(), w=()):
                    S.op("pool", fn, reads=r, writes=w)

                def dv(fn, r=(), w=()):
                    S.op("dve", fn, reads=r, writes=w)

                dmat = scr[:, 5504:5632]
                tpl = scr[:, 5632:5760]
                pl(lambda: nc.gpsimd.iota(dmat, pattern=[[1, 128]], base=0, channel_multiplier=-1,
                                          allow_small_or_imprecise_dtypes=True), w=["dmat"])
                pl(lambda: nc.gpsimd.iota(tpl, pattern=[[1, 128]], base=1, channel_multiplier=0,
                                          allow_small_or_imprecise_dtypes=True), w=["tpl"])
                dv(lambda: nc.vector.tensor_scalar(out=idf, in0=dmat, scalar1=0.0, scalar2=None, op0=ALU.is_equal),
                   r=["dmat"], w=["idf"])
                dv(lambda: nc.vector.tensor_copy(out=ident[:], in_=idf), r=["idf"], w=["ident"])
                def consts_bands(gsel):
                    for g, w in [(g_, (2, 4, 8, 16)[g_]) for g_ in gsel]:
                        hw = (w - 1) / 2.0
                        dv(lambda: nc.vector.tensor_scalar(out=bt, in0=dmat, scalar1=0.0, scalar2=None,
                                                           op0=ALU.is_ge), r=["dmat"], w=["bt"])
                        dv(lambda: nc.vector.scalar_tensor_tensor(out=bm, in0=dmat, scalar=float(w - 1) + 0.25, in1=bt,
                                                                  op0=ALU.is_le, op1=ALU.mult),
                           r=["dmat", "bt"], w=["bm"])
                        dv(lambda: nc.vector.scalar_tensor_tensor(out=bt, in0=bm, scalar=1.0 / w, in1=idf,
                                                                  op0=ALU.mult, op1=ALU.subtract),
                           r=["bm", "idf"], w=["bt"])
                        dv(lambda: nc.vector.tensor_copy(out=bands[:, 0 * 4 + g, :], in_=bt), r=["bt"], w=["bands"])
                        dv(lambda: nc.vector.tensor_scalar(out=colsc, in0=tpl, scalar1=float(w), scalar2=None,
                                                           op0=ALU.min), r=["tpl"], w=["colsc"])
                        dv(lambda: nc.vector.reciprocal(out=colsc, in_=colsc), r=["colsc"], w=["colsc"])
                        dv(lambda: nc.vector.tensor_tensor(out=bt2, in0=bm, in1=colsc, op=ALU.mult),
                           r=["bm", "colsc"], w=["bt2"])
                        dv(lambda: nc.vector.tensor_tensor(out=bt2, in0=bt2, in1=idf, op=ALU.subtract),
                           r=["bt2", "idf"], w=["bt2"])
                        dv(lambda: nc.vector.tensor_tensor(out=bt, in0=bt, in1=bt2, op=ALU.subtract),
                           r=["bt", "bt2"], w=["bt"])
                        dv(lambda: nc.vector.scalar_tensor_tensor(out=bt, in0=bt, scalar=flag[:, 0:1], in1=bt2,
                                                                  op0=ALU.mult, op1=ALU.add),
                           r=["bt", "bt2", "flag"], w=["bt"])
                        dv(lambda: nc.vector.tensor_copy(out=bands[:, 2 * 4 + g, :], in_=bt), r=["bt"], w=["bands"])
                        dv(lambda: nc.vector.tensor_scalar(out=bm, in0=dmat, scalar1=float(w - 129) + 0.25,
                                                           scalar2=1.0 / w, op0=ALU.is_le, op1=ALU.mult),
                           r=["dmat"], w=["bm"])
                        dv(lambda: nc.vector.tensor_copy(out=bands[:, 1 * 4 + g, :], in_=bm), r=["bm"], w=["bands"])
                        dv(lambda: nc.vector.tensor_scalar(out=bands[:, 3 * 4 + g, :], in0=bm, scalar1=flag[:, 0:1],
                                                           scalar2=None, op0=ALU.mult), r=["bm", "flag"], w=["bands"])


                def consts_rope():
                    pl(lambda: nc.gpsimd.iota(invf[:], pattern=[[1, 32]], base=0, channel_multiplier=0,
                                              allow_small_or_imprecise_dtypes=True), w=["invf"])
                    pl(lambda: nc.gpsimd.tensor_scalar(out=invf[:], in0=invf[:], scalar1=-1.0 / 32.0, scalar2=0.0,
                                                       op0=ALU.mult, op1=ALU.add), r=["invf"], w=["invf"])
                    pl(lambda: nc.gpsimd.memset(thb[:], 10000.0), w=["thb"])
                    pl(lambda: nc.gpsimd.tensor_tensor(out=invf[:], in0=thb[:], in1=invf[:], op=ALU.pow),
                       r=["invf", "thb"], w=["invf"])

                def consts_rope_dve():
                    S.op("dve", lambda: nc.vector.tensor_copy(out=posf[:], in_=posi[:]), reads=["posi"], writes=["posf"])
                    S.op("dve", lambda: nc.vector.tensor_tensor(
                        out=ang[:, 0, :, :], in0=posf[:].unsqueeze(2).to_broadcast([128, NT, 32]),
                        in1=invf[:].unsqueeze(1).to_broadcast([128, NT, 32]), op=ALU.mult),
                        reads=["posf", "invf"], writes=["ang"])
                    S.op("dve", lambda: nc.vector.tensor_scalar(out=ang[:, 1, :, :], in0=ang[:, 0, :, :],
                                                                scalar1=math.pi / 2, scalar2=None, op0=ALU.add),
                         reads=["ang"], writes=["ang"])
                    TWO_PI_HI = 6.28125
                    TWO_PI_LO = 2.0 * math.pi - 6.28125
                    S.op("dve", lambda: nc.vector.tensor_scalar(out=kf[:], in0=ang[:], scalar1=1.0 / (2 * math.pi),
                                                                scalar2=None, op0=ALU.mult), reads=["ang"], writes=["kf"])
                    S.op("dve", lambda: nc.vector.tensor_copy(out=ki[:], in_=kf[:]), reads=["kf"], writes=["ki"])
                    S.op("dve", lambda: nc.vector.tensor_copy(out=kf[:], in_=ki[:]), reads=["ki"], writes=["kf"])
                    S.op("dve", lambda: nc.vector.scalar_tensor_tensor(out=ang[:], in0=kf[:], scalar=-TWO_PI_HI,
                                                                       in1=ang[:], op0=ALU.mult, op1=ALU.add),
                         reads=["kf", "ang"], writes=["ang"])
                    S.op("dve", lambda: nc.vector.scalar_tensor_tensor(out=ang[:], in0=kf[:], scalar=-TWO_PI_LO,
                                                                       in1=ang[:], op0=ALU.mult, op1=ALU.add),
                         reads=["kf", "ang"], writes=["ang"])
                    S.op("dve", lambda: nc.vector.tensor_scalar(out=fx[:], in0=ang[:], scalar1=math.pi,
                                                                scalar2=-2 * math.pi, op0=ALU.is_gt, op1=ALU.mult),
                         reads=["ang"], writes=["fx"])
                    S.op("dve", lambda: nc.vector.tensor_tensor(out=ang[:], in0=ang[:], in1=fx[:], op=ALU.add),
                         reads=["ang", "fx"], writes=["ang"])
                    S.op("dve", lambda: nc.vector.tensor_scalar(out=fx[:], in0=ang[:], scalar1=-math.pi,
                                                                scalar2=2 * math.pi, op0=ALU.is_lt, op1=ALU.mult),
                         reads=["ang"], writes=["fx"])
                    S.op("dve", lambda: nc.vector.tensor_tensor(out=ang[:], in0=ang[:], in1=fx[:], op=ALU.add),
                         reads=["ang", "fx"], writes=["ang"])
                    S.op("dve", lambda: nc.vector.tensor_scalar(out=ang[:], in0=ang[:], scalar1=math.pi,
                                                                scalar2=-math.pi, op0=ALU.min, op1=ALU.max),
                         reads=["ang"], writes=["ang"])
                    S.op("act", lambda: nc.scalar.activation(out=cst[:, 1, :, :], in_=ang[:, 0, :, :], func=(AF.Copy if "nosin" in DBG_SKIP else AF.Sin)),
                         reads=["ang"], writes=["cst"])
                    S.op("act", lambda: nc.scalar.activation(out=cst[:, 0, :, :], in_=ang[:, 1, :, :], func=(AF.Copy if "nosin" in DBG_SKIP else AF.Sin)),
                         reads=["ang"], writes=["cst"])
                    S.op("dve", lambda: nc.vector.tensor_scalar(out=cst[:, 2, :, :], in0=cst[:, 1, :, :], scalar1=-1.0,
                                                                scalar2=None, op0=ALU.mult),
                         reads=["cst"], writes=["cst"])

                def consts_done():
                    S.op("dve", lambda: nc.vector.memset(ss[:, 63:64], 0.0),
                         writes=["idf", "bm", "bt", "bt2", "colsc", "posf", "invf", "thb", "ang", "kf", "ki", "fx",
                                 "dmat", "tpl", "h2Tscr_done"])

                S.op("pool", lambda: nc.gpsimd.memset(vaug[:], 1.0), writes=[("vaug", s) for s in range(NV)])

                consts_bands((0,))
                for b_ in range(8):
                    mod_pe()
                    if b_ == 1:
                        consts_rope()
                        consts_bands((1,))
                        consts_rope_dve()
                    if b_ == 4:
                        load_win()
                consts_bands((2,))
                S.op("dve", lambda: nc.vector.scalar_tensor_tensor(out=g1T[:], in0=modT[:, 8:16], scalar=1.0,
                                                                   in1=n1T[:], op0=ALU.add, op1=ALU.mult),
                     reads=["modT", "n1T"], writes=["g1T"])
                mod_ready = {"g2": False}

                def xt_ap(i):
                    return wst[:, 1, :] if i == 0 else resid[:, i - 1, :]

                def xkey(i):
                    return ("wst", 1) if i == 0 else ("resid", i - 1)

                def st_load(i):
                    if i == 0:
                        S.dma("sp", lambda: nc.sync.dma_start(out=xt_ap(0), in_=x_d[0:128, :]), writes=[xkey(0)])
                    elif i % 2 == 1:
                        thr = [("h1T", (i - 3) % 2, 7)] if i >= 3 else []
                        S.dma("sp", lambda: nc.sync.dma_start(
                            out=resid[:, i - 1:i + 1, :],
                            in_=x_d[i * 128:(i + 2) * 128, :].rearrange("(c p) d -> p c d", p=128)),
                            reads=thr, writes=[("resid", i - 1), ("resid", i)])

                def rstd_ops(col, src_ap, src_key, junk_ap, junk_key):
                    S.op("act", lambda: nc.scalar.activation(out=junk_ap, in_=src_ap, func=AF.Square,
                                                             **({} if "noacc" in DBG_SKIP else {"accum_out": ss[:, col:col + 1]})),
                         reads=[src_key], writes=[junk_key, ("ss", col)], sync_waw=[junk_key])
                    S.op("pool", lambda: nc.gpsimd.tensor_scalar(out=ms[:, col:col + 1], in0=ss[:, col:col + 1],
                                                                 scalar1=1.0 / D, scalar2=EPS, op0=ALU.mult, op1=ALU.add),
                         reads=[("ss", col)], writes=[("ms", col)])
                    S.op("pool", lambda: nc.gpsimd.tensor_tensor(out=rstd[:, col:col + 1], in0=ms[:, col:col + 1],
                                                                 in1=neghalf[:, 0:1], op=ALU.pow),
                         reads=[("ms", col), "neghalf"], writes=[("rstd", col)])

                def st_norm(i):
                    s = i % 2
                    if "norm" in DBG_SKIP and i >= 1:
                        return
                    rstd_ops(i, xt_ap(i), xkey(i), xhat[:, 0, :], ("xhat", 0))
                    S.op("pool", lambda: nc.gpsimd.tensor_scalar(out=xhat[:, 0, :], in0=xt_ap(i), scalar1=rstd[:, i:i + 1],
                                                                 scalar2=0.0, op0=ALU.mult, op1=ALU.add),
                         reads=[xkey(i), ("rstd", i)], writes=[("xhat", 0)])

                def transp_mod(src_ap_fn, src_key, gT, shT_ap, gkeys, dst_fn, dst_key, ev_eng="dve"):
                    def f():
                        for k in range(8):
                            i_ = nc.tensor.transpose(out=bfv(0)[:, k, :], in_=src_ap_fn(k), identity=ident[:])
                        return i_
                    if "trxpe" not in DBG_SKIP:
                        S.op("pe", f, reads=[src_key, "ident"], writes=[("ps", 0)])
                    for k in range(8):
                        if "trxev" in DBG_SKIP:
                            break
                        if ev_eng == "act":
                            S.op("act", lambda: nc.scalar.activation(out=dst_fn(k), in_=bfv(0)[:, k, :], func=AF.Identity,
                                                                     scale=gT[:, k:k + 1], bias=shT_ap[:, k:k + 1]),
                                 reads=[("ps", 0)] + gkeys, writes=[dst_key + (k,)])
                        else:
                            S.op("dve", lambda: nc.vector.tensor_scalar(out=dst_fn(k), in0=bfv(0)[:, k, :],
                                                                        scalar1=gT[:, k:k + 1], scalar2=shT_ap[:, k:k + 1],
                                                                        op0=ALU.mult, op1=ALU.add),
                                 reads=[("ps", 0)] + gkeys, writes=[dst_key + (k,)])

                def st_trx(i):
                    s = i % 2
                    if "trx" in DBG_SKIP:
                        return
                    transp_mod(lambda k: xhat[:, 0, k * 128:(k + 1) * 128], ("xhat", 0), g1T, modT[:, 0:8],
                               ["g1T", "modT"], lambda k: h1T[:, s, k, :], ("h1T", s),
                               ev_eng=("act" if i <= 3 else "dve"))

                inproj_bank = [1]

                def st_inproj(i, bsel=(0, 1, 2)):
                    s = i % 2
                    if i == 0:
                        bsel = tuple(b_ for b_ in bsel if b_ != 0)
                        if not bsel:
                            return
                    if "noinproj" in DBG_SKIP:
                        return
                    su = i % NU
                    sv = i % NV
                    hkeys = [("h1T", s, k) for k in range(8)]
                    banks = []
                    for (c0, w, wk) in [((0, 512, "win"), (512, 512, "win2"), (1024, 256, "win2"))[b_] for b_ in bsel]:
                        bank = inproj_bank[0]
                        inproj_bank[0] = 3 - bank
                        banks.append(bank)

                        def f():
                            for k in range(8):
                                i_ = nc.tensor.matmul(ps[bank][:, 0:w], lhsT=h1T[:, s, k, :], rhs=win[:, k, c0:c0 + w],
                                                      start=(k == 0), stop=(k == 7))
                            return i_
                        S.op("pe", f, reads=hkeys + (["win"] if c0 == 0 else ["win2"]),
                             writes=[("ps", bank)])
                        if c0 == 0:
                            rope(i, bank, 0, 8)
                        elif c0 == 512:
                            rope(i, bank, 512, 2)
                            if "novu" in DBG_SKIP:
                                continue
                            S.op("act", lambda: nc.scalar.activation(
                                out=vaug[:, sv, :, 0:64], in_=ps[bank][:, 128:256].rearrange("p (g d) -> p g d", d=64),
                                func=AF.Copy), reads=[("ps", bank)], writes=[("vaug", sv)])
                            S.op("act", lambda: nc.scalar.activation(out=upl[:, su, 0:256], in_=ps[bank][:, 256:512],
                                                                     func=AF.Copy),
                                 reads=[("ps", bank)], writes=[("upl", su, 0)])
                        elif "novu" not in DBG_SKIP:
                            S.op("act", lambda: nc.scalar.activation(out=upl[:, su, 256:512], in_=ps[bank][:, 0:256],
                                                                     func=AF.Copy),
                                 reads=[("ps", bank)], writes=[("upl", su, 1)])
                    s2 = i % 2
                    for g in (range(2) if 0 in bsel else ()):
                        S.op("pool", lambda: nc.gpsimd.tensor_tensor(
                            out=qkr[:, s2, 0:512].rearrange("p (c g d) -> p g c d", g=2, d=64)[:, g, :, :],
                            in0=tmpA[:, g * 256:(g + 1) * 256].rearrange("p (c d) -> p c d", d=64),
                            in1=tmpB[:, g * 256:(g + 1) * 256].rearrange("p (c d) -> p c d", d=64), op=ALU.add),
                            reads=["tmpA0", "tmpB0", "tmpB0a"], writes=[("qkr", s2, g)])
                    if 1 in bsel:
                        S.op("pool", lambda: nc.gpsimd.tensor_tensor(out=qkr[:, s2, 512:640], in0=tmpA[:, 512:640],
                                                                     in1=tmpB[:, 512:640], op=ALU.add),
                             reads=["tmpA512", "tmpB512", "tmpB512a"], writes=[("qkr", s2, 2)])

                def rope(i, bank, off, nh):
                    if "norope" in DBG_SKIP:
                        return
                    wdt = nh * 64
                    src = ps[bank][:, 0:wdt].rearrange("p (h two d) -> p h two d", two=2, d=32)
                    dA = tmpA[:, off:off + wdt].rearrange("p (h two d) -> p h two d", two=2, d=32)
                    dB = tmpB[:, off:off + wdt].rearrange("p (h two d) -> p h two d", two=2, d=32)
                    cos_b = cst[:, 0, i, :].unsqueeze(1).unsqueeze(1).to_broadcast([128, nh, 2, 32])
                    sin_b = cst[:, 1, i, :].unsqueeze(1).to_broadcast([128, nh, 32])
                    nsin_b = cst[:, 2, i, :].unsqueeze(1).to_broadcast([128, nh, 32])
                    ka, kb = "tmpA%d" % off, "tmpB%d" % off
                    S.op("dve", lambda: nc.vector.tensor_tensor(out=dA, in0=src, in1=cos_b, op=ALU.mult),
                         reads=[("ps", bank), "cst"], writes=[ka])
                    S.op("dve", lambda: nc.vector.tensor_tensor(out=dB[:, :, 0, :], in0=src[:, :, 1, :], in1=nsin_b,
                                                                op=ALU.mult),
                         reads=[("ps", bank), "cst"], writes=[kb + "a"])
                    S.op("dve", lambda: nc.vector.tensor_tensor(out=dB[:, :, 1, :], in0=src[:, :, 0, :], in1=sin_b,
                                                                op=ALU.mult),
                         reads=[("ps", bank), "cst"], writes=[kb])

                def st_qkT(i):
                    s = i % 2
                    sk = i % 3
                    bank = scratch()
                    qv = qkr[:, s, :].rearrange("p (h d) -> p h d", d=64)

                    def f():
                        for c in (range(4) if i > 0 else ()):
                            nc.tensor.transpose(out=bfv(bank)[:, c, :], in_=qkr[:, s, c * 128:(c + 1) * 128],
                                                identity=ident[:])
                        return nc.tensor.transpose(out=bfv(bank)[:, 4, :], in_=qkr[:, s, 512:640], identity=ident[:])
                    S.op("pe", f, reads=([("qkr", s, 0), ("qkr", s, 1)] if i > 0 else []) + [("qkr", s, 2), "ident"],
                         writes=[("ps", bank)])
                    if i > 0:
                        S.op("dve", lambda: nc.vector.tensor_copy(out=qT[:, s, :, :], in_=bfv(bank)[:, 0:4, :]),
                             reads=[("ps", bank)], writes=[("qT", s)])
                    S.op("act", lambda: nc.scalar.activation(out=kT[:, sk, :], in_=bfv(bank)[:, 4, :], func=AF.Copy),
                         reads=[("ps", bank)], writes=[("kT", sk)])

                def st_S(i, gsel=(0, 1)):
                    s = i % 2
                    sp0 = 0
                    skc, skp = i % 3, (i - 1) % 3
                    for g in gsel:
                        pr = slice(64 * g, 64 * g + 64)
                        for jj, sk in ((0, skp), (1, skc)):
                            bank = 3 + jj
                            S.op("pe", lambda: nc.tensor.matmul(ps[bank][:, :], lhsT=kT[pr, sk, :],
                                                                rhs=qT[pr, s, :, :], start=True, stop=True),
                                 reads=[("kT", sk), ("qT", s)], writes=[("ps", bank)])
                            S.op("act", lambda: nc.scalar.activation(out=PT[:, 0, g, jj, :], in_=ps[bank][:, :],
                                                                     func=AF.Exp, scale=0.125),
                                 reads=[("ps", bank)], writes=[("PT", 0, g, jj)])
                    if 1 not in gsel:
                        return
                    S.op("pool", lambda: nc.gpsimd.affine_select(
                        out=PT[:, 0, :, 0, :], in_=PT[:, 0, :, 0, :], compare_op=ALU.is_gt, fill=0.0, base=0,
                        pattern=[[0, 2], [0, 4], [-1, 128]], channel_multiplier=1),
                        reads=[("PT", 0, 0, 0), ("PT", 0, 1, 0)], writes=[("PT", 0, 0, 0), ("PT", 0, 1, 0)])
                    S.op("pool", lambda: nc.gpsimd.affine_select(
                        out=PT[:, 0, :, 1, :], in_=PT[:, 0, :, 1, :], compare_op=ALU.is_ge, fill=0.0, base=0,
                        pattern=[[0, 2], [0, 4], [1, 128]], channel_multiplier=-1),
                        reads=[("PT", 0, 0, 1), ("PT", 0, 1, 1)], writes=[("PT", 0, 0, 1), ("PT", 0, 1, 1)])
                    if i == 1:
                        S.op("pool", lambda: nc.gpsimd.tensor_scalar(
                            out=PT[:, 0, :, 0, :], in0=PT[:, 0, :, 0, :], scalar1=flag[:, 0:1], scalar2=0.0,
                            op0=ALU.mult, op1=ALU.add),
                            reads=[("PT", 0, 0, 0), ("PT", 0, 1, 0), "flag"], writes=[("PT", 0, 0, 0), ("PT", 0, 1, 0)])

                def st_PV(i, gsel=(0, 1)):
                    s = i % 2
                    svc, svp = i % NV, (i - 1) % NV
                    for g in gsel:
                        O = ps[5][:, 0:260].rearrange("p (c e) -> p c e", e=65)

                        def f():
                            for c in range(4):
                                nc.tensor.matmul(O[:, c, :], lhsT=PT[:, 0, g, 0, c * 128:(c + 1) * 128],
                                                 rhs=vaug[:, svp, g, :], start=True, stop=False)
                                i_ = nc.tensor.matmul(O[:, c, :], lhsT=PT[:, 0, g, 1, c * 128:(c + 1) * 128],
                                                      rhs=vaug[:, svc, g, :], start=False, stop=True)
                            return i_
                        S.op("pe", f, reads=[("PT", 0, g, 0), ("PT", 0, g, 1), ("vaug", svp), ("vaug", svc)],
                             writes=[("ps", 5)])
                        S.op("dve", lambda: nc.vector.tensor_tensor(out=dn[:, g, :], in0=O[:, :, 64],
                                                                    in1=esink[:, 4 * g:4 * g + 4], op=ALU.add),
                             reads=[("ps", 5), "esink"], writes=[("dn", g)])
                        S.op("dve", lambda: nc.vector.reciprocal(out=rc[:, g, :], in_=dn[:, g, :]),
                             reads=[("dn", g)], writes=[("rc", g)])
                        av = attn[:, s, :].rearrange("p (h d) -> p h d", d=64)
                        S.op("dve", lambda: nc.vector.tensor_tensor(
                            out=av[:, 4 * g:4 * g + 4, :], in0=O[:, :, 0:64],
                            in1=rc[:, g, :].unsqueeze(2).to_broadcast([128, 4, 64]), op=ALU.mult),
                            reads=[("ps", 5), ("rc", g)], writes=[("attn", s, g)])

                def st_attnT(i):
                    s = i % 2
                    sa = i % 2
                    bank = scratch()

                    def f():
                        for cc in range(4):
                            i_ = nc.tensor.transpose(out=bfv(bank)[:, cc, :], in_=attn[:, s, cc * 128:(cc + 1) * 128],
                                                     identity=ident[:])
                        return i_
                    S.op("pe", f, reads=[("attn", s, 0), ("attn", s, 1), "ident"], writes=[("ps", bank)])
                    S.op("act", lambda: nc.scalar.activation(out=AT[:, sa, 0:4, :], in_=bfv(bank)[:, 0:4, :], func=AF.Copy),
                         reads=[("ps", bank)], writes=[("AT", sa, 0)])

                def st_pool1(i):
                    suc, sup = i % NU, (i - 1) % NU
                    sp_ = i % 2
                    bank = scratch()
                    kp, kc = (3, 2) if i == 1 else (1, 0)
                    Y = ps[bank][:, :].rearrange("p (g t) -> p g t", t=128)

                    def f():
                        for g in range(4):
                            nc.tensor.matmul(Y[:, g, :], lhsT=upl[:, sup, g * 128:(g + 1) * 128],
                                             rhs=bands[:, kp * 4 + g, :], start=True, stop=False)
                            i_ = nc.tensor.matmul(Y[:, g, :], lhsT=upl[:, suc, g * 128:(g + 1) * 128],
                                                  rhs=bands[:, kc * 4 + g, :], start=False, stop=True)
                        return i_
                    S.op("pe", f, reads=[("upl", sup, 0), ("upl", sup, 1), ("upl", suc, 0), ("upl", suc, 1), "bands"],
                         writes=[("ps", bank)])
                    S.op("act", lambda: nc.scalar.activation(out=pooledT[:, sp_, :, :], in_=Y, func=AF.Copy),
                         reads=[("ps", bank)], writes=[("pooledT", sp_)])

                def st_pool2(i):
                    sp_ = i % 2
                    sa = i % 2
                    bank = scratch()
                    Z = ps[bank][:, :].rearrange("p (g t) -> p g t", t=128)

                    def f():
                        for g in range(4):
                            i_ = nc.tensor.matmul(Z[:, g, :], lhsT=wpool[:, g, :], rhs=pooledT[:, sp_, g, :],
                                                  start=True, stop=True)
                        return i_
                    S.op("pe", f, reads=[("pooledT", sp_), "wpool"], writes=[("ps", bank)])
                    S.op("dve", lambda: nc.vector.tensor_tensor(out=AT[:, sa, 4:8, :], in0=Z,
                                                                in1=psc[:].unsqueeze(2).to_broadcast([128, 4, 128]),
                                                                op=ALU.mult),
                         reads=[("ps", bank), "psc"], writes=[("AT", sa, 1)])

                def st_outproj(i, hsel=(0, 1)):
                    sa = i % 2
                    for hc in hsel:
                        bank = scratch()

                        def f():
                            for k in range(8):
                                i_ = nc.tensor.matmul(ps[bank][:, :], lhsT=AT[:, sa, k, :],
                                                      rhs=wout[:, k, hc * 512:(hc + 1) * 512],
                                                      start=(k == 0), stop=(k == 7))
                            return i_
                        S.op("pe", f, reads=[("AT", sa, 0), ("AT", sa, 1)] + [("wout", k) for k in range(8)],
                             writes=[("ps", bank)])
                        S.op("dve", lambda: nc.vector.tensor_tensor(
                            out=resid[:, i - 1, hc * 512:(hc + 1) * 512], in0=ps[bank][:, :],
                            in1=resid[:, i - 1, hc * 512:(hc + 1) * 512], op=ALU.add),
                            reads=[("ps", bank), ("resid", i - 1)], writes=[("resid", i - 1)])

                def st_norm2(i):
                    col = 20 + i
                    s = i % 2
                    rstd_ops(col, resid[:, i - 1, :], ("resid", i - 1), xhat2[:, 0, :], ("xhat2", 0))
                    if i >= 12:
                        S.op("dve", lambda: nc.vector.tensor_scalar(out=xhat2[:, 0, :], in0=resid[:, i - 1, :],
                                                                    scalar1=rstd[:, col:col + 1], scalar2=None,
                                                                    op0=ALU.mult),
                             reads=[("resid", i - 1), ("rstd", col)], writes=[("xhat2", 0)])
                    else:
                        S.op("pool", lambda: nc.gpsimd.tensor_scalar(out=xhat2[:, 0, :], in0=resid[:, i - 1, :],
                                                                     scalar1=rstd[:, col:col + 1], scalar2=0.0,
                                                                     op0=ALU.mult, op1=ALU.add),
                             reads=[("resid", i - 1), ("rstd", col)], writes=[("xhat2", 0)])

                def st_trx2(i):
                    s = i % 2
                    if not mod_ready["g2"]:
                        S.op("dve", lambda: nc.vector.scalar_tensor_tensor(out=g2T[:], in0=modT[:, 32:40], scalar=1.0,
                                                                           in1=n2T[:], op0=ALU.add, op1=ALU.mult),
                             reads=["modT", "n2T"], writes=["g2T"])
                        mod_ready["g2"] = True
                    transp_mod(lambda k: xhat2[:, 0, k * 128:(k + 1) * 128], ("xhat2", 0), g2T, modT[:, 24:32],
                               ["g2T", "modT"] + (["h2Tscr_done"] if i == 1 else []),
                               lambda k: h2T[:, k, (i - 1) * 128:i * 128], ("h2T", i - 1), ev_eng="act")

                wout_state = {"k": 0}

                def wout_fold():
                    k = wout_state["k"]
                    if k >= 8 or "wout" in DBG_SKIP:
                        return
                    wout_state["k"] += 2
                    S.dma("sp", lambda: nc.sync.dma_start(
                        out=wst[:, 0:2, :], in_=wout_d[k * 128:(k + 2) * 128, :].rearrange("(c p) d -> p c d", p=128)),
                        writes=[("wst", 0), ("wst", 1)])
                    for slot in range(2):
                        S.op("pool", lambda: nc.gpsimd.tensor_tensor(out=wout[:, k + slot, :], in0=wst[:, slot, :],
                                                                     in1=gate[:, :], op=ALU.mult),
                             reads=[("wst", slot), "gate"], writes=[("wout", k + slot)])
                    k += 1
                    if k == 7:
                        S.dma("sp", lambda: nc.sync.dma_start(out=gate[:],
                                                              in_=bada_d[0:1, 5120:6144].to_broadcast([128, D])),
                              reads=[], writes=["gate"])

                stages = [
                    (6, 1, st_PV, ((0,),)),
                    (5, 1, st_S, ((0,),)),
                    (8, 1, st_outproj, ((0,),)),
                    (10, 1, st_trx2, ()),
                    (9, 1, st_norm2, ()),
                    (6, 1, st_PV, ((1,),)),
                    (5, 1, st_S, ((1,),)),
                    (8, 1, st_outproj, ((1,),)),
                    (4, 0, st_qkT, ()),
                    (3, 0, st_inproj, ((0,),)),
                    (7, 1, st_attnT, ()),
                    (3, 0, st_inproj, ((1,),)),
                    (6, 1, st_pool2, ()),
                    (3, 0, st_inproj, ((2,),)),
                    (5, 1, st_pool1, ()),
                    (2, 0, st_trx, ()),
                    (1, 0, st_norm, ()),
                    (0, 0, st_load, ()),
                ]
                depth = max(o for o, _, _, _ in stages)
                for step in range((NT + depth) if nsteps is None else nsteps):
                    if stop == "setup":
                        break
                    order = stages
                    if step >= NT:
                        order = [st_ for st_ in stages if st_[2] is not st_norm2]
                        k_ = [j for j, st_ in enumerate(order) if st_[2] is st_trx][0]
                        order.insert(k_, (9, 1, st_norm2, ()))
                    for off, first, fn, extra in order:
                        i = step - off
                        if first <= i < NT:
                            fn(i, *extra)
                    if step == 1:
                        for _ in range(3):
                            mod_dma()
                    if step >= 3 and not ("modp1" in DBG_SKIP and step >= 2):
                        for _ in range(2):
                            if mod_state["next_pe"] < 20 or wout_state["k"] == 8:
                                mod_pe()
                    if mod_state["next_pe"] >= 12:
                        wout_fold()
                    if step == 0:
                        consts_bands((3,))
                        consts_done()
                    if mod_state["next_pe"] == 24 and not mod_state.get("g0"):
                        mod_state["g0"] = True
                        S.dma("pool", lambda: nc.gpsimd.dma_start(out=ring[:, 0, :, :], in_=wg_v[:, :, 0:G0]),
                              writes=[("ring", 0)])
                        S.dma("pool", lambda: nc.gpsimd.dma_start(out=ring[:, 1, :, :], in_=wu_v[:, :, 0:G0]),
                              writes=[("ring", 1)])
                while mod_state["next_pe"] < 24 and "modp1" not in DBG_SKIP:
                    mod_pe()
                if "x1" in dbg_d:
                    S.dma("sp", lambda: nc.sync.dma_start(out=dbg_d["x1"].rearrange("(t p) d -> p t d", p=128),
                                                          in_=resid[:, :, :]), reads=[("resid", t) for t in range(16)])
                S.barrier()
        es4 = ExitStack()
        with es4:
            sb4 = lambda name, shape, dtype: es4.enter_context(nc.sbuf_tensor("s_" + name, shape, dtype))
            wg = sb4("wg", [128, 2, 8, 512], BF16)
            wu = sb4("wu", [128, 2, 8, 512], BF16)
            wd = sb4("wd", [128, 2, 4, D], BF16)
            sg = sb4("sg", [128, 2, 512], F32)
            aT = sb4("aT", [128, 2, 4, 512], BF16)
            ybuf = sb4("ybuf", [128, 2, D], F32)
            yjunk = sb4("yjunk", [128, D], BF16)
            normf = sb4("normf", [128, D], F32)
            S.dma("sp", lambda: nc.sync.dma_start(out=normf[:], in_=nf_d[0:1, :].to_broadcast([128, D])),
                  writes=["normf"])
            gstart = [sum(GROUPS[:i]) for i in range(len(GROUPS))]
            wst_i = [0]

            def load_group(gi):
                b = gi % 2
                ncg = GROUPS[gi]
                f0 = gstart[gi]
                if gi > 0 or not mod_state.get("g0"):
                    S.dma("pool", lambda: nc.gpsimd.dma_start(out=wg[:, b, :, 0:ncg * 128],
                                                              in_=wg_v[:, :, f0 * 128:(f0 + ncg) * 128]),
                          writes=[("wg", b)])
                    S.dma("pool", lambda: nc.gpsimd.dma_start(out=wu[:, b, :, 0:ncg * 128],
                                                              in_=wu_v[:, :, f0 * 128:(f0 + ncg) * 128]),
                          writes=[("wu", b)])
                for j in range(0, ncg, 2):
                    f = f0 + j
                    S.dma("sp", lambda: nc.sync.dma_start(
                        out=wst[:, 0:2, :], in_=wd_d[f * 128:(f + 2) * 128, :].rearrange("(c p) d -> p c d", p=128)),
                        writes=[("wst", 0), ("wst", 1)])
                    for slot in range(2):
                        S.op("pool", lambda: nc.gpsimd.tensor_tensor(out=wd[:, b, j + slot, :], in0=wst[:, slot, :],
                                                                     in1=gate[:, :], op=ALU.mult),
                             reads=[("wst", slot), "gate"], writes=[("wd", b, j + slot)])

            par = {"gu": 0, "dn": 0, "a": 0, "y": 0}

            def gu_unit(gi, tb):
                b = gi % 2
                ncg = GROUPS[gi]
                if gi == 0 and mod_state.get("g0"):
                    wgs, wus, wgk, wuk = ring[:, 0, :, :], ring[:, 1, :, :], ("ring", 0), ("ring", 1)
                else:
                    wgs, wus, wgk, wuk = wg[:, b, :, :], wu[:, b, :, :], ("wg", b), ("wu", b)
                sa = par["a"]
                par["a"] ^= 1
                for j in range(ncg):
                    p = par["gu"]
                    par["gu"] ^= 1
                    gb, ub = p, 2 + p

                    def fg():
                        for k in range(8):
                            i_ = nc.tensor.matmul(ps[gb][:, :], lhsT=wgs[:, k, j * 128:(j + 1) * 128],
                                                  rhs=h2T[:, k, tb * 512:(tb + 1) * 512], start=(k == 0), stop=(k == 7))
                        return i_

                    def fu():
                        for k in range(8):
                            i_ = nc.tensor.matmul(ps[ub][:, :], lhsT=wus[:, k, j * 128:(j + 1) * 128],
                                                  rhs=h2T[:, k, tb * 512:(tb + 1) * 512], start=(k == 0), stop=(k == 7))
                        return i_
                    hk = [("h2T", t, k) for t in range(tb * 4, tb * 4 + 4) for k in range(8)]
                    S.op("pe", fg, reads=[wgk] + hk, writes=[("ps", gb)])
                    S.op("pe", fu, reads=[wuk] + hk, writes=[("ps", ub)])
                    S.op("act", lambda: nc.scalar.activation(out=sg[:, p, :], in_=ps[gb][:, :], func=AF.Silu),
                         reads=[("ps", gb)], writes=[("sg", p)])
                    S.op("dve", lambda: nc.vector.tensor_tensor(out=aT[:, sa, j, :], in0=ps[ub][:, :], in1=sg[:, p, :],
                                                                op=ALU.mult),
                         reads=[("ps", ub), ("sg", p)], writes=[("aT", sa, j)])
                return sa

            def dn_unit(gi, tb, sa):
                b = gi % 2
                ncg = GROUPS[gi]
                last = gi == len(GROUPS) - 1
                for tt in range(4):
                    tile = tb * 4 + tt
                    for hc in range(2):
                        p = par["dn"]
                        par["dn"] ^= 1
                        bank = 4 + p

                        def f():
                            for j in range(ncg):
                                i_ = nc.tensor.matmul(ps[bank][:, :], lhsT=aT[:, sa, j, tt * 128:(tt + 1) * 128],
                                                      rhs=wd[:, b, j, hc * 512:(hc + 1) * 512],
                                                      start=(j == 0), stop=(j == ncg - 1))
                            return i_
                        S.op("pe", f, reads=[("aT", sa, j) for j in range(ncg)] + [("wd", b, j) for j in range(ncg)],
                             writes=[("ps", bank)])
                        S.op("dve", lambda: nc.vector.tensor_tensor(
                            out=resid[:, tile, hc * 512:(hc + 1) * 512], in0=ps[bank][:, :],
                            in1=resid[:, tile, hc * 512:(hc + 1) * 512], op=ALU.add),
                            reads=[("ps", bank), ("resid", tile)], writes=[("resid", tile)])
                    if last:
                        if tt % 2 == 0:
                            flush_y()
                        col = 40 + tile
                        sy = par["y"]
                        par["y"] = (sy + 1) % 2
                        rstd_ops(col, resid[:, tile, :], ("resid", tile), yjunk[:, :], "yjunk")
                        deferred_y.append((tile, col, sy))

            deferred_y = []

            def flush_y():
                while deferred_y:
                    tile, col, sy = deferred_y.pop(0)
                    S.op("dve", lambda: nc.vector.scalar_tensor_tensor(
                        out=ybuf[:, sy, :], in0=resid[:, tile, :], scalar=rstd[:, col:col + 1], in1=normf[:, :],
                        op0=ALU.mult, op1=ALU.mult),
                        reads=[("resid", tile), ("rstd", col), "normf"], writes=[("ybuf", sy)])
                    if sy % 2 == 1:
                        S.dma("sp", lambda: nc.sync.dma_start(
                            out=y_d[(tile - 1) * 128:(tile + 1) * 128, :].rearrange("(c p) d -> p c d", p=128),
                            in_=ybuf[:, sy - 1:sy + 1, :]), reads=[("ybuf", sy - 1), ("ybuf", sy)])

            units = [(gi, tb) for gi in range(len(GROUPS)) for tb in range(4)]
            if stop is not None:
                units = []
            else:
                load_group(0)
                load_group(1)
            pending = None
            for n, (gi, tb) in enumerate(units):
                if tb == 3 and gi + 2 < len(GROUPS):
                    pass
                sa = gu_unit(gi, tb)
                if pending is not None:
                    dn_unit(*pending)
                    pg = pending[0]
                    if pending[1] == 3 and pg + 2 < len(GROUPS):
                        load_group(pg + 2)
                pending = (gi, tb, sa)
            if pending is not None:
                dn_unit(*pending)
            flush_y()
            S.barrier()
    return nc


_CACHE = {}


def _prep_inputs(x, c, positions, w_ada, b_ada, norm1, w_in, sinks, w_pool, pool_scale, w_out, norm2,
                 w_gate, w_up, w_down, norm_f):
    f = lambda a: np.ascontiguousarray(np.asarray(a), dtype=np.float32)
    x = f(x)
    positions = np.asarray(positions).astype(np.int32)
    fm = lambda v: np.ascontiguousarray(f(v).reshape(-1, 128).T)
    shared = {
        "w_ada": f(w_ada), "b_fm": fm(b_ada), "b_ada": f(b_ada).reshape(1, -1),
        "norm1": fm(norm1), "norm2": fm(norm2), "norm_f": f(norm_f).reshape(1, -1),
        "w_in": f(w_in), "sinks": f(sinks).reshape(1, 8), "w_pool": f(w_pool),
        "pool_scale": fm(pool_scale), "w_out": f(w_out), "w_gate": f(w_gate), "w_up": f(w_up), "w_down": f(w_down),
    }
    in_maps = []
    per_b = SEQ // TOK
    for core in range(NCORES):
        b, q = divmod(core, per_b)
        s0 = q * TOK
        xs = np.zeros((NT * 128, D), np.float32)
        ps_ = np.zeros((NT * 128,), np.int32)
        if q == 0:
            xs[128:] = x[b, 0:TOK]
            ps_[128:] = positions[b, 0:TOK]
        else:
            xs[:] = x[b, s0 - 128:s0 + TOK]
            ps_[:] = positions[b, s0 - 128:s0 + TOK]
        m = dict(shared)
        m["x"] = xs
        m["pos"] = np.ascontiguousarray(ps_.reshape(NT, 128).T)
        m["c"] = fm(f(c)[b])
        m["flag"] = np.full((128, 1), 0.0 if q == 0 else 1.0, np.float32)
        in_maps.append(m)
    return in_maps


def kernel(**inputs):
    in_maps = _prep_inputs(**inputs)
    if "nc" not in _CACHE:
        _CACHE["nc"] = build()
    res = run_bass_kernel_spmd(_CACHE["nc"], in_maps, core_ids=list(range(NCORES)))
    out = np.empty((2, SEQ, D), np.float32)
    per_b = SEQ // TOK
    for core in range(NCORES):
        b, q = divmod(core, per_b)
        out[b, q * TOK:(q + 1) * TOK] = res.results[core]["y"]
    return out
```

```python
import math
from contextlib import ExitStack

import numpy as np
import concourse.bass as bass
import concourse.mybir as mybir
from concourse.bass_utils import run_bass_kernel_spmd

F32 = mybir.dt.float32
F32R = mybir.dt.float32r
BF16 = mybir.dt.bfloat16
I32 = mybir.dt.int32
ALU = mybir.AluOpType
AF = mybir.ActivationFunctionType

D = 1024
SEQ = 8192
NCORES = 8
TOK = 2048
NT = 17
FF = 2816
NFC = FF // 128
GROUPS = [2, 4, 4, 4, 4, 4]
EPS = 1e-6
NSEM_DMA = {"sp": 46, "pool": 50}
DBG_SKIP = set()


class Ev:
    __slots__ = ("sem", "val", "eng", "dma")

    def __init__(self, sem, val, eng, dma):
        self.sem, self.val, self.eng, self.dma = sem, val, eng, dma


class Sched:
    def __init__(self, nc, es):
        self.nc = nc
        self.engs = {"pe": nc.tensor, "act": nc.scalar, "dve": nc.vector, "pool": nc.gpsimd, "sp": nc.sync}
        self.sem = {e: es.enter_context(nc.semaphore("c_" + e)) for e in ("pe", "act", "dve", "pool")}
        self.cnt = {e: 0 for e in self.sem}
        self.dsem = {q: [es.enter_context(nc.semaphore("d_%s%d" % (q, i))) for i in range(NSEM_DMA[q])]
                     for q in ("sp", "pool")}
        self.dcnt = {q: 0 for q in self.dsem}
        self.dval = {}
        self.waited = {}
        self.lastw = {}
        self.readers = {}

    def _need(self, e, sem, val):
        k = (e, id(sem))
        if self.waited.get(k, 0) < val:
            self.engs[e].wait_ge(sem, val)
            self.waited[k] = val

    def _deps(self, e, reads, writes, true_psw=()):
        need = {}

        def add(ev, raw, psum=False):
            if ev is None:
                return
            if (not ev.dma) and ev.eng == e and not raw and not (psum and e != "pe"):
                return
            k = id(ev.sem)
            if k not in need or need[k][1] < ev.val:
                need[k] = (ev.sem, ev.val)

        for k in reads:
            add(self.lastw.get(k), True)
        for k in writes:
            add(self.lastw.get(k), False, k in true_psw)
            for ev in self.readers.get(k, ()):
                add(ev, False)
        for sem, val in need.values():
            self._need(e, sem, val)

    def _record(self, ev, reads, writes):
        for k in writes:
            self.lastw[k] = ev
            self.readers[k] = []
        for k in reads:
            if k not in writes:
                self.readers.setdefault(k, []).append(ev)

    def op(self, e, fn, reads=(), writes=(), sync_waw=()):
        psr = [k for k in reads if isinstance(k, tuple) and k[0] == "ps"]
        true_psw = [k for k in writes if isinstance(k, tuple) and k[0] == "ps"] + list(sync_waw)
        if psr:
            reads = [k for k in reads if k not in psr]
            writes = list(writes) + [k for k in psr if k not in writes]
        self._deps(e, reads, writes, true_psw)
        inst = fn()
        self.cnt[e] += 1
        inst.then_inc(self.sem[e], 1)
        self._record(Ev(self.sem[e], self.cnt[e], e, False), reads, writes)

    def dma(self, q, fn, reads=(), writes=()):
        self._deps(q, reads, writes)
        j = self.dcnt[q]
        self.dcnt[q] += 1
        slot = j % NSEM_DMA[q]
        sem = self.dsem[q][slot]
        prev = self.dval.get((q, slot), 0)
        if prev:
            self._need(q, sem, prev)
        inst = fn()
        inst.then_inc(sem, 16)
        self.dval[(q, slot)] = prev + 16
        self._record(Ev(sem, prev + 16, q, True), reads, writes)

    def barrier(self, engines=("pe", "act", "dve", "pool", "sp")):
        for e in engines:
            for e2 in self.sem:
                if e2 != e and self.cnt[e2]:
                    self._need(e, self.sem[e2], self.cnt[e2])
            for (q, slot), v in self.dval.items():
                self._need(e, self.dsem[q][slot], v)
        self.lastw.clear()
        self.readers.clear()


def build(dbg=(), stop=None, nsteps=None):
    nc = bass.Bass("TRN2", target_bir_lowering=False)
    dt = nc.dram_tensor
    x_d = dt("x", [NT * 128, D], F32, kind="ExternalInput").ap()
    pos_d = dt("pos", [128, NT], I32, kind="ExternalInput").ap()
    c_d = dt("c", [128, 8], F32, kind="ExternalInput").ap()
    flag_d = dt("flag", [128, 1], F32, kind="ExternalInput").ap()
    wada_d = dt("w_ada", [D, 6 * D], F32, kind="ExternalInput").ap()
    bfm_d = dt("b_fm", [128, 48], F32, kind="ExternalInput").ap()
    bada_d = dt("b_ada", [1, 6 * D], F32, kind="ExternalInput").ap()
    n1_d = dt("norm1", [128, 8], F32, kind="ExternalInput").ap()
    n2_d = dt("norm2", [128, 8], F32, kind="ExternalInput").ap()
    nf_d = dt("norm_f", [1, D], F32, kind="ExternalInput").ap()
    win_d = dt("w_in", [D, 1280], F32, kind="ExternalInput").ap()
    sinks_d = dt("sinks", [1, 8], F32, kind="ExternalInput").ap()
    wpool_d = dt("w_pool", [4, 128, 128], F32, kind="ExternalInput").ap()
    psc_d = dt("pool_scale", [128, 4], F32, kind="ExternalInput").ap()
    wout_d = dt("w_out", [D, D], F32, kind="ExternalInput").ap()
    wg_d = dt("w_gate", [D, FF], F32, kind="ExternalInput").ap()
    wu_d = dt("w_up", [D, FF], F32, kind="ExternalInput").ap()
    wd_d = dt("w_down", [FF, D], F32, kind="ExternalInput").ap()
    y_d = dt("y", [TOK, D], F32, kind="ExternalOutput").ap()
    dbg_d = {}
    for name, shape in dbg:
        dbg_d[name] = dt("dbg_" + name, list(shape), F32, kind="ExternalOutput").ap()

    es = ExitStack()
    with es:
        S = Sched(nc, es)
        sb = lambda name, shape, dtype: es.enter_context(nc.sbuf_tensor("s_" + name, shape, dtype))
        resid = sb("resid", [128, 16, D], F32)
        h2T = sb("h2T", [128, 8, TOK], BF16)
        ident = sb("ident", [128, 128], BF16)
        bands = sb("bands", [128, 16, 128], BF16)
        cst = sb("cst", [128, 3, NT, 32], F32)
        modT = sb("modT", [128, 48], F32)
        g1T = sb("g1T", [128, 8], F32)
        g2T = sb("g2T", [128, 8], F32)
        n1T = sb("n1T", [128, 8], F32)
        n2T = sb("n2T", [128, 8], F32)
        bfm = sb("bfm", [128, 48], F32)
        gate = sb("gate", [128, D], F32)
        esink = sb("esink", [128, 8], F32)
        flag = sb("flag", [128, 1], F32)
        psc = sb("psc", [128, 4], F32)
        ss = sb("ss", [128, 64], F32)
        ms = sb("ms", [128, 64], F32)
        rstd = sb("rstd", [128, 64], F32)
        neghalf = sb("neghalf", [128, 1], F32)
        sc2 = sb("sc2", [128, 8, 2], BF16)
        wst = sb("wst", [128, 2, D], F32)
        ring = sb("ring", [128, 3, 8, 256], BF16)
        ps = [es.enter_context(nc.psum_tensor("ps%d" % i, [128, 512], F32)) for i in range(8)]

        wg_v = wg_d.rearrange("(k p) n -> p k n", p=128)
        wu_v = wu_d.rearrange("(k p) n -> p k n", p=128)
        G0 = GROUPS[0] * 128

        def bfv(bank):
            return ps[bank][:, :].bitcast(BF16).rearrange("p (k t) -> p k t", t=128)

        scr_i = [0]

        def scratch():
            scr_i[0] ^= 1
            return 6 + scr_i[0]

        def dump(name, src_ap, key, rows=None):
            if name in dbg_d:
                S.dma("sp", lambda: nc.sync.dma_start(out=dbg_d[name], in_=src_ap), reads=[key])

        es1 = ExitStack()
        with es1:
            sb1 = lambda name, shape, dtype: es1.enter_context(nc.sbuf_tensor("s_" + name, shape, dtype))
            cT = sb1("cT", [128, 8], F32)
            scb = sb1("scb", [128, 8, 128], BF16)
            posi = sb1("posi", [128, NT], I32)
            S.dma("sp", lambda: nc.sync.dma_start(out=cT[:], in_=c_d[:, :]), writes=["cT"])
            S.dma("sp", lambda: nc.sync.dma_start(out=bfm[:], in_=bfm_d[:, :]), writes=["bfm"])
            S.dma("sp", lambda: nc.sync.dma_start(out=n1T[:], in_=n1_d[:, :]), writes=["n1T"])
            S.dma("sp", lambda: nc.sync.dma_start(out=n2T[:], in_=n2_d[:, :]), writes=["n2T"])
            S.dma("sp", lambda: nc.sync.dma_start(out=flag[:], in_=flag_d[:, :]), writes=["flag"])
            S.dma("sp", lambda: nc.sync.dma_start(out=psc[:], in_=psc_d[:, :]), writes=["psc"])
            S.dma("sp", lambda: nc.sync.dma_start(out=posi[:], in_=pos_d[:, :]), writes=["posi"])
            S.dma("sp", lambda: nc.sync.dma_start(out=esink[:], in_=sinks_d[0:1, :].to_broadcast([128, 8])),
                  writes=["esink"])

            S.op("act", lambda: nc.scalar.activation(out=cT[:], in_=cT[:], func=AF.Silu), reads=["cT"], writes=["cT"])
            S.op("dve", lambda: nc.vector.tensor_copy(out=sc2[:], in_=cT[:].unsqueeze(2).to_broadcast([128, 8, 2])),
                 reads=["cT"], writes=["sc2"])
            S.op("dve", lambda: nc.vector.tensor_copy(out=scb[:], in_=cT[:].unsqueeze(2).to_broadcast([128, 8, 128])),
                 reads=["cT"], writes=["scb"])
            S.op("act", lambda: nc.scalar.activation(out=esink[:], in_=esink[:], func=AF.Exp),
                 reads=["esink"], writes=["esink"])
            S.op("pool", lambda: nc.gpsimd.memset(neghalf[:], -0.5), writes=["neghalf"])

            wada_v = wada_d.rearrange("(k p) n -> p k n", p=128)
            mod_state = {"next_dma": 0, "next_pe": 0}
            order = list(range(24))

            def mod_dma():
                b = mod_state["next_dma"]
                if b >= 24:
                    return
                mod_state["next_dma"] += 1
                slot = b % 3
                S.dma("pool", lambda: nc.gpsimd.dma_start(out=ring[:, slot, :, :],
                                                          in_=wada_v[:, :, b * 256:(b + 1) * 256]),
                      writes=[("ring", slot)])

            def mod_pe():
                b = mod_state["next_pe"]
                if b >= 24:
                    return
                mod_state["next_pe"] += 1
                slot = b % 3
                v = b // 4
                off = (b % 4) * 256
                bank = b if b < 8 else scratch()
                if v in (2, 5):
                    gt = gate
                    gk = "gate"

                    def f():
                        for k in range(8):
                            i_ = nc.tensor.matmul(ps[bank][:, 0:256], lhsT=scb[:, k, :],
                                                  rhs=ring[:, slot, k, :],
                                                  start=(k == 0), stop=(k == 7))
                        return i_
                    S.op("pe", f, reads=["scb", ("ring", slot)], writes=[("ps", bank)])
                    S.op("dve", lambda: nc.vector.tensor_tensor(out=gt[:, off:off + 256], in0=ps[bank][:, 0:256],
                                                                in1=gt[:, off:off + 256], op=ALU.add),
                         reads=[("ps", bank)], writes=[gk])
                else:
                    def f():
                        for cc in range(2):
                            for k in range(8):
                                i_ = nc.tensor.matmul(ps[bank][:, 2 * cc:2 * cc + 2],
                                                      lhsT=ring[:, slot, k, cc * 128:(cc + 1) * 128],
                                                      rhs=sc2[:, k, 0:2],
                                                      start=(k == 0), stop=(k == 7))
                        return i_
                    S.op("pe", f, reads=["sc2", ("ring", slot)], writes=[("ps", bank)])
                    col = v * 8 + (b % 4) * 2
                    S.op("dve", lambda: nc.vector.tensor_tensor(
                        out=modT[:, col:col + 2], in0=ps[bank][:, 0:4].rearrange("p (c t) -> p c t", t=2)[:, :, 0],
                        in1=bfm[:, col:col + 2], op=ALU.add),
                        reads=[("ps", bank), "bfm"], writes=["modT"])
                if mod_state["next_dma"] < 8 or mod_state["next_pe"] > 8:
                    mod_dma()

            S.dma("sp", lambda: nc.sync.dma_start(out=gate[:], in_=bada_d[0:1, 2048:3072].to_broadcast([128, D])),
                  writes=["gate"])
            mod_dma()
            mod_dma()
            mod_dma()

            es2 = ExitStack()
            with es2:
                sb2 = lambda name, shape, dtype: es2.enter_context(nc.sbuf_tensor("s_" + name, shape, dtype))
                win = sb2("win", [128, 8, 1280], BF16)
                wout = sb2("wout", [128, 8, D], BF16)
                wpool = sb2("wpool", [128, 4, 128], BF16)
                xhat = sb2("xhat", [128, 1, D], BF16)
                xhat2 = sb2("xhat2", [128, 1, D], BF16)
                h1T = sb2("h1T", [128, 2, 8, 128], BF16)
                tmpA = sb2("tmpA", [128, 640], F32)
                tmpB = sb2("tmpB", [128, 640], F32)
                qkr = sb2("qkr", [128, 2, 640], BF16)
                qT = sb2("qT", [128, 2, 4, 128], BF16)
                kT = sb2("kT", [128, 3, 128], BF16)
                NV = 6
                vaug = sb2("vaug", [128, NV, 2, 65], BF16)
                NU = 4
                upl = sb2("upl", [128, NU, 512], BF16)
                PT = sb2("PT", [128, 1, 2, 2, 512], BF16)
                attn = sb2("attn", [128, 2, 512], BF16)
                dn = sb2("dn", [128, 2, 4], F32)
                rc = sb2("rc", [128, 2, 4], F32)
                pooledT = sb2("pooledT", [128, 2, 4, 128], BF16)
                AT = sb2("AT", [128, 2, 8, 128], BF16)

                win_v = win_d.rearrange("(k p) n -> p k n", p=128)
                def load_win():
                    S.dma("pool", lambda: nc.gpsimd.dma_start(out=win[:, :, 512:1280], in_=win_v[:, :, 512:1280]),
                          writes=["win2"])

                def load_win_q():
                    S.dma("pool", lambda: nc.gpsimd.dma_start(out=win[:, :, 0:512], in_=win_v[:, :, 0:512]),
                          writes=["win"])
                    S.dma("pool", lambda: nc.gpsimd.dma_start(out=wpool[:], in_=wpool_d.rearrange("g c d -> c g d")),
                          writes=["wpool"])

                scr = h2T[:, :, :].rearrange("p k t -> p (k t)").bitcast(F32)
                scri = h2T[:, :, :].rearrange("p k t -> p (k t)").bitcast(I32)
                idf = scr[:, 0:128]
                bm = scr[:, 128:256]
                bt = scr[:, 256:384]
                bt2 = scr[:, 384:512]
                colsc = scr[:, 512:640]
                posf = scr[:, 640:640 + NT]
                invf = scr[:, 672:704]
                thb = scr[:, 704:736]
                A4 = 2 * NT * 32
                v4 = lambda a: a.rearrange("p (s t d) -> p s t d", s=2, d=32)
                ang = v4(scr[:, 1024:1024 + A4])
                kf = v4(scr[:, 2112:2112 + A4])
                fx = v4(scr[:, 3200:3200 + A4])
                ki = v4(scri[:, 4288:4288 + A4])

                def pl(fn, r=(), w=()):
                    S.op("pool", fn, reads=r, writes=w)

                def dv(fn, r=(), w=()):
                    S.op("dve", fn, reads=r, writes=w)

                dmat = scr[:, 5504:5632]
                tpl = scr[:, 5632:5760]
                pl(lambda: nc.gpsimd.iota(dmat, pattern=[[1, 128]], base=0, channel_multiplier=-1,
                                          allow_small_or_imprecise_dtypes=True), w=["dmat"])
                pl(lambda: nc.gpsimd.iota(tpl, pattern=[[1, 128]], base=1, channel_multiplier=0,
                                          allow_small_or_imprecise_dtypes=True), w=["tpl"])
                dv(lambda: nc.vector.tensor_scalar(out=idf, in0=dmat, scalar1=0.0, scalar2=None, op0=ALU.is_equal),
                   r=["dmat"], w=["idf"])
                dv(lambda: nc.vector.tensor_copy(out=ident[:], in_=idf), r=["idf"], w=["ident"])
                def consts_bands(gsel):
                    for g, w in [(g_, (2, 4, 8, 16)[g_]) for g_ in gsel]:
                        hw = (w - 1) / 2.0
                        dv(lambda: nc.vector.tensor_scalar(out=bt, in0=dmat, scalar1=0.0, scalar2=None,
                                                           op0=ALU.is_ge), r=["dmat"], w=["bt"])
                        dv(lambda: nc.vector.scalar_tensor_tensor(out=bm, in0=dmat, scalar=float(w - 1) + 0.25, in1=bt,
                                                                  op0=ALU.is_le, op1=ALU.mult),
                           r=["dmat", "bt"], w=["bm"])
                        dv(lambda: nc.vector.scalar_tensor_tensor(out=bt, in0=bm, scalar=1.0 / w, in1=idf,
                                                                  op0=ALU.mult, op1=ALU.subtract),
                           r=["bm", "idf"], w=["bt"])
                        dv(lambda: nc.vector.tensor_copy(out=bands[:, 0 * 4 + g, :], in_=bt), r=["bt"], w=["bands"])
                        dv(lambda: nc.vector.tensor_scalar(out=colsc, in0=tpl, scalar1=float(w), scalar2=None,
                                                           op0=ALU.min), r=["tpl"], w=["colsc"])
                        dv(lambda: nc.vector.reciprocal(out=colsc, in_=colsc), r=["colsc"], w=["colsc"])
                        dv(lambda: nc.vector.tensor_tensor(out=bt2, in0=bm, in1=colsc, op=ALU.mult),
                           r=["bm", "colsc"], w=["bt2"])
                        dv(lambda: nc.vector.tensor_tensor(out=bt2, in0=bt2, in1=idf, op=ALU.subtract),
                           r=["bt2", "idf"], w=["bt2"])
                        dv(lambda: nc.vector.tensor_tensor(out=bt, in0=bt, in1=bt2, op=ALU.subtract),
                           r=["bt", "bt2"], w=["bt"])
                        dv(lambda: nc.vector.scalar_tensor_tensor(out=bt, in0=bt, scalar=flag[:, 0:1], in1=bt2,
                                                                  op0=ALU.mult, op1=ALU.add),
                           r=["bt", "bt2", "flag"], w=["bt"])
                        dv(lambda: nc.vector.tensor_copy(out=bands[:, 2 * 4 + g, :], in_=bt), r=["bt"], w=["bands"])
                        dv(lambda: nc.vector.tensor_scalar(out=bm, in0=dmat, scalar1=float(w - 129) + 0.25,
                                                           scalar2=1.0 / w, op0=ALU.is_le, op1=ALU.mult),
                           r=["dmat"], w=["bm"])
                        dv(lambda: nc.vector.tensor_copy(out=bands[:, 1 * 4 + g, :], in_=bm), r=["bm"], w=["bands"])
                        dv(lambda: nc.vector.tensor_scalar(out=bands[:, 3 * 4 + g, :], in0=bm, scalar1=flag[:, 0:1],
                                                           scalar2=None, op0=ALU.mult), r=["bm", "flag"], w=["bands"])


                def consts_rope():
                    pl(lambda: nc.gpsimd.iota(invf[:], pattern=[[1, 32]], base=0, channel_multiplier=0,
                                              allow_small_or_imprecise_dtypes=True), w=["invf"])
                    pl(lambda: nc.gpsimd.tensor_scalar(out=invf[:], in0=invf[:], scalar1=-1.0 / 32.0, scalar2=0.0,
                                                       op0=ALU.mult, op1=ALU.add), r=["invf"], w=["invf"])
                    pl(lambda: nc.gpsimd.memset(thb[:], 10000.0), w=["thb"])
                    pl(lambda: nc.gpsimd.tensor_tensor(out=invf[:], in0=thb[:], in1=invf[:], op=ALU.pow),
                       r=["invf", "thb"], w=["invf"])

                def consts_rope_dve():
                    S.op("dve", lambda: nc.vector.tensor_copy(out=posf[:], in_=posi[:]), reads=["posi"], writes=["posf"])
                    S.op("dve", lambda: nc.vector.tensor_tensor(
                        out=ang[:, 0, :, :], in0=posf[:].unsqueeze(2).to_broadcast([128, NT, 32]),
                        in1=invf[:].unsqueeze(1).to_broadcast([128, NT, 32]), op=ALU.mult),
                        reads=["posf", "invf"], writes=["ang"])
                    S.op("dve", lambda: nc.vector.tensor_scalar(out=ang[:, 1, :, :], in0=ang[:, 0, :, :],
                                                                scalar1=math.pi / 2, scalar2=None, op0=ALU.add),
                         reads=["ang"], writes=["ang"])
                    TWO_PI_HI = 6.28125
                    TWO_PI_LO = 2.0 * math.pi - 6.28125
                    S.op("dve", lambda: nc.vector.tensor_scalar(out=kf[:], in0=ang[:], scalar1=1.0 / (2 * math.pi),
                                                                scalar2=None, op0=ALU.mult), reads=["ang"], writes=["kf"])
                    S.op("dve", lambda: nc.vector.tensor_copy(out=ki[:], in_=kf[:]), reads=["kf"], writes=["ki"])
                    S.op("dve", lambda: nc.vector.tensor_copy(out=kf[:], in_=ki[:]), reads=["ki"], writes=["kf"])
                    S.op("dve", lambda: nc.vector.scalar_tensor_tensor(out=ang[:], in0=kf[:], scalar=-TWO_PI_HI,
                                                                       in1=ang[:], op0=ALU.mult, op1=ALU.add),
                         reads=["kf", "ang"], writes=["ang"])
                    S.op("dve", lambda: nc.vector.scalar_tensor_tensor(out=ang[:], in0=kf[:], scalar=-TWO_PI_LO,
                                                                       in1=ang[:], op0=ALU.mult, op1=ALU.add),
                         reads=["kf", "ang"], writes=["ang"])
                    S.op("dve", lambda: nc.vector.tensor_scalar(out=fx[:], in0=ang[:], scalar1=math.pi,
                                                                scalar2=-2 * math.pi, op0=ALU.is_gt, op1=ALU.mult),
                         reads=["ang"], writes=["fx"])
                    S.op("dve", lambda: nc.vector.tensor_tensor(out=ang[:], in0=ang[:], in1=fx[:], op=ALU.add),
                         reads=["ang", "fx"], writes=["ang"])
                    S.op("dve", lambda: nc.vector.tensor_scalar(out=fx[:], in0=ang[:], scalar1=-math.pi,
                                                                scalar2=2 * math.pi, op0=ALU.is_lt, op1=ALU.mult),
                         reads=["ang"], writes=["fx"])
                    S.op("dve", lambda: nc.vector.tensor_tensor(out=ang[:], in0=ang[:], in1=fx[:], op=ALU.add),
                         reads=["ang", "fx"], writes=["ang"])
                    S.op("dve", lambda: nc.vector.tensor_scalar(out=ang[:], in0=ang[:], scalar1=math.pi,
                                                                scalar2=-math.pi, op0=ALU.min, op1=ALU.max),
                         reads=["ang"], writes=["ang"])
                    S.op("act", lambda: nc.scalar.activation(out=cst[:, 1, :, :], in_=ang[:, 0, :, :], func=(AF.Copy if "nosin" in DBG_SKIP else AF.Sin)),
                         reads=["ang"], writes=["cst"])
                    S.op("act", lambda: nc.scalar.activation(out=cst[:, 0, :, :], in_=ang[:, 1, :, :], func=(AF.Copy if "nosin" in DBG_SKIP else AF.Sin)),
                         reads=["ang"], writes=["cst"])
                    S.op("dve", lambda: nc.vector.tensor_scalar(out=cst[:, 2, :, :], in0=cst[:, 1, :, :], scalar1=-1.0,
                                                                scalar2=None, op0=ALU.mult),
                         reads=["cst"], writes=["cst"])

                def consts_done():
                    S.op("dve", lambda: nc.vector.memset(ss[:, 63:64], 0.0),
                         writes=["idf", "bm", "bt", "bt2", "colsc", "posf", "invf", "thb", "ang", "kf", "ki", "fx",
                                 "dmat", "tpl", "h2Tscr_done"])

                S.op("pool", lambda: nc.gpsimd.memset(vaug[:], 1.0), writes=[("vaug", s) for s in range(NV)])

                consts_bands((0,))
                for b_ in range(8):
                    mod_pe()
                    if b_ == 1:
                        consts_rope()
                        consts_bands((1,))
                        consts_rope_dve()
                    if b_ == 4:
                        load_win()
                consts_bands((2,))
                S.op("dve", lambda: nc.vector.scalar_tensor_tensor(out=g1T[:], in0=modT[:, 8:16], scalar=1.0,
                                                                   in1=n1T[:], op0=ALU.add, op1=ALU.mult),
                     reads=["modT", "n1T"], writes=["g1T"])
                mod_ready = {"g2": False}

                def xt_ap(i):
                    return wst[:, 1, :] if i == 0 else resid[:, i - 1, :]

                def xkey(i):
                    return ("wst", 1) if i == 0 else ("resid", i - 1)

                def st_load(i):
                    if i == 0:
                        S.dma("sp", lambda: nc.sync.dma_start(out=xt_ap(0), in_=x_d[0:128, :]), writes=[xkey(0)])
                    elif i % 2 == 1:
                        thr = [("h1T", (i - 3) % 2, 7)] if i >= 3 else []
                        S.dma("sp", lambda: nc.sync.dma_start(
                            out=resid[:, i - 1:i + 1, :],
                            in_=x_d[i * 128:(i + 2) * 128, :].rearrange("(c p) d -> p c d", p=128)),
                            reads=thr, writes=[("resid", i - 1), ("resid", i)])

                def rstd_ops(col, src_ap, src_key, junk_ap, junk_key):
                    S.op("act", lambda: nc.scalar.activation(out=junk_ap, in_=src_ap, func=AF.Square,
                                                             **({} if "noacc" in DBG_SKIP else {"accum_out": ss[:, col:col + 1]})),
                         reads=[src_key], writes=[junk_key, ("ss", col)], sync_waw=[junk_key])
                    S.op("pool", lambda: nc.gpsimd.tensor_scalar(out=ms[:, col:col + 1], in0=ss[:, col:col + 1],
                                                                 scalar1=1.0 / D, scalar2=EPS, op0=ALU.mult, op1=ALU.add),
                         reads=[("ss", col)], writes=[("ms", col)])
                    S.op("pool", lambda: nc.gpsimd.tensor_tensor(out=rstd[:, col:col + 1], in0=ms[:, col:col + 1],
                                                                 in1=neghalf[:, 0:1], op=ALU.pow),
                         reads=[("ms", col), "neghalf"], writes=[("rstd", col)])

                def st_norm(i):
                    s = i % 2
                    if "norm" in DBG_SKIP and i >= 1:
                        return
                    rstd_ops(i, xt_ap(i), xkey(i), xhat[:, 0, :], ("xhat", 0))
                    S.op("pool", lambda: nc.gpsimd.tensor_scalar(out=xhat[:, 0, :], in0=xt_ap(i), scalar1=rstd[:, i:i + 1],
                                                                 scalar2=0.0, op0=ALU.mult, op1=ALU.add),
                         reads=[xkey(i), ("rstd", i)], writes=[("xhat", 0)])

                def transp_mod(src_ap_fn, src_key, gT, shT_ap, gkeys, dst_fn, dst_key, ev_eng="dve"):
                    def f():
                        for k in range(8):
                            i_ = nc.tensor.transpose(out=bfv(0)[:, k, :], in_=src_ap_fn(k), identity=ident[:])
                        return i_
                    if "trxpe" not in DBG_SKIP:
                        S.op("pe", f, reads=[src_key, "ident"], writes=[("ps", 0)])
                    for k in range(8):
                        if "trxev" in DBG_SKIP:
                            break
                        if ev_eng == "act":
                            S.op("act", lambda: nc.scalar.activation(out=dst_fn(k), in_=bfv(0)[:, k, :], func=AF.Identity,
                                                                     scale=gT[:, k:k + 1], bias=shT_ap[:, k:k + 1]),
                                 reads=[("ps", 0)] + gkeys, writes=[dst_key + (k,)])
                        else:
                            S.op("dve", lambda: nc.vector.tensor_scalar(out=dst_fn(k), in0=bfv(0)[:, k, :],
                                                                        scalar1=gT[:, k:k + 1], scalar2=shT_ap[:, k:k + 1],
                                                                        op0=ALU.mult, op1=ALU.add),
                                 reads=[("ps", 0)] + gkeys, writes=[dst_key + (k,)])

                def st_trx(i):
                    s = i % 2
                    if "trx" in DBG_SKIP:
                        return
                    transp_mod(lambda k: xhat[:, 0, k * 128:(k + 1) * 128], ("xhat", 0), g1T, modT[:, 0:8],
                               ["g1T", "modT"], lambda k: h1T[:, s, k, :], ("h1T", s),
                               ev_eng=("act" if i <= 3 else "dve"))

                inproj_bank = [1]

                def st_inproj(i, bsel=(0, 1, 2)):
                    s = i % 2
                    if i == 0:
                        bsel = tuple(b_ for b_ in bsel if b_ != 0)
                        if not bsel:
                            return
                    if "noinproj" in DBG_SKIP:
                        return
                    su = i % NU
                    sv = i % NV
                    hkeys = [("h1T", s, k) for k in range(8)]
                    banks = []
                    for (c0, w, wk) in [((0, 512, "win"), (512, 512, "win2"), (1024, 256, "win2"))[b_] for b_ in bsel]:
                        bank = inproj_bank[0]
                        inproj_bank[0] = 3 - bank
                        banks.append(bank)

                        def f():
                            for k in range(8):
                                i_ = nc.tensor.matmul(ps[bank][:, 0:w], lhsT=h1T[:, s, k, :], rhs=win[:, k, c0:c0 + w],
                                                      start=(k == 0), stop=(k == 7))
                            return i_
                        S.op("pe", f, reads=hkeys + (["win"] if c0 == 0 else ["win2"]),
                             writes=[("ps", bank)])
                        if c0 == 0:
                            rope(i, bank, 0, 8)
                        elif c0 == 512:
                            rope(i, bank, 512, 2)
                            if "novu" in DBG_SKIP:
                                continue
                            S.op("act", lambda: nc.scalar.activation(
                                out=vaug[:, sv, :, 0:64], in_=ps[bank][:, 128:256].rearrange("p (g d) -> p g d", d=64),
                                func=AF.Copy), reads=[("ps", bank)], writes=[("vaug", sv)])
                            S.op("act", lambda: nc.scalar.activation(out=upl[:, su, 0:256], in_=ps[bank][:, 256:512],
                                                                     func=AF.Copy),
                                 reads=[("ps", bank)], writes=[("upl", su, 0)])
                        elif "novu" not in DBG_SKIP:
                            S.op("act", lambda: nc.scalar.activation(out=upl[:, su, 256:512], in_=ps[bank][:, 0:256],
                                                                     func=AF.Copy),
                                 reads=[("ps", bank)], writes=[("upl", su, 1)])
                    s2 = i % 2
                    for g in (range(2) if 0 in bsel else ()):
                        S.op("pool", lambda: nc.gpsimd.tensor_tensor(
                            out=qkr[:, s2, 0:512].rearrange("p (c g d) -> p g c d", g=2, d=64)[:, g, :, :],
                            in0=tmpA[:, g * 256:(g + 1) * 256].rearrange("p (c d) -> p c d", d=64),
                            in1=tmpB[:, g * 256:(g + 1) * 256].rearrange("p (c d) -> p c d", d=64), op=ALU.add),
                            reads=["tmpA0", "tmpB0", "tmpB0a"], writes=[("qkr", s2, g)])
                    if 1 in bsel:
                        S.op("pool", lambda: nc.gpsimd.tensor_tensor(out=qkr[:, s2, 512:640], in0=tmpA[:, 512:640],
                                                                     in1=tmpB[:, 512:640], op=ALU.add),
                             reads=["tmpA512", "tmpB512", "tmpB512a"], writes=[("qkr", s2, 2)])

                def rope(i, bank, off, nh):
                    if "norope" in DBG_SKIP:
                        return
                    wdt = nh * 64
                    src = ps[bank][:, 0:wdt].rearrange("p (h two d) -> p h two d", two=2, d=32)
                    dA = tmpA[:, off:off + wdt].rearrange("p (h two d) -> p h two d", two=2, d=32)
                    dB = tmpB[:, off:off + wdt].rearrange("p (h two d) -> p h two d", two=2, d=32)
                    cos_b = cst[:, 0, i, :].unsqueeze(1).unsqueeze(1).to_broadcast([128, nh, 2, 32])
                    sin_b = cst[:, 1, i, :].unsqueeze(1).to_broadcast([128, nh, 32])
                    nsin_b = cst[:, 2, i, :].unsqueeze(1).to_broadcast([128, nh, 32])
                    ka, kb = "tmpA%d" % off, "tmpB%d" % off
                    S.op("dve", lambda: nc.vector.tensor_tensor(out=dA, in0=src, in1=cos_b, op=ALU.mult),
                         reads=[("ps", bank), "cst"], writes=[ka])
                    S.op("dve", lambda: nc.vector.tensor_tensor(out=dB[:, :, 0, :], in0=src[:, :, 1, :], in1=nsin_b,
                                                                op=ALU.mult),
                         reads=[("ps", bank), "cst"], writes=[kb + "a"])
                    S.op("dve", lambda: nc.vector.tensor_tensor(out=dB[:, :, 1, :], in0=src[:, :, 0, :], in1=sin_b,
                                                                op=ALU.mult),
                         reads=[("ps", bank), "cst"], writes=[kb])

                def st_qkT(i):
                    s = i % 2
                    sk = i % 3
                    bank = scratch()
                    qv = qkr[:, s, :].rearrange("p (h d) -> p h d", d=64)

                    def f():
                        for c in (range(4) if i > 0 else ()):
                            nc.tensor.transpose(out=bfv(bank)[:, c, :], in_=qkr[:, s, c * 128:(c + 1) * 128],
                                                identity=ident[:])
                        return nc.tensor.transpose(out=bfv(bank)[:, 4, :], in_=qkr[:, s, 512:640], identity=ident[:])
                    S.op("pe", f, reads=([("qkr", s, 0), ("qkr", s, 1)] if i > 0 else []) + [("qkr", s, 2), "ident"],
                         writes=[("ps", bank)])
                    if i > 0:
                        S.op("dve", lambda: nc.vector.tensor_copy(out=qT[:, s, :, :], in_=bfv(bank)[:, 0:4, :]),
                             reads=[("ps", bank)], writes=[("qT", s)])
                    S.op("act", lambda: nc.scalar.activation(out=kT[:, sk, :], in_=bfv(bank)[:, 4, :], func=AF.Copy),
                         reads=[("ps", bank)], writes=[("kT", sk)])

                def st_S(i, gsel=(0, 1)):
                    s = i % 2
                    sp0 = 0
                    skc, skp = i % 3, (i - 1) % 3
                    for g in gsel:
                        pr = slice(64 * g, 64 * g + 64)
                        for jj, sk in ((0, skp), (1, skc)):
                            bank = 3 + jj
                            S.op("pe", lambda: nc.tensor.matmul(ps[bank][:, :], lhsT=kT[pr, sk, :],
                                                                rhs=qT[pr, s, :, :], start=True, stop=True),
                                 reads=[("kT", sk), ("qT", s)], writes=[("ps", bank)])
                            S.op("act", lambda: nc.scalar.activation(out=PT[:, 0, g, jj, :], in_=ps[bank][:, :],
                                                                     func=AF.Exp, scale=0.125),
                                 reads=[("ps", bank)], writes=[("PT", 0, g, jj)])
                    if 1 not in gsel:
                        return
                    S.op("pool", lambda: nc.gpsimd.affine_select(
                        out=PT[:, 0, :, 0, :], in_=PT[:, 0, :, 0, :], compare_op=ALU.is_gt, fill=0.0, base=0,
                        pattern=[[0, 2], [0, 4], [-1, 128]], channel_multiplier=1),
                        reads=[("PT", 0, 0, 0), ("PT", 0, 1, 0)], writes=[("PT", 0, 0, 0), ("PT", 0, 1, 0)])
                    S.op("pool", lambda: nc.gpsimd.affine_select(
                        out=PT[:, 0, :, 1, :], in_=PT[:, 0, :, 1, :], compare_op=ALU.is_ge, fill=0.0, base=0,
                        pattern=[[0, 2], [0, 4], [1, 128]], channel_multiplier=-1),
                        reads=[("PT", 0, 0, 1), ("PT", 0, 1, 1)], writes=[("PT", 0, 0, 1), ("PT", 0, 1, 1)])
                    if i == 1:
                        S.op("pool", lambda: nc.gpsimd.tensor_scalar(
                            out=PT[:, 0, :, 0, :], in0=PT[:, 0, :, 0, :], scalar1=flag[:, 0:1], scalar2=0.0,
                            op0=ALU.mult, op1=ALU.add),
                            reads=[("PT", 0, 0, 0), ("PT", 0, 1, 0), "flag"], writes=[("PT", 0, 0, 0), ("PT", 0, 1, 0)])

                def st_PV(i, gsel=(0, 1)):
                    s = i % 2
                    svc, svp = i % NV, (i - 1) % NV
                    for g in gsel:
                        O = ps[5][:, 0:260].rearrange("p (c e) -> p c e", e=65)

                        def f():
                            for c in range(4):
                                nc.tensor.matmul(O[:, c, :], lhsT=PT[:, 0, g, 0, c * 128:(c + 1) * 128],
                                                 rhs=vaug[:, svp, g, :], start=True, stop=False)
                                i_ = nc.tensor.matmul(O[:, c, :], lhsT=PT[:, 0, g, 1, c * 128:(c + 1) * 128],
                                                      rhs=vaug[:, svc, g, :], start=False, stop=True)
                            return i_
                        S.op("pe", f, reads=[("PT", 0, g, 0), ("PT", 0, g, 1), ("vaug", svp), ("vaug", svc)],
                             writes=[("ps", 5)])
                        S.op("dve", lambda: nc.vector.tensor_tensor(out=dn[:, g, :], in0=O[:, :, 64],
                                                                    in1=esink[:, 4 * g:4 * g + 4], op=ALU.add),
                             reads=[("ps", 5), "esink"], writes=[("dn", g)])
                        S.op("dve", lambda: nc.vector.reciprocal(out=rc[:, g, :], in_=dn[:, g, :]),
                             reads=[("dn", g)], writes=[("rc", g)])
                        av = attn[:, s, :].rearrange("p (h d) -> p h d", d=64)
                        S.op("dve", lambda: nc.vector.tensor_tensor(
                            out=av[:, 4 * g:4 * g + 4, :], in0=O[:, :, 0:64],
                            in1=rc[:, g, :].unsqueeze(2).to_broadcast([128, 4, 64]), op=ALU.mult),
                            reads=[("ps", 5), ("rc", g)], writes=[("attn", s, g)])

                def st_attnT(i):
                    s = i % 2
                    sa = i % 2
                    bank = scratch()

                    def f():
                        for cc in range(4):
                            i_ = nc.tensor.transpose(out=bfv(bank)[:, cc, :], in_=attn[:, s, cc * 128:(cc + 1) * 128],
                                                     identity=ident[:])
                        return i_
                    S.op("pe", f, reads=[("attn", s, 0), ("attn", s, 1), "ident"], writes=[("ps", bank)])
                    S.op("act", lambda: nc.scalar.activation(out=AT[:, sa, 0:4, :], in_=bfv(bank)[:, 0:4, :], func=AF.Copy),
                         reads=[("ps", bank)], writes=[("AT", sa, 0)])

                def st_pool1(i):
                    suc, sup = i % NU, (i - 1) % NU
                    sp_ = i % 2
                    bank = scratch()
                    kp, kc = (3, 2) if i == 1 else (1, 0)
                    Y = ps[bank][:, :].rearrange("p (g t) -> p g t", t=128)

                    def f():
                        for g in range(4):
                            nc.tensor.matmul(Y[:, g, :], lhsT=upl[:, sup, g * 128:(g + 1) * 128],
                                             rhs=bands[:, kp * 4 + g, :], start=True, stop=False)
                            i_ = nc.tensor.matmul(Y[:, g, :], lhsT=upl[:, suc, g * 128:(g + 1) * 128],
                                                  rhs=bands[:, kc * 4 + g, :], start=False, stop=True)
                        return i_
                    S.op("pe", f, reads=[("upl", sup, 0), ("upl", sup, 1), ("upl", suc, 0), ("upl", suc, 1), "bands"],
                         writes=[("ps", bank)])
                    S.op("act", lambda: nc.scalar.activation(out=pooledT[:, sp_, :, :], in_=Y, func=AF.Copy),
                         reads=[("ps", bank)], writes=[("pooledT", sp_)])

                def st_pool2(i):
                    sp_ = i % 2
                    sa = i % 2
                    bank = scratch()
                    Z = ps[bank][:, :].rearrange("p (g t) -> p g t", t=128)

                    def f():
                        for g in range(4):
                            i_ = nc.tensor.matmul(Z[:, g, :], lhsT=wpool[:, g, :], rhs=pooledT[:, sp_, g, :],
                                                  start=True, stop=True)
                        return i_
                    S.op("pe", f, reads=[("pooledT", sp_), "wpool"], writes=[("ps", bank)])
                    S.op("dve", lambda: nc.vector.tensor_tensor(out=AT[:, sa, 4:8, :], in0=Z,
                                                                in1=psc[:].unsqueeze(2).to_broadcast([128, 4, 128]),
                                                                op=ALU.mult),
                         reads=[("ps", bank), "psc"], writes=[("AT", sa, 1)])

                def st_outproj(i, hsel=(0, 1)):
                    sa = i % 2
                    for hc in hsel:
                        bank = scratch()

                        def f():
                            for k in range(8):
                                i_ = nc.tensor.matmul(ps[bank][:, :], lhsT=AT[:, sa, k, :],
                                                      rhs=wout[:, k, hc * 512:(hc + 1) * 512],
                                                      start=(k == 0), stop=(k == 7))
                            return i_
                        S.op("pe", f, reads=[("AT", sa, 0), ("AT", sa, 1)] + [("wout", k) for k in range(8)],
                             writes=[("ps", bank)])
                        S.op("dve", lambda: nc.vector.tensor_tensor(
                            out=resid[:, i - 1, hc * 512:(hc + 1) * 512], in0=ps[bank][:, :],
                            in1=resid[:, i - 1, hc * 512:(hc + 1) * 512], op=ALU.add),
                            reads=[("ps", bank), ("resid", i - 1)], writes=[("resid", i - 1)])

                def st_norm2(i):
                    col = 20 + i
                    s = i % 2
                    rstd_ops(col, resid[:, i - 1, :], ("resid", i - 1), xhat2[:, 0, :], ("xhat2", 0))
                    if i >= 12:
                        S.op("dve", lambda: nc.vector.tensor_scalar(out=xhat2[:, 0, :], in0=resid[:, i - 1, :],
                                                                    scalar1=rstd[:, col:col + 1], scalar2=None,
                                                                    op0=ALU.mult),
                             reads=[("resid", i - 1), ("rstd", col)], writes=[("xhat2", 0)])
                    else:
                        S.op("pool", lambda: nc.gpsimd.tensor_scalar(out=xhat2[:, 0, :], in0=resid[:, i - 1, :],
                                                                     scalar1=rstd[:, col:col + 1], scalar2=0.0,
                                                                     op0=ALU.mult, op1=ALU.add),
                             reads=[("resid", i - 1), ("rstd", col)], writes=[("xhat2", 0)])

                def st_trx2(i):
                    s = i % 2
                    if not mod_ready["g2"]:
                        S.op("dve", lambda: nc.vector.scalar_tensor_tensor(out=g2T[:], in0=modT[:, 32:40], scalar=1.0,
                                                                           in1=n2T[:], op0=ALU.add, op1=ALU.mult),
                             reads=["modT", "n2T"], writes=["g2T"])
                        mod_ready["g2"] = True
                    transp_mod(lambda k: xhat2[:, 0, k * 128:(k + 1) * 128], ("xhat2", 0), g2T, modT[:, 24:32],
                               ["g2T", "modT"] + (["h2Tscr_done"] if i == 1 else []),
                               lambda k: h2T[:, k, (i - 1) * 128:i * 128], ("h2T", i - 1), ev_eng="act")

                wout_state = {"k": 0}

                def wout_fold():
                    k = wout_state["k"]
                    if k >= 8 or "wout" in DBG_SKIP:
                        return
                    wout_state["k"] += 2
                    S.dma("sp", lambda: nc.sync.dma_start(
                        out=wst[:, 0:2, :], in_=wout_d[k * 128:(k + 2) * 128, :].rearrange("(c p) d -> p c d", p=128)),
                        writes=[("wst", 0), ("wst", 1)])
                    for slot in range(2):
                        S.op("pool", lambda: nc.gpsimd.tensor_tensor(out=wout[:, k + slot, :], in0=wst[:, slot, :],
                                                                     in1=gate[:, :], op=ALU.mult),
                             reads=[("wst", slot), "gate"], writes=[("wout", k + slot)])
                    k += 1
                    if k == 7:
                        S.dma("sp", lambda: nc.sync.dma_start(out=gate[:],
                                                              in_=bada_d[0:1, 5120:6144].to_broadcast([128, D])),
                              reads=[], writes=["gate"])

                stages = [
                    (6, 1, st_PV, ((0,),)),
                    (5, 1, st_S, ((0,),)),
                    (8, 1, st_outproj, ((0,),)),
                    (10, 1, st_trx2, ()),
                    (9, 1, st_norm2, ()),
                    (6, 1, st_PV, ((1,),)),
                    (5, 1, st_S, ((1,),)),
                    (8, 1, st_outproj, ((1,),)),
                    (4, 0, st_qkT, ()),
                    (3, 0, st_inproj, ((0,),)),
                    (7, 1, st_attnT, ()),
                    (3, 0, st_inproj, ((1,),)),
                    (6, 1, st_pool2, ()),
                    (3, 0, st_inproj, ((2,),)),
                    (5, 1, st_pool1, ()),
                    (2, 0, st_trx, ()),
                    (1, 0, st_norm, ()),
                    (0, 0, st_load, ()),
                ]
                depth = max(o for o, _, _, _ in stages)
                for step in range((NT + depth) if nsteps is None else nsteps):
                    if stop == "setup":
                        break
                    order = stages
                    if step >= NT:
                        order = [st_ for st_ in stages if st_[2] is not st_norm2]
                        k_ = [j for j, st_ in enumerate(order) if st_[2] is st_trx][0]
                        order.insert(k_, (9, 1, st_norm2, ()))
                    for off, first, fn, extra in order:
                        i = step - off
                        if first <= i < NT:
                            fn(i, *extra)
                    if step == 1:
                        for _ in range(3):
                            mod_dma()
                    if step >= 3 and not ("modp1" in DBG_SKIP and step >= 2):
                        for _ in range(2):
                            if mod_state["next_pe"] < 20 or wout_state["k"] == 8:
                                mod_pe()
                    if mod_state["next_pe"] >= 12:
                        wout_fold()
                    if step == 0:
                        load_win_q()
                        consts_bands((3,))
                        consts_done()
                    if mod_state["next_pe"] == 24 and not mod_state.get("g0"):
                        mod_state["g0"] = True
                        S.dma("pool", lambda: nc.gpsimd.dma_start(out=ring[:, 0, :, :], in_=wg_v[:, :, 0:G0]),
                              writes=[("ring", 0)])
                        S.dma("pool", lambda: nc.gpsimd.dma_start(out=ring[:, 1, :, :], in_=wu_v[:, :, 0:G0]),
                              writes=[("ring", 1)])
                while mod_state["next_pe"] < 24 and "modp1" not in DBG_SKIP:
                    mod_pe()
                if "x1" in dbg_d:
                    S.dma("sp", lambda: nc.sync.dma_start(out=dbg_d["x1"].rearrange("(t p) d -> p t d", p=128),
                                                          in_=resid[:, :, :]), reads=[("resid", t) for t in range(16)])
                S.barrier()
        es4 = ExitStack()
        with es4:
            sb4 = lambda name, shape, dtype: es4.enter_context(nc.sbuf_tensor("s_" + name, shape, dtype))
            wg = sb4("wg", [128, 2, 8, 512], BF16)
            wu = sb4("wu", [128, 2, 8, 512], BF16)
            wd = sb4("wd", [128, 2, 4, D], BF16)
            sg = sb4("sg", [128, 2, 512], F32)
            aT = sb4("aT", [128, 2, 4, 512], BF16)
            ybuf = sb4("ybuf", [128, 2, D], F32)
            yjunk = sb4("yjunk", [128, D], BF16)
            normf = sb4("normf", [128, D], F32)
            S.dma("sp", lambda: nc.sync.dma_start(out=normf[:], in_=nf_d[0:1, :].to_broadcast([128, D])),
                  writes=["normf"])
            gstart = [sum(GROUPS[:i]) for i in range(len(GROUPS))]
            wst_i = [0]

            def load_group(gi):
                b = gi % 2
                ncg = GROUPS[gi]
                f0 = gstart[gi]
                if gi > 0 or not mod_state.get("g0"):
                    S.dma("pool", lambda: nc.gpsimd.dma_start(out=wg[:, b, :, 0:ncg * 128],
                                                              in_=wg_v[:, :, f0 * 128:(f0 + ncg) * 128]),
                          writes=[("wg", b)])
                    S.dma("pool", lambda: nc.gpsimd.dma_start(out=wu[:, b, :, 0:ncg * 128],
                                                              in_=wu_v[:, :, f0 * 128:(f0 + ncg) * 128]),
                          writes=[("wu", b)])
                for j in range(0, ncg, 2):
                    f = f0 + j
                    S.dma("sp", lambda: nc.sync.dma_start(
                        out=wst[:, 0:2, :], in_=wd_d[f * 128:(f + 2) * 128, :].rearrange("(c p) d -> p c d", p=128)),
                        writes=[("wst", 0), ("wst", 1)])
                    for slot in range(2):
                        S.op("pool", lambda: nc.gpsimd.tensor_tensor(out=wd[:, b, j + slot, :], in0=wst[:, slot, :],
                                                                     in1=gate[:, :], op=ALU.mult),
                             reads=[("wst", slot), "gate"], writes=[("wd", b, j + slot)])

            par = {"gu": 0, "dn": 0, "a": 0, "y": 0}

            def gu_unit(gi, tb):
                b = gi % 2
                ncg = GROUPS[gi]
                if gi == 0 and mod_state.get("g0"):
                    wgs, wus, wgk, wuk = ring[:, 0, :, :], ring[:, 1, :, :], ("ring", 0), ("ring", 1)
                else:
                    wgs, wus, wgk, wuk = wg[:, b, :, :], wu[:, b, :, :], ("wg", b), ("wu", b)
                sa = par["a"]
                par["a"] ^= 1
                for j in range(ncg):
                    p = par["gu"]
                    par["gu"] ^= 1
                    gb, ub = p, 2 + p

                    def fg():
                        for k in range(8):
                            i_ = nc.tensor.matmul(ps[gb][:, :], lhsT=wgs[:, k, j * 128:(j + 1) * 128],
                                                  rhs=h2T[:, k, tb * 512:(tb + 1) * 512], start=(k == 0), stop=(k == 7))
                        return i_

                    def fu():
                        for k in range(8):
                            i_ = nc.tensor.matmul(ps[ub][:, :], lhsT=wus[:, k, j * 128:(j + 1) * 128],
                                                  rhs=h2T[:, k, tb * 512:(tb + 1) * 512], start=(k == 0), stop=(k == 7))
                        return i_
                    hk = [("h2T", t, k) for t in range(tb * 4, tb * 4 + 4) for k in range(8)]
                    S.op("pe", fg, reads=[wgk] + hk, writes=[("ps", gb)])
                    S.op("pe", fu, reads=[wuk] + hk, writes=[("ps", ub)])
                    S.op("act", lambda: nc.scalar.activation(out=sg[:, p, :], in_=ps[gb][:, :], func=AF.Silu),
                         reads=[("ps", gb)], writes=[("sg", p)])
                    S.op("dve", lambda: nc.vector.tensor_tensor(out=aT[:, sa, j, :], in0=ps[ub][:, :], in1=sg[:, p, :],
                                                                op=ALU.mult),
                         reads=[("ps", ub), ("sg", p)], writes=[("aT", sa, j)])
                return sa

            def dn_unit(gi, tb, sa):
                b = gi % 2
                ncg = GROUPS[gi]
                last = gi == len(GROUPS) - 1
                for tt in range(4):
                    tile = tb * 4 + tt
                    for hc in range(2):
                        p = par["dn"]
                        par["dn"] ^= 1
                        bank = 4 + p

                        def f():
                            for j in range(ncg):
                                i_ = nc.tensor.matmul(ps[bank][:, :], lhsT=aT[:, sa, j, tt * 128:(tt + 1) * 128],
                                                      rhs=wd[:, b, j, hc * 512:(hc + 1) * 512],
                                                      start=(j == 0), stop=(j == ncg - 1))
                            return i_
                        S.op("pe", f, reads=[("aT", sa, j) for j in range(ncg)] + [("wd", b, j) for j in range(ncg)],
                             writes=[("ps", bank)])
                        S.op("dve", lambda: nc.vector.tensor_tensor(
                            out=resid[:, tile, hc * 512:(hc + 1) * 512], in0=ps[bank][:, :],
                            in1=resid[:, tile, hc * 512:(hc + 1) * 512], op=ALU.add),
                            reads=[("ps", bank), ("resid", tile)], writes=[("resid", tile)])
                    if last:
                        if tt % 2 == 0:
                            flush_y()
                        col = 40 + tile
                        sy = par["y"]
                        par["y"] = (sy + 1) % 2
                        rstd_ops(col, resid[:, tile, :], ("resid", tile), yjunk[:, :], "yjunk")
                        deferred_y.append((tile, col, sy))

            deferred_y = []

            def flush_y():
                while deferred_y:
                    tile, col, sy = deferred_y.pop(0)
                    S.op("dve", lambda: nc.vector.scalar_tensor_tensor(
                        out=ybuf[:, sy, :], in0=resid[:, tile, :], scalar=rstd[:, col:col + 1], in1=normf[:, :],
                        op0=ALU.mult, op1=ALU.mult),
                        reads=[("resid", tile), ("rstd", col), "normf"], writes=[("ybuf", sy)])
                    if sy % 2 == 1:
                        S.dma("sp", lambda: nc.sync.dma_start(
                            out=y_d[(tile - 1) * 128:(tile + 1) * 128, :].rearrange("(c p) d -> p c d", p=128),
                            in_=ybuf[:, sy - 1:sy + 1, :]), reads=[("ybuf", sy - 1), ("ybuf", sy)])

            units = [(gi, tb) for gi in range(len(GROUPS)) for tb in range(4)]
            if stop is not None:
                units = []
            else:
                load_group(0)
                load_group(1)
            pending = None
            for n, (gi, tb) in enumerate(units):
                if tb == 3 and gi + 2 < len(GROUPS):
                    pass
                sa = gu_unit(gi, tb)
                if pending is not None:
                    dn_unit(*pending)
                    pg = pending[0]
                    if pending[1] == 3 and pg + 2 < len(GROUPS):
                        load_group(pg + 2)
                pending = (gi, tb, sa)
            if pending is not None:
                dn_unit(*pending)
            flush_y()
            S.barrier()
    return nc


_CACHE = {}


def _prep_inputs(x, c, positions, w_ada, b_ada, norm1, w_in, sinks, w_pool, pool_scale, w_out, norm2,
                 w_gate, w_up, w_down, norm_f):
    f = lambda a: np.ascontiguousarray(np.asarray(a), dtype=np.float32)
    x = f(x)
    positions = np.asarray(positions).astype(np.int32)
    fm = lambda v: np.ascontiguousarray(f(v).reshape(-1, 128).T)
    shared = {
        "w_ada": f(w_ada), "b_fm": fm(b_ada), "b_ada": f(b_ada).reshape(1, -1),
        "norm1": fm(norm1), "norm2": fm(norm2), "norm_f": f(norm_f).reshape(1, -1),
        "w_in": f(w_in), "sinks": f(sinks).reshape(1, 8), "w_pool": f(w_pool),
        "pool_scale": fm(pool_scale), "w_out": f(w_out), "w_gate": f(w_gate), "w_up": f(w_up), "w_down": f(w_down),
    }
    in_maps = []
    per_b = SEQ // TOK
    for core in range(NCORES):
        b, q = divmod(core, per_b)
        s0 = q * TOK
        xs = np.zeros((NT * 128, D), np.float32)
        ps_ = np.zeros((NT * 128,), np.int32)
        if q == 0:
            xs[128:] = x[b, 0:TOK]
            ps_[128:] = positions[b, 0:TOK]
        else:
            xs[:] = x[b, s0 - 128:s0 + TOK]
            ps_[:] = positions[b, s0 - 128:s0 + TOK]
        m = dict(shared)
        m["x"] = xs
        m["pos"] = np.ascontiguousarray(ps_.reshape(NT, 128).T)
        m["c"] = fm(f(c)[b])
        m["flag"] = np.full((128, 1), 0.0 if q == 0 else 1.0, np.float32)
        in_maps.append(m)
    return in_maps


def kernel(**inputs):
    in_maps = _prep_inputs(**inputs)
    if "nc" not in _CACHE:
        _CACHE["nc"] = build()
    res = run_bass_kernel_spmd(_CACHE["nc"], in_maps, core_ids=list(range(NCORES)))
    out = np.empty((2, SEQ, D), np.float32)
    per_b = SEQ // TOK
    for core in range(NCORES):
        b, q = divmod(core, per_b)
        out[b, q * TOK:(q + 1) * TOK] = res.results[core]["y"]
    return out
```
